# Optimizing a Trainium2 kernel written in Bass

```python
import math
import jax, jax.numpy as jnp
from jax import lax

D_MODEL = 1024
BATCH = 16
SEQ = 256
DEPTH = 2
DEC_BATCH = 2
DEC_SEQ = 2048
PAST_LEN = 512

GRID_W = 64
HEAD_DIM = 64
N_HEADS_GROUP = 4
W_GROUP = N_HEADS_GROUP * HEAD_DIM
N_MIXERS = 4
D_MIX = N_MIXERS * W_GROUP
D_FF = 4 * D_MODEL
Q_BLOCK = 128
SSD_HEADS = N_HEADS_GROUP
SSD_P = HEAD_DIM
SSD_GROUPS = 2
SSD_N = 64
SSD_CONV = 5
SSD_CHUNK = 128
SSD_XBC = W_GROUP + 2 * SSD_GROUPS * SSD_N
WIN_R = 8
WIN_C = 16
RWKV_SHIFT = 3
RWKV_DECAY_RANK = 64
RWKV_ICLR_RANK = 64
RWKV_GATE_RANK = 128
RWKV_SIZES = (W_GROUP, W_GROUP, W_GROUP, 2 * RWKV_DECAY_RANK, 2 * RWKV_ICLR_RANK, RWKV_GATE_RANK)
RWKV_COLS = 3 * W_GROUP + 2 * RWKV_DECAY_RANK + 2 * RWKV_ICLR_RANK + RWKV_GATE_RANK
RWKV_LN_EPS = 64e-5
DIFF_SUB = HEAD_DIM // 2
ROPE_THETA = 10000.0
IN_SIZES = (W_GROUP, SSD_XBC, 2 * SSD_HEADS, 3 * W_GROUP, RWKV_COLS, 3 * W_GROUP)
N_IN = W_GROUP + SSD_XBC + 2 * SSD_HEADS + 3 * W_GROUP + RWKV_COLS + 3 * W_GROUP

kernel_name = 'hybrid_diffusion_parallel_heads_step'


def split_points(sizes):
    pts, acc = [], 0
    for s in sizes[:-1]:
        acc += s
        pts.append(acc)
    return pts


def rms_norm(x, g, eps=1e-6):
    xf = x.astype(jnp.float32)
    y = xf * lax.rsqrt(jnp.mean(xf * xf, axis=-1, keepdims=True) + eps)
    return (y * g.astype(jnp.float32)).astype(x.dtype)


def dwconv_centred(x, w):
    k = w.shape[0]
    return lax.conv_general_dilated(x, w[:, None, :].astype(x.dtype), window_strides=(1,),
                                    padding=[(k // 2, k // 2)],
                                    dimension_numbers=('NWC', 'WIO', 'NWC'),
                                    feature_group_count=x.shape[-1])


def sweep_query_blocks(fn, q):
    b, L = q.shape[:2]
    nb = L // Q_BLOCK
    qb = jnp.moveaxis(q.reshape((b, nb, Q_BLOCK) + q.shape[2:]), 1, 0)
    out = jnp.moveaxis(lax.map(fn, qb), 0, 1)
    return out.reshape((b, L) + out.shape[3:])


def axial_rope(t):
    L = t.shape[1]
    pos = jnp.arange(L)
    half = DIFF_SUB // 2
    inv = 1.0 / (ROPE_THETA ** (jnp.arange(0, half, 2, dtype=jnp.float32) / half))

    def rot(u, p):
        ang = p.astype(jnp.float32)[:, None] * inv[None, :]
        cos = jnp.cos(ang)[None, :, None, None, :].astype(u.dtype)
        sin = jnp.sin(ang)[None, :, None, None, :].astype(u.dtype)
        u1, u2 = u[..., :half // 2], u[..., half // 2:]
        return jnp.concatenate([u1 * cos - u2 * sin, u2 * cos + u1 * sin], axis=-1)

    return jnp.concatenate([rot(t[..., :half], pos // GRID_W), rot(t[..., half:], pos % GRID_W)], axis=-1)


def ssd_chunked(x, dt, A, B, C, h0):
    b, L, h, p = x.shape
    n = B.shape[-1]
    Q = SSD_CHUNK
    nc = L // Q
    dtype = x.dtype
    la = (dt * A).reshape(b, nc, Q, h).transpose(0, 3, 1, 2)
    cs = jnp.cumsum(la, axis=-1)
    xdt = (x.astype(jnp.float32) * dt[..., None]).astype(dtype).reshape(b, nc, Q, h, p)
    Bc = B.reshape(b, nc, Q, h, n)
    Cc = C.reshape(b, nc, Q, h, n)
    tri = jnp.tril(jnp.ones((Q, Q), dtype=bool))
    seg = jnp.where(tri, cs[..., :, None] - cs[..., None, :], -jnp.inf)
    lmat = jnp.exp(seg).astype(dtype)
    gmat = jnp.einsum('bclhn,bcshn->bhcls', Cc, Bc) * lmat
    y_diag = jnp.einsum('bhcls,bcshp->bclhp', gmat, xdt)
    decay_to_end = jnp.exp(cs[..., -1:] - cs).astype(dtype)
    chunk_states = jnp.einsum('bclhn,bhcl,bclhp->bchpn', Bc, decay_to_end, xdt)
    chunk_decay = jnp.exp(cs[..., -1]).astype(dtype)

    def carry_step(s, inp):
        dec, st = inp
        return s * dec[..., None, None] + st, s

    final, s_in = lax.scan(carry_step, h0.astype(dtype),
                           (jnp.moveaxis(chunk_decay, 2, 0), jnp.moveaxis(chunk_states, 1, 0)))
    s_in = jnp.moveaxis(s_in, 0, 1)
    y_off = jnp.einsum('bclhn,bchpn,bhcl->bclhp', Cc, s_in, jnp.exp(cs).astype(dtype))
    return (y_diag + y_off).reshape(b, L, h, p), final


def ssd_mixer(z, xbc, dt_raw, p, h0_fwd, h0_bwd):
    b, L, _ = xbc.shape
    xbc = jax.nn.silu(dwconv_centred(xbc, p['ssd_conv_w']) + p['ssd_conv_b'])
    xs, bm, cm = jnp.split(xbc, [W_GROUP, W_GROUP + SSD_GROUPS * SSD_N], axis=-1)
    xs = xs.reshape(b, L, SSD_HEADS, SSD_P)
    rep = SSD_HEADS // SSD_GROUPS
    bm = jnp.repeat(bm.reshape(b, L, SSD_GROUPS, SSD_N), rep, axis=2)
    cm = jnp.repeat(cm.reshape(b, L, SSD_GROUPS, SSD_N), rep, axis=2)
    dt = jax.nn.softplus(dt_raw.reshape(b, L, 2, SSD_HEADS).astype(jnp.float32)
                         + p['ssd_dt_bias'].astype(jnp.float32))
    A = -jnp.exp(p['ssd_a_log'].astype(jnp.float32))
    flip = lambda t: t[:, ::-1]
    y_f, s_f = ssd_chunked(xs, dt[:, :, 0], A[0], bm, cm, h0_fwd)
    y_b, s_b = ssd_chunked(flip(xs), flip(dt[:, :, 1]), A[1], flip(bm), flip(cm), h0_bwd)
    y = y_f + flip(y_b) + p['ssd_d'][:, None] * xs
    y = y.reshape(b, L, W_GROUP) * jax.nn.silu(z)
    return rms_norm(y, p['ssd_norm_g']), jnp.stack([s_f, s_b], axis=1)


def dense_attend(q, k, v):
    scale = q.shape[-1] ** -0.5

    def block(qb):
        s = jnp.einsum('bqhd,bkhd->bhqk', qb, k).astype(jnp.float32) * scale
        return jnp.einsum('bhqk,bkhd->bqhd', jax.nn.softmax(s, axis=-1).astype(v.dtype), v)

    return sweep_query_blocks(block, q)


def nat_latent_attend(q, k, v, ck, cv, rel_bias):
    b, L, h, d = q.shape
    W = GRID_W
    R = L // W
    KR = min(WIN_R, R)
    scale = d ** -0.5
    rows = jnp.arange(R)
    cols = jnp.arange(W)
    row_idx = jnp.clip(rows - KR // 2, 0, R - KR)[:, None] + jnp.arange(KR)[None, :]
    col_start = jnp.clip(cols - WIN_C // 2, 0, W - WIN_C)
    col_ok = (cols[None, :] >= col_start[:, None]) & (cols[None, :] < col_start[:, None] + WIN_C)
    dr = row_idx - rows[:, None] + (WIN_R - 1)
    dc = jnp.clip(cols[None, :] - cols[:, None], -(WIN_C - 1), WIN_C - 1) + (WIN_C - 1)
    bias = rel_bias[:, dr[:, :, None, None], dc[None, None, :, :]]
    bias = bias.transpose(1, 3, 0, 2, 4).astype(jnp.float32)
    qg = q.reshape(b, R, W, h, d)
    kg = k.reshape(b, R, W, h, d)[:, row_idx]
    vg = v.reshape(b, R, W, h, d)[:, row_idx]
    s_lat = jnp.einsum('brqhd,brikhd->brqhik', qg, kg).astype(jnp.float32) * scale + bias
    s_lat = jnp.where(col_ok[None, None, :, None, None, :], s_lat, -jnp.inf)
    s_ctx = jnp.einsum('brqhd,bchd->brqhc', qg, ck).astype(jnp.float32) * scale
    n_lat = KR * W
    probs = jax.nn.softmax(jnp.concatenate([s_lat.reshape(b, R, W, h, n_lat), s_ctx], axis=-1), axis=-1)
    probs = probs.astype(v.dtype)
    out = (jnp.einsum('brqhik,brikhd->brqhd', probs[..., :n_lat].reshape(b, R, W, h, KR, W), vg)
           + jnp.einsum('brqhc,bchd->brqhd', probs[..., n_lat:], cv))
    return out.reshape(b, L, h, d)


def rwkv7_scan(r, w, kk, a, kt, v, s0):
    def step(S, inp):
        r_t, w_t, kk_t, a_t, kt_t, v_t = inp
        s_kk = jnp.einsum('bhvk,bhk->bhv', S, kk_t)
        S = (S * w_t[:, :, None, :] - s_kk[..., None] * (kk_t * a_t)[:, :, None, :]
             + v_t[..., None] * kt_t[:, :, None, :])
        return S, jnp.einsum('bhvk,bhk->bhv', S, r_t)

    xs = tuple(jnp.moveaxis(t, 1, 0) for t in (r, w, kk, a, kt, v))
    s_fin, ys = lax.scan(step, s0, xs)
    return jnp.moveaxis(ys, 0, 1), s_fin


def rwkv_mixer(u, p, s0_fwd, s0_bwd):
    b, L, _ = u.shape
    f32 = jnp.float32
    u = dwconv_centred(u, p['rwkv_conv_w'])
    r, k, v, w_dn, a_dn, g_dn = jnp.split(u, split_points(RWKV_SIZES), axis=-1)
    hd = lambda t: t.reshape(b, L, N_HEADS_GROUP, HEAD_DIM).astype(f32)
    kk = hd(k * p['rwkv_k_k'])
    kk = kk / jnp.maximum(jnp.sqrt(jnp.sum(kk * kk, axis=-1, keepdims=True)), 1e-12)
    w_dn = w_dn.reshape(b, L, 2, RWKV_DECAY_RANK)
    a_dn = a_dn.reshape(b, L, 2, RWKV_ICLR_RANK)
    ys, finals = [], []
    for d, s0 in enumerate((s0_fwd, s0_bwd)):
        w_log = -jax.nn.softplus(-(p['rwkv_w0'][d] + jnp.tanh(w_dn[:, :, d]) @ p['rwkv_w_up'][d])) - 0.5
        decay = jnp.exp(-jnp.exp(w_log.astype(f32)))
        a = jax.nn.sigmoid(p['rwkv_a0'][d] + a_dn[:, :, d] @ p['rwkv_a_up'][d])
        kt = k * (1 + (a - 1) * p['rwkv_k_a'])
        seqs = (hd(r), hd(decay), kk, hd(a), hd(kt), hd(v))
        if d == 1:
            seqs = tuple(t[:, ::-1] for t in seqs)
        y, s_fin = rwkv7_scan(*seqs, s0.astype(f32))
        ys.append(y if d == 0 else y[:, ::-1])
        finals.append(s_fin)
    y = ys[0] + ys[1]
    mu = jnp.mean(y, axis=-1, keepdims=True)
    var = jnp.mean(jnp.square(y - mu), axis=-1, keepdims=True)
    y = ((y - mu) * lax.rsqrt(var + RWKV_LN_EPS)
         * p['rwkv_ln_g'].reshape(N_HEADS_GROUP, HEAD_DIM).astype(f32)
         + p['rwkv_ln_b'].reshape(N_HEADS_GROUP, HEAD_DIM).astype(f32))
    bonus = jnp.sum(hd(r) * hd(k) * p['rwkv_r_k'].astype(f32), axis=-1, keepdims=True) * hd(v)
    gate = jax.nn.sigmoid(g_dn) @ p['rwkv_g_up']
    out = (y + bonus).reshape(b, L, W_GROUP).astype(u.dtype) * gate
    return out, jnp.stack(finals, axis=1).astype(u.dtype)


def diff_attend(q, k, v, lam):
    scale = DIFF_SUB ** -0.5

    def block(qb):
        s = jnp.einsum('bqhcd,bkhcd->bhcqk', qb, k).astype(jnp.float32) * scale
        pr = jax.nn.softmax(s, axis=-1)
        amap = pr[:, :, 0] - lam * pr[:, :, 1]
        return jnp.einsum('bhqk,bkhd->bqhd', amap.astype(v.dtype), v)

    return sweep_query_blocks(block, q)


def trunk_layer(x, cond, p, layer_idx, cache):
    b, L, _ = x.shape
    is_ctx = cache is None
    heads = lambda t: t.reshape(b, L, N_HEADS_GROUP, HEAD_DIM)
    mod = (jax.nn.silu(cond) @ p['w_mod'] + p['b_mod']).reshape(-1, 1, 6 * D_MODEL)
    sh1, sc1, g1, sh2, sc2, g2 = jnp.split(mod, 6, axis=-1)
    hn = rms_norm(x, p['norm1_g']) * (1 + sc1) + sh1
    z, xbc, dt_raw, nat_qkv, rwkv_in, diff_qkv = jnp.split(hn @ p['w_in'], split_points(IN_SIZES), axis=-1)
    if is_ctx:
        ssd0 = jnp.zeros((b, 2, SSD_HEADS, SSD_P, SSD_N), x.dtype)
        rwkv0 = jnp.zeros((b, 2, N_HEADS_GROUP, HEAD_DIM, HEAD_DIM), x.dtype)
    else:
        ssd0, nat_kc, nat_vc, rwkv0, diff_kc, diff_vc = cache
    y_ssd, ssd_fin = ssd_mixer(z, xbc, dt_raw, p, ssd0[:, 0], ssd0[:, 1])
    nq, nk, nv = jnp.split(nat_qkv, 3, axis=-1)
    nq = rms_norm(heads(nq), p['nat_q_g'])
    nk = rms_norm(heads(nk), p['nat_k_g'])
    nv = heads(nv)
    if is_ctx:
        y_nat = dense_attend(nq, nk, nv)
    else:
        y_nat = nat_latent_attend(nq, nk, nv, nat_kc, nat_vc, p['nat_rel_bias'])
    y_rwkv, rwkv_fin = rwkv_mixer(rwkv_in, p, rwkv0[:, 0], rwkv0[:, 1])
    dq, dk, dv = jnp.split(diff_qkv, 3, axis=-1)
    dq = rms_norm(dq.reshape(b, L, N_HEADS_GROUP, 2, DIFF_SUB), p['diff_q_g'])
    dk = rms_norm(dk.reshape(b, L, N_HEADS_GROUP, 2, DIFF_SUB), p['diff_k_g'])
    dv = heads(dv)
    lam_init = 0.8 - 0.6 * math.exp(-0.3 * layer_idx)
    lv = p['diff_lam'].astype(jnp.float32)
    lam = jnp.exp(jnp.sum(lv[0] * lv[1])) - jnp.exp(jnp.sum(lv[2] * lv[3])) + lam_init
    if is_ctx:
        k_all, v_all = dk, dv
    else:
        dq = axial_rope(dq)
        k_all = jnp.concatenate([diff_kc, axial_rope(dk)], axis=1)
        v_all = jnp.concatenate([diff_vc, dv], axis=1)
    y_diff = rms_norm(diff_attend(dq, k_all, v_all, lam), p['diff_subln_g']) * (1 - lam_init)
    mixed = jnp.concatenate([y_ssd, y_nat.reshape(b, L, W_GROUP), y_rwkv,
                             y_diff.reshape(b, L, W_GROUP)], axis=-1)
    x = x + g1 * (mixed @ p['w_out'])
    hn2 = rms_norm(x, p['norm2_g']) * (1 + sc2) + sh2
    x = x + g2 * (jnp.square(jax.nn.relu(hn2 @ p['w_ff1'])) @ p['w_ff2'])
    if is_ctx:
        return x, (ssd_fin, nk, nv, rwkv_fin, dk, dv)
    return x, None


def setup_inputs(seed: int = 0) -> dict:
    key = jax.random.key(seed)
    ks = iter(jax.random.split(key, 64))
    f32 = jnp.float32
    H = N_HEADS_GROUP

    def nrm(shape, scale):
        return scale * jax.random.normal(next(ks), shape, f32)

    def unif(shape, lo, hi):
        return jax.random.uniform(next(ks), shape, f32, lo, hi)

    dt0 = jnp.exp(unif((DEPTH, 2, SSD_HEADS), math.log(1e-3), math.log(1e-1)))
    return {
        'x_prompt': nrm((BATCH, SEQ, D_MODEL), 1.0),
        'x_sample': nrm((DEC_BATCH, DEC_SEQ, D_MODEL), 1.0),
        'state_ssd': nrm((DEC_BATCH, DEPTH, 2, SSD_HEADS, SSD_P, SSD_N), 0.5),
        'cache_nat_k': nrm((DEC_BATCH, DEPTH, PAST_LEN, H, HEAD_DIM), 1.0),
        'cache_nat_v': nrm((DEC_BATCH, DEPTH, PAST_LEN, H, HEAD_DIM), 1.0),
        'state_rwkv': nrm((DEC_BATCH, DEPTH, 2, H, HEAD_DIM, HEAD_DIM), 0.5),
        'cache_diff_k': nrm((DEC_BATCH, DEPTH, PAST_LEN, H, 2, DIFF_SUB), 1.0),
        'cache_diff_v': nrm((DEC_BATCH, DEPTH, PAST_LEN, H, HEAD_DIM), 1.0),
        'c': nrm((DEC_BATCH, D_MODEL), 1.0),
        'c_ctx': nrm((D_MODEL,), 1.0),
        'w_mod': nrm((DEPTH, D_MODEL, 6 * D_MODEL), D_MODEL ** -0.5),
        'b_mod': nrm((DEPTH, 6 * D_MODEL), 0.01),
        'norm1_g': 1.0 + nrm((DEPTH, D_MODEL), 0.01),
        'norm2_g': 1.0 + nrm((DEPTH, D_MODEL), 0.01),
        'w_in': nrm((DEPTH, D_MODEL, N_IN), D_MODEL ** -0.5),
        'w_out': nrm((DEPTH, D_MIX, D_MODEL), D_MIX ** -0.5),
        'ssd_conv_w': nrm((DEPTH, SSD_CONV, SSD_XBC), SSD_CONV ** -0.5),
        'ssd_conv_b': nrm((DEPTH, SSD_XBC), 0.01),
        'ssd_a_log': jnp.log(unif((DEPTH, 2, SSD_HEADS), 1.0, 16.0)),
        'ssd_dt_bias': dt0 + jnp.log(-jnp.expm1(-dt0)),
        'ssd_d': 1.0 + nrm((DEPTH, SSD_HEADS), 0.01),
        'ssd_norm_g': 1.0 + nrm((DEPTH, W_GROUP), 0.01),
        'nat_q_g': 1.0 + nrm((DEPTH, HEAD_DIM), 0.01),
        'nat_k_g': 1.0 + nrm((DEPTH, HEAD_DIM), 0.01),
        'nat_rel_bias': nrm((DEPTH, H, 2 * WIN_R - 1, 2 * WIN_C - 1), 0.1),
        'rwkv_conv_w': nrm((DEPTH, RWKV_SHIFT, RWKV_COLS), RWKV_SHIFT ** -0.5),
        'rwkv_w0': unif((DEPTH, 2, W_GROUP), -5.0, -0.5),
        'rwkv_w_up': nrm((DEPTH, 2, RWKV_DECAY_RANK, W_GROUP), 0.5 * RWKV_DECAY_RANK ** -0.5),
        'rwkv_a0': nrm((DEPTH, 2, W_GROUP), 0.1),
        'rwkv_a_up': nrm((DEPTH, 2, RWKV_ICLR_RANK, W_GROUP), 0.5 * RWKV_ICLR_RANK ** -0.5),
        'rwkv_g_up': nrm((DEPTH, RWKV_GATE_RANK, W_GROUP), RWKV_GATE_RANK ** -0.5),
        'rwkv_k_k': 0.85 + nrm((DEPTH, W_GROUP), 0.02),
        'rwkv_k_a': 1.0 + nrm((DEPTH, W_GROUP), 0.02),
        'rwkv_r_k': nrm((DEPTH, H, HEAD_DIM), 0.1),
        'rwkv_ln_g': 1.0 + nrm((DEPTH, W_GROUP), 0.01),
        'rwkv_ln_b': nrm((DEPTH, W_GROUP), 0.01),
        'diff_q_g': 1.0 + nrm((DEPTH, DIFF_SUB), 0.01),
        'diff_k_g': 1.0 + nrm((DEPTH, DIFF_SUB), 0.01),
        'diff_lam': nrm((DEPTH, 4, DIFF_SUB), 0.1),
        'diff_subln_g': 1.0 + nrm((DEPTH, HEAD_DIM), 0.01),
        'w_ff1': nrm((DEPTH, D_MODEL, D_FF), D_MODEL ** -0.5),
        'w_ff2': nrm((DEPTH, D_FF, D_MODEL), D_FF ** -0.5),
    }


def reference(x_prompt, x_sample, state_ssd, cache_nat_k, cache_nat_v, state_rwkv, cache_diff_k,
              cache_diff_v, c, c_ctx, w_mod, b_mod, norm1_g, norm2_g, w_in, w_out, ssd_conv_w,
              ssd_conv_b, ssd_a_log, ssd_dt_bias, ssd_d, ssd_norm_g, nat_q_g, nat_k_g, nat_rel_bias,
              rwkv_conv_w, rwkv_w0, rwkv_w_up, rwkv_a0, rwkv_a_up, rwkv_g_up, rwkv_k_k, rwkv_k_a,
              rwkv_r_k, rwkv_ln_g, rwkv_ln_b, diff_q_g, diff_k_g, diff_lam, diff_subln_g, w_ff1, w_ff2):
    def layer_params(l):
        return {
            'w_mod': w_mod[l], 'b_mod': b_mod[l], 'norm1_g': norm1_g[l], 'norm2_g': norm2_g[l],
            'w_in': w_in[l], 'w_out': w_out[l], 'ssd_conv_w': ssd_conv_w[l], 'ssd_conv_b': ssd_conv_b[l],
            'ssd_a_log': ssd_a_log[l], 'ssd_dt_bias': ssd_dt_bias[l], 'ssd_d': ssd_d[l],
            'ssd_norm_g': ssd_norm_g[l], 'nat_q_g': nat_q_g[l], 'nat_k_g': nat_k_g[l],
            'nat_rel_bias': nat_rel_bias[l], 'rwkv_conv_w': rwkv_conv_w[l], 'rwkv_w0': rwkv_w0[l],
            'rwkv_w_up': rwkv_w_up[l], 'rwkv_a0': rwkv_a0[l], 'rwkv_a_up': rwkv_a_up[l],
            'rwkv_g_up': rwkv_g_up[l], 'rwkv_k_k': rwkv_k_k[l], 'rwkv_k_a': rwkv_k_a[l],
            'rwkv_r_k': rwkv_r_k[l], 'rwkv_ln_g': rwkv_ln_g[l], 'rwkv_ln_b': rwkv_ln_b[l],
            'diff_q_g': diff_q_g[l], 'diff_k_g': diff_k_g[l], 'diff_lam': diff_lam[l],
            'diff_subln_g': diff_subln_g[l], 'w_ff1': w_ff1[l], 'w_ff2': w_ff2[l],
        }

    y_prompt = x_prompt
    ctx = []
    for l in range(DEPTH):
        y_prompt, st = trunk_layer(y_prompt, c_ctx, layer_params(l), l, None)
        ctx.append(st)
    new_state_ssd = jnp.stack([s[0] for s in ctx], axis=1)
    new_cache_nat_k = jnp.stack([s[1] for s in ctx], axis=1)
    new_cache_nat_v = jnp.stack([s[2] for s in ctx], axis=1)
    new_state_rwkv = jnp.stack([s[3] for s in ctx], axis=1)
    new_cache_diff_k = jnp.stack([s[4] for s in ctx], axis=1)
    new_cache_diff_v = jnp.stack([s[5] for s in ctx], axis=1)

    y_sample = x_sample
    for l in range(DEPTH):
        cache = (state_ssd[:, l], cache_nat_k[:, l], cache_nat_v[:, l], state_rwkv[:, l],
                 cache_diff_k[:, l], cache_diff_v[:, l])
        y_sample, _ = trunk_layer(y_sample, c, layer_params(l), l, cache)

    return (y_prompt, y_sample, new_state_ssd, new_cache_nat_k, new_cache_nat_v, new_state_rwkv,
            new_cache_diff_k, new_cache_diff_v)
```

```python
import math
import numpy as np
from contextlib import ExitStack
import concourse.bass as bass
import concourse.mybir as mybir
from concourse.bass_utils import run_bass_kernel_spmd

F32 = mybir.dt.float32
BF16 = mybir.dt.bfloat16
AF = mybir.ActivationFunctionType
ALU = mybir.AluOpType
AX = mybir.AxisListType

D = 1024
DEPTH = 2
H = 4
HD = 64
NIN = 3464
DFF = 4096
LP, NSP = 256, 2
LS = 2048
LCTX = 512
OFF = dict(z=0, xs=256, B=512, C=640, dt=768, nq=776, nk=1032, nv=1288, rr=1544, rk=1800, rv=2056,
           wdn=2312, adn=2440, gdn=2568, dq=2696, dk=2952, dv=3208)
NEGBIG = -30000.0
import os
SKIP = set()


class S:
    def __init__(self, ap, sub):
        self.ap, self.sub = ap, sub


class _EngProxy:
    def __init__(self, prog, name):
        self.prog, self.name = prog, name

    def __getattr__(self, meth):
        def call(**kw):
            return self.prog._record(self.name, meth, kw)
        return call


class _Scope:
    def __init__(self, prog):
        self.prog = prog
        self.es = ExitStack()
        self.names = set()

    def __enter__(self):
        self.es.__enter__()
        return self

    def sb(self, name, shape, dt=F32):
        self.prog.n_t += 1
        nm = f"{name}_{self.prog.n_t}"
        self.names.add(nm)
        return self.es.enter_context(self.prog.nc.sbuf_tensor(nm, list(shape), dt))

    def __exit__(self, *a):
        self.prog.barrier()
        for key in list(self.prog.dma_slot):
            if key[0] in self.names:
                sl = self.prog.dma_slot.pop(key)
                if sl not in self.prog.store_slots:
                    self.prog.slot_free.setdefault(self.prog.slot_eng[sl], []).append(sl)
        return self.es.__exit__(*a)


class Prog:
    WRITE_KW = ("out", "accum_out", "ap")
    ENGS = ("pe", "dve", "act", "pool", "sp")

    def __init__(self, nc, es):
        self.nc, self.es = nc, es
        self.E = {"pe": nc.tensor, "dve": nc.vector, "act": nc.scalar, "pool": nc.gpsimd, "sp": nc.sync}
        self.ops = []
        self.res = {}
        self.latest = {}
        self.pe, self.dve, self.act, self.pool = (_EngProxy(self, n) for n in ("pe", "dve", "act", "pool"))
        self.sems = {}
        self.n_t = 0
        self.banks = []
        self.bank_i = 0
        self.dma_slot = {}
        self.slot_free = {}
        self.slot_eng = {}
        self.n_slot = 0
        self.store_slots = set()

    def sb(self, name, shape, dt=F32):
        return self.es.enter_context(self.nc.sbuf_tensor(name, list(shape), dt))

    def scope(self):
        return _Scope(self)

    def make_banks(self, nrot=6):
        for i in range(8):
            self.banks.append(self.es.enter_context(self.nc.psum_tensor(f"pb{i}", [128, 512], F32)))
        self.nrot = nrot

    def bank(self):
        b = self.banks[self.bank_i % self.nrot]
        self.bank_i += 1
        d = self.res.get(b.name, {})
        e = d.get(None)
        if e is not None and e["w"] is not None and not e["r"]:
            raise RuntimeError(f"PSUM bank {b.name} reused before it was consumed")
        return b

    def _entries(self, name, sub):
        d = self.res.setdefault(name, {})
        if sub is None:
            return list(d.values())
        out = []
        if None in d:
            out.append(d[None])
        if sub in d:
            out.append(d[sub])
        return out

    def _key(self, a):
        if isinstance(a, S):
            ap, sub = a.ap, a.sub
        else:
            ap, sub = a, None
        if str(ap.space) == "DRAM":
            return ap, None
        return ap, (ap.name, sub)

    def _add_dep(self, deps, opid, eng):
        o = self.ops[opid]
        if eng == "pe" and o["eng"] == "pe" and o["dma"] is None:
            return
        sk = o["semkey"]
        if sk not in deps or deps[sk] < opid:
            deps[sk] = opid

    def _access(self, eng, reads, writes, opid, semkey):
        deps = {}
        for (name, sub) in reads:
            for e in self._entries(name, sub):
                if e["w"] is not None:
                    self._add_dep(deps, e["w"], eng)
                if name.startswith("pb"):
                    for rk, r in e["r"].items():
                        if rk != semkey:
                            self._add_dep(deps, r, eng)
        for (name, sub) in writes:
            for e in self._entries(name, sub):
                if e["w"] is not None:
                    self._add_dep(deps, e["w"], eng)
                for r in e["r"].values():
                    self._add_dep(deps, r, eng)
        for (name, sub) in reads:
            d = self.res.setdefault(name, {})
            e = d.setdefault(sub, {"w": None, "r": {}})
            e["r"][semkey] = opid
        for (name, sub) in writes:
            d = self.res.setdefault(name, {})
            if sub is None:
                d.clear()
            d[sub] = {"w": opid, "r": {}}
        return deps

    def _record(self, eng, meth, kw, dma=None):
        reads, writes, args = [], [], {}
        extra_r = kw.pop("_r", [])
        extra_w = kw.pop("_w", [])
        for k, v in kw.items():
            if isinstance(v, (S, bass.AP)):
                ap, key = self._key(v)
                args[k] = ap
                if key is not None:
                    (writes if k in self.WRITE_KW else reads).append(key)
            else:
                args[k] = v
        opid = len(self.ops)
        if dma is not None:
            sbkeys = writes if writes else reads
            dk = dma if dma is not True else sbkeys[0]
            if dk not in self.dma_slot:
                fl = self.slot_free.setdefault(eng, [])
                if fl:
                    self.dma_slot[dk] = fl.pop()
                else:
                    self.dma_slot[dk] = self.n_slot
                    self.slot_eng[self.n_slot] = eng
                    self.n_slot += 1
            assert self.slot_eng[self.dma_slot[dk]] == eng, ("DMA semaphore shared across queues", dk)
            semkey = ("dma", self.dma_slot[dk])
            if not writes:
                self.store_slots.add(self.dma_slot[dk])
        else:
            semkey = eng
        reads = reads + list(extra_r)
        writes = writes + list(extra_w)
        deps = self._access(eng, reads, writes, opid, semkey)
        self.ops.append(dict(eng=eng, meth=meth, args=args, deps=deps, dma=dma, semkey=semkey, id=opid))
        self.latest[semkey] = opid
        return opid

    def dma(self, out, in_, eng="sp", key=True, **kw):
        return self._record(eng, "dma_start", dict(out=out, in_=in_, **kw), dma=key)

    def barrier(self):
        snap = dict(self.latest)
        for eng in self.ENGS:
            deps = dict(snap)
            if eng == "pe":
                deps.pop("pe", None)
            self.ops.append(dict(eng=eng, meth=None, args={}, deps=deps, dma=None, semkey=eng, id=len(self.ops)))

    def finalize(self):
        nc = self.nc
        needs = set()
        for o in self.ops:
            for d in o["deps"].values():
                needs.add(d)
        cnt, val = {}, {}
        for o in self.ops:
            if o["meth"] is None:
                continue
            sk = o["semkey"]
            if o["dma"] is not None:
                cnt[sk] = cnt.get(sk, 0) + 16
                val[o["id"]] = cnt[sk]
            elif o["id"] in needs:
                cnt[sk] = cnt.get(sk, 0) + 1
                val[o["id"]] = cnt[sk]
        for i, sk in enumerate(cnt):
            self.sems[sk] = self.es.enter_context(nc.semaphore(f"sem{i}"))
        seen = {e: {} for e in self.E}
        n_wait = 0
        n_ins = 0
        for o in self.ops:
            eng = self.E[o["eng"]]
            for sk, d in o["deps"].items():
                v = val[d]
                if seen[o["eng"]].get(sk, 0) < v:
                    eng.wait_ge(self.sems[sk], v)
                    seen[o["eng"]][sk] = v
                    n_wait += 1
            if o["meth"] is None:
                continue
            ins = getattr(eng, o["meth"])(**o["args"])
            n_ins += 1
            if o["dma"] is not None:
                ins.then_inc(self.sems[o["semkey"]], 16)
            elif o["id"] in needs:
                ins.then_inc(self.sems[o["semkey"]], 1)
        for sk, c in cnt.items():
            if isinstance(sk, tuple) and sk[0] == "dma" and seen["sp"].get(sk, 0) < c:
                nc.sync.wait_ge(self.sems[sk], c)
        self.stats = dict(n_ins=n_ins, n_wait=n_wait, n_sems=len(self.sems))
        return self.stats


def nat_patterns():
    R, W, KR = 32, 64, 8
    rs = np.clip(np.arange(R) - KR // 2, 0, R - KR)
    cs = np.clip(np.arange(W) - 8, 0, W - 16)
    pats, pairs, idx = {}, [], []
    for i in range(16):
        q = np.arange(128) + 128 * i
        qr, qc = q // W, q % W
        lst = []
        for j in range(16):
            k = np.arange(128) + 128 * j
            kr, kc = k // W, k % W
            ok = ((kr[:, None] >= rs[qr][None, :]) & (kr[:, None] < rs[qr][None, :] + KR)
                  & (kc[:, None] >= cs[qc][None, :]) & (kc[:, None] < cs[qc][None, :] + 16))
            if not ok.any():
                continue
            dr = np.where(ok, kr[:, None] - qr[None, :] + 7, 0)
            dc = np.where(ok, np.clip(kc[:, None] - qc[None, :], -15, 15) + 15, 0)
            key = (dr.tobytes(), dc.tobytes(), ok.tobytes())
            if key not in pats:
                pats[key] = len(idx)
                idx.append((dr, dc, ok))
            lst.append((j, pats[key]))
        pairs.append(lst)
    return pairs, idx


NAT_PAIRS, NAT_IDX = nat_patterns()
NPAT = len(NAT_IDX)


def host_consts():
    c = {}
    ar = np.arange(128)
    trif = (ar[:, None] <= ar[None, :]).astype(np.float32)
    trib = (ar[:, None] >= ar[None, :]).astype(np.float32)
    c["c_ident"] = np.eye(128, dtype=np.float32)
    c["c_tri"] = np.stack([trif, trib], 1)
    c["c_ones"] = np.ones((128, 128), np.float32)
    neg = np.stack([np.where(trif > 0, 0.0, NEGBIG), np.where(trib > 0, 0.0, NEGBIG)], 1)
    c["c_neg4"] = np.repeat(neg[:, :, None, :], 4, axis=2).reshape(128, 2, 512).astype(np.float32)
    rwm = np.zeros((128, 2, 3, 128), np.float32)
    sel = np.zeros((128, 2, 2), np.float32)
    mk = np.zeros((128, 2, 512), np.float32)
    mkt = np.zeros((128, 2, 128), np.float32)
    for d in range(2):
        tri = trif if d == 0 else trib
        strict = tri - np.eye(128, dtype=np.float32)
        im = (ar <= 63).astype(np.float32) if d == 0 else (ar >= 64).astype(np.float32)
        rwm[:, d, 0, :] = tri - im[:, None]
        rwm[:, d, 1, :] = 1.0 - tri
        rwm[:, d, 2, :] = strict - im[:, None]
        sel[:, d, 0] = 1.0
        sel[:, d, 1] = im
        mk[:, d, 0:128] = -strict
        mk[:, d, 128:256] = tri
        mk[:, d, 256:384] = strict
        mk[:, d, 384:512] = tri
        mkt[:, d, :] = -strict.T
    c["c_rwm"], c["c_rwsel"], c["c_rwmk"], c["c_rwmkt"] = rwm, sel, mk, mkt
    bu = np.zeros((128, 5, 128), np.float32)
    bu[:, 0, :] = (ar[:, None] // 8 == ar[None, :] // 8)
    for li in range(4):
        b = 8 << li
        bu[:, li + 1, :] = ((ar[:, None] // (2 * b) == ar[None, :] // (2 * b)) & ((ar[:, None] // b) % 2 == 0) & ((ar[None, :] // b) % 2 == 1))
    c["c_blk"] = np.stack([bu, bu.transpose(2, 1, 0)], 1)
    c["c_cmask"] = np.stack([((ar % 64) < 32), ((ar % 64) >= 32)], 1).astype(np.float32)
    pos = np.arange(LS)
    inv = (1.0 / (10000.0 ** (np.arange(0, 16, 2, dtype=np.float32) / 16))).astype(np.float32)
    ang_r = (pos // 64).astype(np.float32)[:, None] * inv[None, :]
    ang_c = (pos % 64).astype(np.float32)[:, None] * inv[None, :]
    cos = np.concatenate([np.cos(ang_r), np.cos(ang_r), np.cos(ang_c), np.cos(ang_c)], 1).astype(np.float32)
    sin = np.stack([np.sin(ang_r), np.sin(ang_c)], 1).astype(np.float32)
    c["c_ropec"] = np.ascontiguousarray(cos.reshape(16, 128, 32).transpose(1, 0, 2))
    c["c_ropes"] = np.ascontiguousarray(sin.reshape(16, 128, 16).transpose(1, 0, 2))
    return c


CONST_SHAPES = {k: v.shape for k, v in host_consts().items()}

WEIGHT_SHAPES = dict(
    w_mod=(DEPTH, D, 6 * D), b_mod=(DEPTH, 6 * D), norm1_g=(DEPTH, D), norm2_g=(DEPTH, D),
    w_in=(DEPTH, D, NIN), w_out=(DEPTH, D, D), ssd_conv_w=(DEPTH, 5, 512), ssd_conv_b=(DEPTH, 512),
    ssd_a_log=(DEPTH, 8), ssd_dt_bias=(DEPTH, 8), ssd_d=(DEPTH, 4), ssd_norm_g=(DEPTH, 256),
    nat_q_g=(DEPTH, 64), nat_k_g=(DEPTH, 64), nat_bm=(DEPTH, H, 128, NPAT, 128),
    rwkv_conv_w=(DEPTH, 3, 1152), rwkv_w0=(DEPTH, 2, 256), rwkv_w_up=(DEPTH, 128, 256),
    rwkv_a0=(DEPTH, 2, 256), rwkv_a_up=(DEPTH, 128, 256), rwkv_g_up=(DEPTH, 128, 256),
    rwkv_k_k=(DEPTH, 256), rwkv_k_a=(DEPTH, 256), rwkv_r_k=(DEPTH, 256), rwkv_ln_g=(DEPTH, 256),
    rwkv_ln_b=(DEPTH, 256), diff_q_g=(DEPTH, 32), diff_k_g=(DEPTH, 32), diff_lam=(DEPTH, 128),
    diff_subln_g=(DEPTH, 64), w_ff1=(DEPTH, D, DFF), w_ff2=(DEPTH, DFF, D),
)
ACT_SHAPES = dict(xp=(NSP * LP, D), xs=(LS, D), st_ssd=(DEPTH, 2, 64, 256), c_nk=(DEPTH, LCTX, 256),
                  c_nv=(DEPTH, LCTX, 256), st_rw=(DEPTH, 2, 64, 256), c_dk=(DEPTH, LCTX, 256),
                  c_dv=(DEPTH, LCTX, 256), cond=(2, 128, 8))
OUT_SHAPES = dict(yp=(NSP * LP, D), ys=(512, D), o_ssd=(NSP, DEPTH, 2, H, 64, 64), o_nk=(NSP, DEPTH, LP, 256),
                  o_nv=(NSP, DEPTH, LP, 256), o_rw=(NSP, DEPTH, 2, H, 64, 64), o_dk=(NSP, DEPTH, LP, 256),
                  o_dv=(NSP, DEPTH, LP, 256))


class Path:
    def __init__(self, name, ns, L, is_ctx, ci, xin, xscr, yout):
        self.name, self.ns, self.L, self.is_ctx, self.ci = name, ns, L, is_ctx, ci
        self.T = ns * L
        self.NT = self.T // 128
        self.nj = L // 128
        self.tiles = [(s, j) for s in range(ns) for j in range(self.nj)]
        self.xin, self.xscr, self.yout = xin, xscr, yout

    def tidx(self, s, j):
        return s * self.nj + j


def V(ap, a):
    return ap.rearrange("p (a b) -> p a b", a=a)


def BC(ap, shape):
    return ap.to_broadcast(list(shape))


def build(stop=None, dbg=None, paths="PS", layers=DEPTH, substop=None, mixers=("ssd", "nat", "diff", "rwkv")):
    nc = bass.Bass("TRN2", target_bir_lowering=False)
    A = {}
    for k, shp in {**ACT_SHAPES, **WEIGHT_SHAPES, **CONST_SHAPES}.items():
        A[k] = nc.dram_tensor(k, list(shp), F32, kind="ExternalInput").ap()
    O = {k: nc.dram_tensor(k, list(shp), F32, kind="ExternalOutput").ap() for k, shp in OUT_SHAPES.items()}
    DBG = {}
    if dbg:
        for k, shp in dbg.items():
            DBG[k] = nc.dram_tensor("dbg_" + k, list(shp), F32, kind="ExternalOutput").ap()
    xscrP = nc.dram_tensor("xscrP", [NSP * LP, D], F32, kind="Internal").ap()
    xscrS = nc.dram_tensor("xscrS", [LS, D], F32, kind="Internal").ap()
    mixscr = {"P": nc.dram_tensor("mixP", [NSP * LP, D], BF16, kind="Internal").ap(),
              "S": nc.dram_tensor("mixS", [LS, D], BF16, kind="Internal").ap()}
    qscr = nc.dram_tensor("qscr", [LS, 256], BF16, kind="Internal").ap()
    mixD = nc.dram_tensor("mixD", [LS, 1024], F32, kind="Internal").ap()
    PP = Path("P", NSP, LP, True, 0, A["xp"], xscrP, O["yp"])
    PS_ = Path("S", 1, LS, False, 1, A["xs"], xscrS, O["ys"])

    class Stop(Exception):
        pass

    es = ExitStack()
    if True:
        p = Prog(nc, es)
        p.make_banks(8)
        acc0, acc1 = p.banks[6], p.banks[7]
        ident = p.sb("ident", [128, 128], BF16)
        identf = p.sb("identf", [128, 128])
        tri = p.sb("tri", [128, 2, 128])
        onesf = p.sb("onesf", [128, 128])
        neg4 = p.sb("neg4", [128, 2, 512])
        rwm = p.sb("rwm", [128, 2, 3, 128])
        rwsel = p.sb("rwsel", [128, 2, 2])
        rwmk = p.sb("rwmk", [128, 2, 512])
        rwmkt = p.sb("rwmkt", [128, 2, 128])
        ropec = p.sb("ropec", [128, 16, 32])
        ropes = p.sb("ropes", [128, 16, 16])
        cmask = p.sb("cmask", [128, 2])
        blk = p.sb("blk", [128, 2, 5, 128], BF16)
        wring = [p.sb(f"wr{i}", [128, 2048], BF16) for i in range(6)]
        CB = p.sb("CB", [128, 8, 128], BF16)
        cnd = p.sb("cnd", [128, 8])
        cnds = p.sb("cnds", [128, 8])
        p.dma(out=ident[:], in_=A["c_ident"], eng="pool")
        p.dma(out=identf[:], in_=A["c_ident"])
        p.dma(out=tri[:], in_=A["c_tri"])
        p.dma(out=onesf[:], in_=A["c_ones"])
        p.dma(out=neg4[:], in_=A["c_neg4"])
        p.dma(out=rwm[:], in_=A["c_rwm"])
        p.dma(out=rwsel[:], in_=A["c_rwsel"])
        p.dma(out=rwmk[:], in_=A["c_rwmk"])
        p.dma(out=rwmkt[:], in_=A["c_rwmkt"])
        p.dma(out=ropec[:], in_=A["c_ropec"])
        p.dma(out=ropes[:], in_=A["c_ropes"])
        p.dma(out=cmask[:], in_=A["c_cmask"])
        p.dma(out=blk[:], in_=A["c_blk"], eng="pool")

        wi = [0]

        def wslot():
            t = wring[wi[0] % len(wring)]
            wi[0] += 1
            return t

        def wview(t, n, k=8):
            return t[:, 0:k * n].rearrange("p (k n) -> p k n", k=k)

        def load_w(dap, n):
            v = wview(wslot(), n)
            p.dma(out=v, in_=dap.rearrange("(k p) n -> p k n", p=128), eng="pool")
            return v

        def load_taps(cwb, l, c0, n, K, cwname, cw0):
            raw = load_w(A["w_in"][l, :, c0:c0 + n], n)
            if K == 1:
                return [(raw, 0)]
            p.dma(out=cwb[:, 0:K, 0:n], in_=A[cwname][l, :, cw0:cw0 + n].partition_broadcast(128))
            res = []
            for k in range(K):
                tv = wview(wslot(), n)
                p.pool.tensor_tensor(out=tv, in0=raw, in1=BC(cwb[:, k, 0:n].unsqueeze(1), [128, 8, n]), op=ALU.mult)
                res.append((tv, k - K // 2))
            return res

        def bcast_load(dst_ap, src_row):
            p.dma(out=dst_ap, in_=src_row.partition_broadcast(128))

        def dump_mix(path):
            if "mix" in DBG:
                p.dma(out=DBG["mix"], in_=mixscr[path.name], eng="pool", key=("st_dbg", 0),
                      _r=[("mix" + path.name, t) for t in range(path.NT)])

        def dump(name, ap):
            if name in DBG:
                p.dma(out=DBG[name], in_=ap, eng="pool")

        def hview(hnT, path):
            return hnT[:, :, 0:path.ns * (path.L + 4)].rearrange("p k (s l) -> p k s l", s=path.ns)

        def group_rms(sc, src, G, gs, eps, gain, out_ap, tag, l2=False):
            n = G * gs
            sq = sc.sb("rms_sq" + tag, [128, n])
            ss = sc.sb("rms_ss" + tag, [128, G])
            ms = sc.sb("rms_ms" + tag, [128, G])
            sd = sc.sb("rms_sd" + tag, [128, G])
            rs = sc.sb("rms_rs" + tag, [128, G])
            tm = sc.sb("rms_tm" + tag, [128, n])

            def run(src, gain, out_ap):
                p.act.activation(out=sq[:], in_=src, func=AF.Square)
                p.dve.tensor_reduce(out=ss[:], in_=V(sq[:], G), axis=AX.X, op=ALU.add)
                if l2:
                    p.act.activation(out=sd[:], in_=ss[:], func=AF.Sqrt)
                    p.dve.tensor_single_scalar(out=ms[:], in_=sd[:], scalar=eps, op=ALU.max)
                    p.dve.reciprocal(out=rs[:], in_=ms[:])
                else:
                    p.dve.tensor_scalar(out=ms[:], in0=ss[:], scalar1=1.0 / gs, scalar2=eps, op0=ALU.mult, op1=ALU.add)
                    p.act.activation(out=sd[:], in_=ms[:], func=AF.Sqrt)
                    p.dve.reciprocal(out=rs[:], in_=sd[:])
                p.dve.tensor_tensor(out=V(tm[:], G), in0=V(src, G), in1=BC(rs[:].unsqueeze(2), [128, G, gs]), op=ALU.mult)
                if gain is None:
                    p.pool.tensor_copy(out=out_ap, in_=tm[:])
                else:
                    p.pool.tensor_tensor(out=out_ap, in0=tm[:], in1=gain, op=ALU.mult)
            return run

        mixc = [0]

        JOWN = [None]

        def own_rows(t):
            if JOWN[0] is None:
                JOWN[0] = nc.sync.partition_id() % 4
            return bass.ds(JOWN[0] * 512 + t * 128, 128)

        def store_mix(path, src_bf, t, c0, own=False):
            mixc[0] += 1
            if own:
                p.dma(out=mixD[own_rows(t), :], in_=src_bf, key=("st_m", mixc[0] % 2), _w=[("mixD", t)])
                return
            p.dma(out=mixscr[path.name][t * 128:(t + 1) * 128, c0 * 128:(c0 + 2) * 128], in_=src_bf,
                  key=("st_m", mixc[0] % 2), _w=[("mix" + path.name, t)])

        def cond_setup(path):
            p.dma(out=cnd[:], in_=A["cond"][path.ci])
            p.act.activation(out=cnds[:], in_=cnd[:], func=AF.Silu)
            p.dve.tensor_copy(out=CB[:], in_=BC(cnds[:].unsqueeze(2), [128, 8, 128]))

        def mod_block(l, blk, dst):
            bcast_load(dst[:], A["b_mod"][l:l + 1, blk * 1024:(blk + 1) * 1024])
            for q in range(4):
                c0 = blk * 1024 + q * 256
                w = load_w(A["w_mod"][l, :, c0:c0 + 256], 256)
                bk = p.bank()
                for kc in range(8):
                    p.pe.matmul(out=bk[:, 0:256], lhsT=CB[:, kc, :], rhs=w[:, kc, :], start=(kc == 0), stop=(kc == 7))
                p.dve.tensor_tensor(out=dst[:, q * 256:(q + 1) * 256], in0=bk[:, 0:256],
                                    in1=dst[:, q * 256:(q + 1) * 256], op=ALU.add)

        def make_gain(l, gname, sct, tmp):
            bcast_load(tmp[:], A[gname][l:l + 1, :])
            p.dve.scalar_tensor_tensor(out=sct[:], in0=sct[:], scalar=1.0, in1=tmp[:], op0=ALU.add, op1=ALU.mult)

        def norm_tile(sc_t, xtile, G, SH, hnT, path, s, j):
            tmpB, hnb, ssq, ms, sd, rstd = sc_t
            p.act.activation(out=tmpB[:], in_=xtile, func=AF.Square, accum_out=ssq[:])
            p.dve.tensor_scalar(out=ms[:], in0=ssq[:], scalar1=1.0 / D, scalar2=1e-6, op0=ALU.mult, op1=ALU.add)
            p.act.activation(out=sd[:], in_=ms[:], func=AF.Sqrt)
            p.dve.reciprocal(out=rstd[:], in_=sd[:])
            p.dve.scalar_tensor_tensor(out=tmpB[:], in0=xtile, scalar=rstd[:, 0:1], in1=G[:], op0=ALU.mult, op1=ALU.mult)
            p.pool.tensor_tensor(out=hnb[:], in0=tmpB[:], in1=SH[:], op=ALU.add)
            bk = p.bank().bitcast(BF16)
            for kc in range(8):
                p.pe.transpose(out=bk[:, kc * 128:(kc + 1) * 128], in_=hnb[:, kc * 128:(kc + 1) * 128], identity=ident[:])
            p.act.copy(out=hview(hnT, path)[:, :, s, 2 + 128 * j:2 + 128 * (j + 1)], in_=V(bk[:, 0:1024], 8))

        def norm_scratch(sc):
            return (sc.sb("tmpB", [128, 1024]), sc.sb("hnb", [128, 1024], BF16), sc.sb("ssq", [128, 1]),
                    sc.sb("nms", [128, 1]), sc.sb("nsd", [128, 1]), sc.sb("nrstd", [128, 1]))

        def stage_A(path, l, hnT):
            with p.scope() as sc:
                G1 = sc.sb("G1", [128, 1024])
                SH1 = sc.sb("SH1", [128, 1024])
                tmpA = sc.sb("tmpA", [128, 1024])
                xt = [sc.sb(f"xt{i}", [128, 1024]) for i in range(2)]
                nsc = norm_scratch(sc)
                mod_block(l, 0, SH1)
                mod_block(l, 1, G1)
                make_gain(l, "norm1_g", G1, tmpA)
                xsrc = path.xin if l == 0 else path.xscr
                for (s, j) in path.tiles:
                    t = path.tidx(s, j)
                    xb = xt[t % 2]
                    rk = [] if l == 0 else [("xscr" + path.name, t)]
                    p.dma(out=xb[:], in_=xsrc[t * 128:(t + 1) * 128, :], _r=rk)
                    norm_tile(nsc, xb[:], G1, SH1, hnT, path, s, j)

        def stage_CD(path, l, hnT):
            last = (l == DEPTH - 1)
            own = last and path.name == "S" and l > 0
            NTall = path.NT
            if own:
                rows = own_rows
                tiles = [(0, ti) for ti in range(4)]
                rkeys = lambda name, t: [(name, tt) for tt in range(NTall)]
                okey = "xown"
            else:
                rows = lambda t: slice(t * 128, (t + 1) * 128)
                tiles = path.tiles
                rkeys = lambda name, t: [(name, t)]
                okey = "xscr" + path.name
            with p.scope() as sc:
                g1 = sc.sb("g1", [128, 1024])
                G2 = sc.sb("G2", [128, 1024])
                SH2 = sc.sb("SH2", [128, 1024])
                tmpA = sc.sb("tmpA", [128, 1024])
                xt = [sc.sb(f"xt{i}", [128, 1024]) for i in range(2)]
                nsc = norm_scratch(sc)
                mxb = [sc.sb(f"mxb{i}", [128, 1024], BF16) for i in range(2)]
                mxT = sc.sb("mxT", [128, 8, 128], BF16)
                mod_block(l, 2, g1)
                mod_block(l, 3, SH2)
                mod_block(l, 4, G2)
                make_gain(l, "norm2_g", G2, tmpA)
                wo = [load_w(A["w_out"][l, :, q * 256:(q + 1) * 256], 256) for q in range(4)]
                xsrc = path.xin if l == 0 else path.xscr
                for (s, j) in tiles:
                    t = path.tidx(s, j)
                    xb = xt[t % 2]
                    rk = [] if l == 0 else rkeys("xscr" + path.name, t)
                    p.dma(out=xb[:], in_=xsrc[rows(t), :], _r=rk)
                    mb = mxb[t % 2]
                    p.dma(out=mb[:], in_=mixscr[path.name][rows(t), :], _r=rkeys("mix" + path.name, t))
                    if own:
                        p.dma(out=tmpA[:], in_=mixD[rows(t), :], _r=[("mixD", t)])
                        p.dve.tensor_copy(out=mb[:, 768:1024], in_=tmpA[:, 0:256])
                    bkm = p.bank().bitcast(BF16)
                    for kc in range(8):
                        p.pe.transpose(out=bkm[:, kc * 128:(kc + 1) * 128], in_=mb[:, kc * 128:(kc + 1) * 128], identity=ident[:])
                    p.act.copy(out=mxT[:], in_=V(bkm[:, 0:1024], 8))
                    bks = [p.bank(), p.bank()]
                    for q in range(4):
                        bk = bks[q // 2]
                        for kc in range(8):
                            p.pe.matmul(out=bk[:, (q % 2) * 256:(q % 2 + 1) * 256], lhsT=mxT[:, kc, :],
                                        rhs=wo[q][:, kc, :], start=(kc == 0), stop=(kc == 7))
                    for hf in range(2):
                        p.dve.tensor_tensor(out=tmpA[:, hf * 512:(hf + 1) * 512], in0=bks[hf][:, 0:512],
                                            in1=g1[:, hf * 512:(hf + 1) * 512], op=ALU.mult)
                    p.pool.tensor_tensor(out=xb[:], in0=xb[:], in1=tmpA[:], op=ALU.add)
                    p.dma(out=path.xscr[rows(t), :], in_=xb[:], key=("st_x", t % 2),
                          _w=([(okey, t)] + (rkeys("xscr" + path.name, t) if own else [])))
                    norm_tile(nsc, xb[:], G2, SH2, hnT, path, s, j)
                mod_block(l, 5, g1)
                BT = 1024 if (path.ns == 1 and not own) else 512
                tpb = BT // 128
                hT = sc.sb("hT", [128, 32, BT], BF16)
                tr = [sc.sb(f"tr{i}", [128, 512]) for i in range(2)]
                xh = [sc.sb(f"xh{i}", [128, 512]) for i in range(2)]
                hv = hview(hnT, path)
                nblk = 1 if own else path.T // BT
                ydst = path.yout if last else path.xscr
                for b in range(nblk):
                    for cp in range(16):
                        w = load_w(A["w_ff1"][l, :, cp * 256:(cp + 1) * 256], 256)
                        for c2 in range(2):
                            c = cp * 2 + c2
                            for hh in range(BT // 512):
                                bk = p.bank()
                                if path.ns == 2:
                                    rhs_of = lambda kc: hv[:, kc, :, 2:2 + 256]
                                    ov = bk[:, 0:512].rearrange("p (s l) -> p s l", s=2)
                                else:
                                    c0_ = 2 + BT * b + 512 * hh
                                    rhs_of = lambda kc, c0_=c0_: hv[:, kc, 0, c0_:c0_ + 512]
                                    ov = bk[:, 0:512]
                                for kc in range(8):
                                    p.pe.matmul(out=ov, lhsT=w[:, kc, c2 * 128:(c2 + 1) * 128], rhs=rhs_of(kc),
                                                start=(kc == 0), stop=(kc == 7))
                                trb = tr[(c * 2 + hh) % 2]
                                p.act.activation(out=trb[:], in_=bk[:, 0:512], func=AF.Relu)
                                p.dve.tensor_tensor(out=hT[:, c, hh * 512:(hh + 1) * 512], in0=trb[:], in1=trb[:], op=ALU.mult)
                    for hf in range(2):
                        bks = [p.bank() for _ in range(tpb)]
                        for cq in range(8):
                            wv = wview(wslot(), 512, k=4)
                            p.dma(out=wv, in_=A["w_ff2"][l, cq * 512:(cq + 1) * 512, hf * 512:(hf + 1) * 512]
                                  .rearrange("(c p) n -> p c n", p=128), eng="pool")
                            for cc in range(4):
                                c = cq * 4 + cc
                                for ti in range(tpb):
                                    p.pe.matmul(out=bks[ti][:, 0:512], lhsT=hT[:, c, ti * 128:(ti + 1) * 128], rhs=wv[:, cc, :],
                                                start=(c == 0), stop=(c == 31))
                        for ti in range(tpb):
                            t = b * tpb + ti
                            xb = xh[ti % 2]
                            p.dma(out=xb[:], in_=path.xscr[rows(t), hf * 512:(hf + 1) * 512], _r=[(okey, t)])
                            trb = tr[ti % 2]
                            p.dve.tensor_tensor(out=trb[:], in0=bks[ti][:, 0:512], in1=g1[:, hf * 512:(hf + 1) * 512], op=ALU.mult)
                            p.dve.tensor_tensor(out=xb[:], in0=xb[:], in1=trb[:], op=ALU.add)
                            p.dma(out=ydst[t * 128:(t + 1) * 128, hf * 512:(hf + 1) * 512], in_=xb[:], key=("st_xh", ti % 2),
                                  _w=[] if last else [("xscr" + path.name, t)])

        def attn_mixer(path, l, hnT, kind):
            nat = (kind == "nat")
            G, gs = (4, 64) if nat else (8, 32)
            ncomp = 1 if nat else 2
            scale = gs ** -0.5
            qoff = OFF["nq"] if nat else OFF["dq"]
            T, NT = path.T, path.NT
            nctx = 0 if path.is_ctx else LCTX // 128
            lam_init = 0.8 - 0.6 * math.exp(-0.3 * l)
            p.nrot = 6
            own = (not nat) and (not path.is_ctx) and l == DEPTH - 1 and l > 0
            TQ = 512 if own else T
            NTQ = TQ // 128
            with p.scope() as sc:
                QT = sc.sb("QT", [128, 2, TQ], BF16)
                KTs = [sc.sb(f"KT{c}", [128, 2, nctx * 128 + T], BF16) for c in range(ncomp)]
                V1 = sc.sb("V1", [128, nctx + NT, 4, 65], BF16)
                YO = sc.sb("YO", [128, NTQ, 256], BF16 if (nat or not path.is_ctx) else F32)
                gq = sc.sb("gq", [128, gs])
                gk = sc.sb("gk", [128, gs])
                QKG = sc.sb("QKG", [128, 512])
                vst = [sc.sb(f"vst{i}", [128, 256]) for i in range(2 if path.is_ctx else 1)] * 2
                qkn = [sc.sb(f"qkn{i}", [128, 512]) for i in range(2)]
                qkb = sc.sb("qkb", [128, 512], BF16)
                rms = group_rms(sc, None, 2 * G, gs, 1e-6, None, None, "a")
                p.pool.memset(ap=V1[:, :, :, 64:65], constant=1.0)
                bcast_load(gq[:], A["nat_q_g" if nat else "diff_q_g"][l:l + 1, :])
                bcast_load(gk[:], A["nat_k_g" if nat else "diff_k_g"][l:l + 1, :])
                p.dve.tensor_scalar(out=V(QKG[:, 0:256], G), in0=BC(gq[:].unsqueeze(1), [128, G, gs]), scalar1=scale,
                                    scalar2=None, op0=ALU.mult)
                p.dve.tensor_copy(out=V(QKG[:, 256:512], G), in_=BC(gk[:].unsqueeze(1), [128, G, gs]))
                if not nat:
                    lv = sc.sb("lv", [128, 128])
                    lp = sc.sb("lp", [128, 64])
                    ls = sc.sb("ls", [128, 2])
                    le = sc.sb("le", [128, 2])
                    lam = sc.sb("lam", [128, 1])
                    nlam = sc.sb("nlam", [128, 1])
                    rz = sc.sb("rz", [128, 2])
                    t0 = sc.sb("t0", [128, 64])
                    r1l = sc.sb("r1l", [128, 1])
                    gsub = sc.sb("gsub", [128, 64])
                    GS4 = sc.sb("GS4", [128, 256])
                    bcast_load(lv[:], A["diff_lam"][l:l + 1, :])
                    p.dve.tensor_tensor(out=lp[:, 0:32], in0=lv[:, 0:32], in1=lv[:, 32:64], op=ALU.mult)
                    p.dve.tensor_tensor(out=lp[:, 32:64], in0=lv[:, 64:96], in1=lv[:, 96:128], op=ALU.mult)
                    p.dve.tensor_reduce(out=ls[:], in_=V(lp[:], 2), axis=AX.X, op=ALU.add)
                    p.act.activation(out=le[:], in_=ls[:], func=AF.Exp)
                    p.dve.tensor_tensor(out=lam[:], in0=le[:, 0:1], in1=le[:, 1:2], op=ALU.subtract)
                    p.dve.tensor_scalar(out=nlam[:], in0=lam[:], scalar1=-1.0, scalar2=-lam_init, op0=ALU.mult, op1=ALU.add)
                    bcast_load(gsub[:], A["diff_subln_g"][l:l + 1, :])
                    p.dve.tensor_scalar(out=V(GS4[:], 4), in0=BC(gsub[:].unsqueeze(1), [128, 4, 64]), scalar1=1.0 - lam_init,
                                        scalar2=None, op0=ALU.mult)
                else:
                    rzn = sc.sb("rzn", [128, 1])
                def put_keys(src, k0):
                    if nat:
                        p.act.copy(out=KTs[0][:, :, k0:k0 + 128], in_=V(src, 2))
                    else:
                        for c in range(2):
                            p.act.activation(out=KTs[c][:, :, k0:k0 + 128], in_=V(src, 2), func=AF.Copy, scale=cmask[:, c:c + 1])

                if nctx:
                    ckn, cvn = ("c_nk", "c_nv") if nat else ("c_dk", "c_dv")
                    for c in range(nctx):
                        kb = qkn[c % 2]
                        p.dma(out=kb[:, 0:256], in_=A[ckn][l, c * 128:(c + 1) * 128, :])
                        p.dve.tensor_copy(out=qkb[:, 0:256], in_=kb[:, 0:256])
                        bk = p.bank().bitcast(BF16)
                        for hp in range(2):
                            p.pe.transpose(out=bk[:, hp * 128:(hp + 1) * 128], in_=qkb[:, hp * 128:(hp + 1) * 128], identity=ident[:])
                        put_keys(bk[:, 0:256], c * 128)
                        vb = vst[c % 2]
                        p.dma(out=vb[:], in_=A[cvn][l, c * 128:(c + 1) * 128, :])
                        p.pool.tensor_copy(out=V1[:, c, :, 0:64], in_=V(vb[:], 4))
                wq = load_w(A["w_in"][l, :, qoff:qoff + 256], 256)
                wk = load_w(A["w_in"][l, :, qoff + 256:qoff + 512], 256)
                wv = load_w(A["w_in"][l, :, qoff + 512:qoff + 768], 256)
                hv = hview(hnT, path)
                okn, ovn = ("o_nk", "o_nv") if nat else ("o_dk", "o_dv")
                if (not nat) and nctx:
                    rt1 = sc.sb("rt1", [128, 512])
                    rt2 = sc.sb("rt2", [128, 512])
                if substop == "setup":
                    raise Stop()
                for (s, j) in path.tiles:
                    t = path.tidx(s, j)
                    lhs = lambda kc: hv[:, kc, s, 2 + 128 * j:2 + 128 * (j + 1)]
                    bA, bB = p.bank(), p.bank()
                    for kc in range(8):
                        p.pe.matmul(out=bA[:, 0:256], lhsT=lhs(kc), rhs=wq[:, kc, :], start=(kc == 0), stop=(kc == 7))
                    for kc in range(8):
                        p.pe.matmul(out=bA[:, 256:512], lhsT=lhs(kc), rhs=wk[:, kc, :], start=(kc == 0), stop=(kc == 7))
                    for kc in range(8):
                        p.pe.matmul(out=bB[:, 0:256], lhsT=lhs(kc), rhs=wv[:, kc, :], start=(kc == 0), stop=(kc == 7))
                    vb = vst[t % 2]
                    p.act.copy(out=vb[:], in_=bB[:, 0:256])
                    if path.is_ctx and 'st1' not in SKIP:
                        p.dma(out=O[ovn][s, l, j * 128:(j + 1) * 128, :], in_=vb[:], key=("st_v", t % 2))
                    p.pool.tensor_copy(out=V1[:, nctx + t, :, 0:64], in_=V(vb[:], 4))
                    qn = qkn[t % 2]
                    if 'rms' in SKIP:
                        p.dve.tensor_copy(out=qn[:], in_=bA[:, 0:512])
                    else:
                        rms(bA[:, 0:512], QKG[:], qn[:])
                    if path.is_ctx and 'st2' not in SKIP:
                        p.dma(out=O[okn][s, l, j * 128:(j + 1) * 128, :], in_=qn[:, 256:512], key=("st_k", t % 2))
                    if (not nat) and nctx:
                        u = qn[:]
                        p.dve.tensor_tensor(out=V(rt1[:], 16), in0=V(u, 16), in1=BC(ropec[:, j, :].unsqueeze(1), [128, 16, 32]), op=ALU.mult)
                        u5 = u.rearrange("p (g f b e) -> p g f b e", g=16, f=2, b=2)
                        t5 = rt2[:].rearrange("p (g f b e) -> p g f b e", g=16, f=2, b=2)
                        o5 = qkb[:].rearrange("p (g f b e) -> p g f b e", g=16, f=2, b=2)
                        a5 = rt1[:].rearrange("p (g f b e) -> p g f b e", g=16, f=2, b=2)
                        sn = BC(ropes[:, j, :].rearrange("p (f e) -> p f e", f=2).unsqueeze(1), [128, 16, 2, 8])
                        p.pool.tensor_tensor(out=t5[:, :, :, 0, :], in0=u5[:, :, :, 1, :], in1=sn, op=ALU.mult)
                        p.pool.tensor_tensor(out=t5[:, :, :, 1, :], in0=u5[:, :, :, 0, :], in1=sn, op=ALU.mult)
                        p.dve.tensor_tensor(out=o5[:, :, :, 0, :], in0=a5[:, :, :, 0, :], in1=t5[:, :, :, 0, :], op=ALU.subtract)
                        p.dve.tensor_tensor(out=o5[:, :, :, 1, :], in0=a5[:, :, :, 1, :], in1=t5[:, :, :, 1, :], op=ALU.add)
                    else:
                        p.dve.tensor_copy(out=qkb[:], in_=qn[:])
                    if 'tr' in SKIP:
                        continue
                    bk = p.bank().bitcast(BF16)
                    for g in (range(2, 4) if own else range(4)):
                        p.pe.transpose(out=bk[:, g * 128:(g + 1) * 128], in_=qkb[:, g * 128:(g + 1) * 128], identity=ident[:])
                    if own:
                        p.dma(out=qscr[t * 128:(t + 1) * 128, :], in_=qkb[:, 0:256], key=("st_q", 0), _w=[("qscr", t)])
                    else:
                        p.act.copy(out=QT[:, :, t * 128:(t + 1) * 128], in_=V(bk[:, 0:256], 2))
                    put_keys(bk[:, 256:512], (nctx + t) * 128)
                if own:
                    for ti in range(NTQ):
                        qo = qkn[ti % 2][:, 0:128].bitcast(BF16)
                        p.dma(out=qo, in_=qscr[own_rows(ti), :], _r=[("qscr", tt) for tt in range(NT)])
                        bk = p.bank().bitcast(BF16)
                        for g in range(2):
                            p.pe.transpose(out=bk[:, g * 128:(g + 1) * 128], in_=qo[:, g * 128:(g + 1) * 128], identity=ident[:])
                        p.act.copy(out=QT[:, :, ti * 128:(ti + 1) * 128], in_=V(bk[:, 0:256], 2))
                if substop == "proj":
                    raise Stop()
                if path.is_ctx:
                    blocks = [(s * path.L, path.L, [(s * path.nj + jj, None) for jj in range(path.nj)]) for s in range(path.ns)]
                elif nat:
                    blocks = [(i * 128, 128, [(c, None) for c in range(nctx)] + [(nctx + jj, pat) for (jj, pat) in NAT_PAIRS[i]])
                              for i in range(NT)]
                else:
                    blocks = [(b * 256, 256, [(c, None) for c in range(nctx + NT)]) for b in range(TQ // 256)]
                maxk = max(len(b[2]) for b in blocks)
                maxq = max(b[1] for b in blocks)
                with p.scope() as csc:
                    Eb = [csc.sb(f"Eb{i}", [128, maxk, maxq], BF16) for i in range(2)]
                    BM = csc.sb("BM", [128, NPAT, 128], BF16) if (nat and nctx) else None
                    items = [(h, q0, nq, keys, c) for h in range(4) for (q0, nq, keys) in blocks for c in range(ncomp)]
                    bm_head = [None]

                    def phase1(i):
                        h, q0, nq, keys, c = items[i]
                        if BM is not None and bm_head[0] != h:
                            p.dma(out=BM[:], in_=A["nat_bm"][l, h], eng="pool")
                            bm_head[0] = h
                        r0, r1 = 64 * (h % 2), 64 * (h % 2) + 64
                        KT = KTs[c]
                        E = Eb[i % 2]
                        for ki, (kt, pat) in enumerate(keys):
                            bk = p.bank()
                            p.pe.matmul(out=bk[:, 0:nq], lhsT=KT[r0:r1, h // 2, kt * 128:(kt + 1) * 128], rhs=QT[r0:r1, h // 2, q0:q0 + nq],
                                        start=True, stop=(pat is None))
                            if pat is not None:
                                p.pe.matmul(out=bk[:, 0:nq], lhsT=ident[:], rhs=BM[:, pat, :], start=False, stop=True)
                            p.act.activation(out=E[:, ki, 0:nq], in_=bk[:, 0:nq], func=AF.Exp)

                    def phase2(i):
                        h, q0, nq, keys, c = items[i]
                        E = Eb[i % 2]
                        ob = (acc0, acc1)[c]
                        for qi in range(nq // 128):
                            for ki, (kt, pat) in enumerate(keys):
                                p.pe.matmul(out=ob[:, qi * 65:(qi + 1) * 65], lhsT=E[:, ki, qi * 128:(qi + 1) * 128],
                                            rhs=V1[:, kt, h, :], start=(ki == 0), stop=(ki == len(keys) - 1))
                        if c != ncomp - 1:
                            return
                        for qi in range(nq // 128):
                            t = q0 // 128 + qi
                            o0 = acc0[:, qi * 65:qi * 65 + 64]
                            if nat:
                                p.dve.reciprocal(out=rzn[:], in_=acc0[:, qi * 65 + 64:qi * 65 + 65])
                                p.dve.tensor_scalar(out=YO[:, t, h * 64:(h + 1) * 64], in0=o0, scalar1=rzn[:, 0:1], scalar2=None,
                                                    op0=ALU.mult)
                            else:
                                o1 = acc1[:, qi * 65:qi * 65 + 64]
                                p.dve.reciprocal(out=rz[:, 0:1], in_=acc0[:, qi * 65 + 64:qi * 65 + 65])
                                p.dve.reciprocal(out=rz[:, 1:2], in_=acc1[:, qi * 65 + 64:qi * 65 + 65])
                                p.dve.tensor_tensor(out=r1l[:], in0=rz[:, 1:2], in1=nlam[:], op=ALU.mult)
                                p.dve.tensor_scalar(out=t0[:], in0=o0, scalar1=rz[:, 0:1], scalar2=None, op0=ALU.mult)
                                p.dve.scalar_tensor_tensor(out=YO[:, t, h * 64:(h + 1) * 64], in0=o1, scalar=r1l[:, 0:1], in1=t0[:],
                                                           op0=ALU.mult, op1=ALU.add)
                    phase1(0)
                    for i in range(len(items)):
                        if i + 1 < len(items):
                            phase1(i + 1)
                        phase2(i)
                if substop == "core":
                    raise Stop()
                c0 = 2 if nat else 6
                if nat:
                    for t in range(NT):
                        store_mix(path, YO[:, t, :], t, c0)
                else:
                    rms2 = group_rms(sc, None, 4, 64, 1e-6, None, None, "b")
                    if own:
                        yb = [sc.sb(f"ybw{i}", [128, 1024]) for i in range(2)]
                        for t in range(NTQ):
                            rms2(YO[:, t, :], GS4[:], yb[t % 2][:, 0:256])
                            store_mix(path, yb[t % 2][:], t, c0, own=True)
                    else:
                        yb = [sc.sb(f"yb{i}", [128, 256], BF16) for i in range(2)]
                        for t in range(NTQ):
                            rms2(YO[:, t, :], GS4[:], yb[t % 2][:])
                            store_mix(path, yb[t % 2][:], t, c0)
            p.nrot = 8

        def ssd_mixer(path, l, hnT):
            T, NT, nj = path.T, path.NT, path.nj
            hv = hview(hnT, path)
            with p.scope() as sc:
                ZS = sc.sb("ZS", [128, NT, 256], BF16)
                XS = sc.sb("XS", [128, NT, 256], BF16)
                BTM = sc.sb("BTM", [128, NT, 128], BF16)
                DT = sc.sb("DT", [128, NT, 8])
                LA = sc.sb("LA", [128, NT, 8])
                YACC = sc.sb("YACC", [128, NT, 256])
                BCT = sc.sb("BCT", [64, 4, T], BF16)
                Abc = sc.sb("Abc", [128, 8])
                dtb = sc.sb("dtb", [128, 8])
                Dbc = sc.sb("Dbc", [128, 4])
                cbx = sc.sb("cbx", [128, 384])
                cbf = sc.sb("cbf", [64, 4])
                ng = sc.sb("ng", [128, 256])
                bcast_load(Abc[:], A["ssd_a_log"][l:l + 1, :])
                p.act.activation(out=Abc[:], in_=Abc[:], func=AF.Exp)
                p.dve.tensor_scalar_mul(out=Abc[:], in0=Abc[:], scalar1=-1.0)
                bcast_load(dtb[:], A["ssd_dt_bias"][l:l + 1, :])
                bcast_load(Dbc[:], A["ssd_d"][l:l + 1, :])
                bcast_load(cbx[:], A["ssd_conv_b"][l:l + 1, 0:384])
                p.dma(out=cbf[:], in_=A["ssd_conv_b"][l, 256:512].rearrange("(c p) -> p c", p=64), allow_slow_non_contiguous=True)
                bcast_load(ng[:], A["ssd_norm_g"][l:l + 1, :])
                with p.scope() as psc:
                    cwb = psc.sb("cwb", [128, 5, 256])
                    tA = psc.sb("tA", [128, 384])
                    xsf = psc.sb("xsf", [128, 256])
                    dtr = psc.sb("dtr", [128, 8])
                    dte0 = psc.sb("dte0", [128, 8])
                    lhs = lambda s, j, kc, sh=0: hv[:, kc, s, 2 + 128 * j + sh:2 + 128 * (j + 1) + sh]
                    wz = load_w(A["w_in"][l, :, OFF["z"]:OFF["z"] + 256], 256)
                    wdt = load_w(A["w_in"][l, :, OFF["dt"]:OFF["dt"] + 8], 8)
                    for (s, j) in (path.tiles if 'j1' not in SKIP else []):
                        t = path.tidx(s, j)
                        bk = p.bank()
                        for kc in range(8):
                            p.pe.matmul(out=bk[:, 0:256], lhsT=lhs(s, j, kc), rhs=wz[:, kc, :], start=(kc == 0), stop=(kc == 7))
                        for kc in range(8):
                            p.pe.matmul(out=bk[:, 256:264], lhsT=lhs(s, j, kc), rhs=wdt[:, kc, :], start=(kc == 0), stop=(kc == 7))
                        p.act.activation(out=ZS[:, t, :], in_=bk[:, 0:256], func=(AF.Copy if 'e1' in SKIP else AF.Silu))
                        if 'nodt' in SKIP:
                            continue
                        p.dve.tensor_tensor(out=dtr[:], in0=bk[:, 256:264], in1=dtb[:], op=ALU.add)
                        p.act.activation(out=dte0[:], in_=dtr[:], func=AF.Exp)
                        p.dve.tensor_scalar_add(out=dtr[:], in0=dte0[:], scalar1=1.0)
                        p.act.activation(out=DT[:, t, :], in_=dtr[:], func=AF.Ln)
                        p.dve.tensor_tensor(out=LA[:, t, :], in0=DT[:, t, :], in1=Abc[:], op=ALU.mult)
                    for (c0, n, which) in (((OFF["xs"], 256, "x"), (OFF["B"], 128, "b")) if 'j2' not in SKIP else ()):
                        taps = load_taps(cwb, l, c0, n, 5, "ssd_conv_w", c0 - OFF["xs"])
                        for (s, j) in path.tiles:
                            t = path.tidx(s, j)
                            bk = p.bank()
                            i, ntot = 0, len(taps) * 8
                            for (wv_, sh) in taps:
                                for kc in range(8):
                                    p.pe.matmul(out=bk[:, 0:n], lhsT=lhs(s, j, kc, sh), rhs=wv_[:, kc, :], start=(i == 0), stop=(i == ntot - 1))
                                    i += 1
                            cb0 = c0 - OFF["xs"]
                            p.dve.tensor_tensor(out=tA[:, 0:n], in0=bk[:, 0:n], in1=cbx[:, cb0:cb0 + n], op=ALU.add)
                            if which == "x":
                                p.act.activation(out=xsf[:], in_=tA[:, 0:n], func=AF.Silu)
                                p.pool.tensor_copy(out=XS[:, t, :], in_=xsf[:])
                                p.dve.tensor_tensor(out=V(YACC[:, t, :], 4), in0=V(xsf[:], 4), in1=BC(Dbc[:].unsqueeze(2), [128, 4, 64]), op=ALU.mult)
                            else:
                                p.act.activation(out=BTM[:, t, :], in_=tA[:, 0:n], func=AF.Silu)
                    taps = load_taps(cwb, l, OFF["B"], 256, 5, "ssd_conv_w", OFF["B"] - OFF["xs"])
                    nb = min(512, path.L)
                    for s in (range(path.ns) if 'j3' not in SKIP else []):
                        for b0 in range(0, path.L, nb):
                            for pc in range(4):
                                bk = p.bank()
                                i, ntot = 0, len(taps) * 8
                                for (wv_, sh) in taps:
                                    for kc in range(8):
                                        p.pe.matmul(out=bk[0:64, 0:nb], lhsT=wv_[:, kc, pc * 64:(pc + 1) * 64],
                                                    rhs=hv[:, kc, s, 2 + b0 + sh:2 + b0 + nb + sh], start=(i == 0), stop=(i == ntot - 1))
                                        i += 1
                                tok0 = s * path.L + b0
                                p.act.activation(out=BCT[:, pc, tok0:tok0 + nb], in_=bk[0:64, 0:nb], func=AF.Silu, bias=cbf[:, pc:pc + 1])
                if substop == "ssdproj":
                    raise Stop()
                nsd = path.ns * 2
                ST = [sc.sb(f"ST{i}", [64, 256]) for i in range(nsd)]
                STb = [sc.sb(f"STb{i}", [64, 256], BF16) for i in range(nsd)]
                st0 = sc.sb("st0", [64, 4, 64])
                R1 = [sc.sb(f"R1_{i}", [128, 512]) for i in range(2)]
                R2 = [sc.sb(f"R2_{i}", [128, 512]) for i in range(2)]
                LM = [sc.sb(f"LM{i}", [128, 512], BF16) for i in range(2)]
                GM = [sc.sb(f"GM{i}", [128, 512], BF16) for i in range(2)]
                css = [sc.sb(f"css{i}", [128, 8]) for i in range(2)]
                ecs = [sc.sb(f"ecs{i}", [128, 4]) for i in range(2)]
                dec = [sc.sb(f"dec{i}", [128, 4]) for i in range(2)]
                dif = [sc.sb(f"dif{i}", [128, 4]) for i in range(2)]
                dte = [sc.sb(f"dte{i}", [128, 4]) for i in range(2)]
                ddt = [sc.sb(f"ddt{i}", [128, 4]) for i in range(2)]
                xdt = [sc.sb(f"xdt{i}", [128, 256], BF16) for i in range(2)]
                xdd = [sc.sb(f"xdd{i}", [128, 256], BF16) for i in range(2)]
                yo = [sc.sb(f"yo{i}", [128, 256]) for i in range(2)]
                for s in range(path.ns):
                    for d in range(2):
                        sd = s * 2 + d
                        if path.is_ctx:
                            p.dve.memset(ap=ST[sd][:], constant=0.0)
                            p.pool.memset(ap=STb[sd][:], constant=0.0)
                        else:
                            p.dma(out=ST[sd][:], in_=A["st_ssd"][l, d])
                            p.act.copy(out=STb[sd][:], in_=ST[sd][:])
                if substop == "ssdinit":
                    raise Stop()
                def schunk(d, s, step):
                    j = step if d == 0 else nj - 1 - step
                    t = path.tidx(s, j)
                    sd = s * 2 + d
                    q = d
                    tok = slice(t * 128, (t + 1) * 128)
                    la_d = LA[:, t, d * 4:(d + 1) * 4]
                    dt_d = DT[:, t, d * 4:(d + 1) * 4]
                    p.dve.tensor_tensor(out=V(R1[q][:], 4), in0=BC(tri[:, d, :].unsqueeze(1), [128, 4, 128]),
                                        in1=BC(la_d.unsqueeze(2), [128, 4, 128]), op=ALU.mult)
                    p.pool.tensor_scalar_mul(out=V(R2[q][:], 4), in0=BC(la_d.unsqueeze(2), [128, 4, 128]), scalar1=-1.0)
                    yield
                    bS = p.bank()
                    p.pe.matmul(out=bS[:, 0:512], lhsT=onesf[:], rhs=R1[q][:], start=True, stop=False)
                    p.pe.matmul(out=bS[:, 0:512], lhsT=tri[:, d, :], rhs=R2[q][:], start=False, stop=False)
                    p.pe.matmul(out=bS[:, 0:512], lhsT=identf[:], rhs=neg4[:, d, :], start=False, stop=True)
                    p.act.activation(out=LM[q][:], in_=bS[:, 0:512], func=AF.Exp)
                    bC = p.bank()
                    p.pe.matmul(out=bC[:, 0:4], lhsT=tri[:, d, :], rhs=la_d, start=True, stop=True)
                    p.pe.matmul(out=bC[:, 4:8], lhsT=onesf[:], rhs=la_d, start=True, stop=True)
                    yield
                    p.dve.tensor_copy(out=css[q][:], in_=bC[:, 0:8])
                    p.act.activation(out=ecs[q][:], in_=css[q][:, 0:4], func=AF.Exp)
                    p.act.activation(out=dec[q][:], in_=css[q][:, 4:8], func=AF.Exp)
                    yield
                    p.dve.tensor_tensor(out=dif[q][:], in0=css[q][:, 4:8], in1=css[q][:, 0:4], op=ALU.subtract)
                    p.act.activation(out=dte[q][:], in_=dif[q][:], func=AF.Exp)
                    yield
                    p.dve.tensor_tensor(out=ddt[q][:], in0=dt_d, in1=dte[q][:], op=ALU.mult)
                    p.dve.tensor_tensor(out=V(xdt[q][:], 4), in0=V(XS[:, t, :], 4), in1=BC(dt_d.unsqueeze(2), [128, 4, 64]), op=ALU.mult)
                    p.pool.tensor_tensor(out=V(xdd[q][:], 4), in0=V(XS[:, t, :], 4), in1=BC(ddt[q][:].unsqueeze(2), [128, 4, 64]), op=ALU.mult)
                    yield
                    bB = p.bank()
                    for g in range(2):
                        p.pe.matmul(out=bB[:, g * 128:(g + 1) * 128], lhsT=BCT[:, g, tok], rhs=BCT[:, 2 + g, tok], start=True, stop=True)
                    gm4 = GM[q][:].rearrange("p (g r l) -> p g r l", g=2, r=2)
                    lm4 = LM[q][:].rearrange("p (g r l) -> p g r l", g=2, r=2)
                    p.dve.tensor_tensor(out=gm4, in0=BC(V(bB[:, 0:256], 2).unsqueeze(2), [128, 2, 2, 128]), in1=lm4, op=ALU.mult)
                    yield
                    bY = p.bank()
                    for h in range(4):
                        p.pe.matmul(out=bY[:, h * 64:(h + 1) * 64], lhsT=GM[q][:, h * 128:(h + 1) * 128], rhs=xdt[q][:, h * 64:(h + 1) * 64],
                                    start=True, stop=True)
                    bO = p.bank()
                    for h in range(4):
                        p.pe.matmul(out=bO[:, h * 64:(h + 1) * 64], lhsT=BCT[:, 2 + h // 2, tok], rhs=STb[sd][:, h * 64:(h + 1) * 64],
                                    start=True, stop=True)
                    p.dve.tensor_tensor(out=V(yo[q][:], 4), in0=V(bO[:, 0:256], 4), in1=BC(ecs[q][:].unsqueeze(2), [128, 4, 64]), op=ALU.mult)
                    yield
                    p.dve.tensor_tensor(out=S(YACC[:, t, :], t), in0=bY[:, 0:256], in1=S(YACC[:, t, :], t), op=ALU.add)
                    p.pool.tensor_tensor(out=S(YACC[:, t, :], t), in0=S(YACC[:, t, :], t), in1=yo[q][:], op=ALU.add)
                    yield
                    bN = p.bank()
                    for g in range(2):
                        p.pe.matmul(out=bN[0:64, g * 128:(g + 1) * 128], lhsT=BTM[:, t, g * 64:(g + 1) * 64], rhs=xdd[q][:, g * 128:(g + 1) * 128],
                                    start=True, stop=True)
                    p.dve.tensor_tensor(out=V(ST[sd][:], 4), in0=V(ST[sd][:], 4), in1=BC(dec[q][0:64, :].unsqueeze(2), [64, 4, 64]), op=ALU.mult)
                    p.dve.tensor_tensor(out=ST[sd][:], in0=ST[sd][:], in1=bN[0:64, 0:256], op=ALU.add)
                    p.act.copy(out=STb[sd][:], in_=ST[sd][:])
                for step in range(nj):
                    for s in range(path.ns):
                        gens = [schunk(d, s, step) for d in range(2)]
                        while gens:
                            for g in list(gens):
                                try:
                                    next(g)
                                except StopIteration:
                                    gens.remove(g)
                if path.is_ctx:
                    for s in range(path.ns):
                        for d in range(2):
                            bk = p.bank()
                            for h in range(4):
                                p.pe.transpose(out=bk[0:64, h * 64:(h + 1) * 64], in_=ST[s * 2 + d][:, h * 64:(h + 1) * 64], identity=identf[0:64, 0:64])
                            p.dve.tensor_copy(out=st0[:], in_=V(bk[0:64, 0:256], 4))
                            p.dma(out=O["o_ssd"][s, l, d].rearrange("h p n -> p h n"), in_=st0[:], key=("st_s", 0))
                rmsy = group_rms(sc, None, 1, 256, 1e-6, None, None, "s")
                yg = sc.sb("yg", [128, 256])
                yb = [sc.sb(f"ybs{i}", [128, 256], BF16) for i in range(2)]
                for t in range(NT):
                    p.dve.tensor_tensor(out=yg[:], in0=YACC[:, t, :], in1=ZS[:, t, :], op=ALU.mult)
                    rmsy(yg[:], ng[:], yb[t % 2][:])
                    store_mix(path, yb[t % 2][:], t, 0)

        def rwkv_alloc(path, sc):
            NT, T = path.NT, path.T
            return dict(RB=sc.sb("RB", [128, NT, 256], BF16), KB=sc.sb("KB", [128, NT, 256], BF16),
                        VB=sc.sb("VB", [128, NT, 256], BF16), KKB=sc.sb("KKB", [128, NT, 256], BF16),
                        BS=sc.sb("BS", [128, NT, 4]), TW=sc.sb("TW", [128, T], BF16), AD=sc.sb("AD", [128, T], BF16),
                        SG=sc.sb("SG", [128, T], BF16))

        def rwkv_proj(path, l, hnT, W):
            RB, KB, VB, KKB, BS, TW, AD, SG = (W[k] for k in ("RB", "KB", "VB", "KKB", "BS", "TW", "AD", "SG"))
            hv = hview(hnT, path)
            lhs = lambda s, j, kc, sh=0: hv[:, kc, s, 2 + 128 * j + sh:2 + 128 * (j + 1) + sh]
            with p.scope() as psc:
                kkg = psc.sb("kkg", [128, 256])
                rkg = psc.sb("rkg", [128, 256])
                bcast_load(kkg[:], A["rwkv_k_k"][l:l + 1, :])
                bcast_load(rkg[:], A["rwkv_r_k"][l:l + 1, :])
                cwb = psc.sb("cwb", [128, 5, 256])
                kf = psc.sb("kf", [128, 256])
                t1 = psc.sb("t1", [128, 256])
                t2 = psc.sb("t2", [128, 256])
                l2n = group_rms(psc, None, 4, 64, 1e-12, None, None, "k", l2=True)
                for gi, c0 in enumerate((OFF["rr"], OFF["rk"], OFF["rv"])):
                    taps = load_taps(cwb, l, c0, 256, 3, "rwkv_conv_w", c0 - OFF["rr"])
                    for (s, j) in path.tiles:
                        t = path.tidx(s, j)
                        bk = p.bank()
                        i, ntot = 0, len(taps) * 8
                        for (wv_, sh) in taps:
                            for kc in range(8):
                                p.pe.matmul(out=bk[:, 0:256], lhsT=lhs(s, j, kc, sh), rhs=wv_[:, kc, :], start=(i == 0), stop=(i == ntot - 1))
                                i += 1
                        if gi == 0:
                            p.act.copy(out=RB[:, t, :], in_=bk[:, 0:256])
                        elif gi == 2:
                            p.act.copy(out=VB[:, t, :], in_=bk[:, 0:256])
                        else:
                            p.act.copy(out=kf[:], in_=bk[:, 0:256])
                            p.pool.tensor_copy(out=KB[:, t, :], in_=kf[:])
                            p.dve.tensor_tensor(out=t1[:], in0=kf[:], in1=kkg[:], op=ALU.mult)
                            l2n(t1[:], None, KKB[:, t, :])
                            p.dve.tensor_tensor(out=t2[:], in0=RB[:, t, :], in1=rkg[:], op=ALU.mult)
                            p.pool.tensor_tensor(out=t2[:], in0=t2[:], in1=kf[:], op=ALU.mult)
                            p.dve.tensor_reduce(out=BS[:, t, :], in_=V(t2[:], 4), axis=AX.X, op=ALU.add)
                nb = min(512, path.L)
                for gi, c0 in enumerate((OFF["wdn"], OFF["adn"], OFF["gdn"])):
                    taps = load_taps(cwb, l, c0, 128, 3, "rwkv_conv_w", c0 - OFF["rr"])
                    for s in range(path.ns):
                        for b0 in range(0, path.L, nb):
                            bk = p.bank()
                            i, ntot = 0, len(taps) * 8
                            for (wv_, sh) in taps:
                                for kc in range(8):
                                    p.pe.matmul(out=bk[:, 0:nb], lhsT=wv_[:, kc, :], rhs=hv[:, kc, s, 2 + b0 + sh:2 + b0 + nb + sh],
                                                start=(i == 0), stop=(i == ntot - 1))
                                    i += 1
                            tok0 = s * path.L + b0
                            if gi == 0:
                                p.act.activation(out=TW[:, tok0:tok0 + nb], in_=bk[:, 0:nb], func=AF.Tanh)
                            elif gi == 1:
                                p.dve.tensor_copy(out=AD[:, tok0:tok0 + nb], in_=bk[:, 0:nb])
                            else:
                                p.act.activation(out=SG[:, tok0:tok0 + nb], in_=bk[:, 0:nb], func=AF.Sigmoid)

        def rwkv_scan(path, l, W):
            RB, KB, VB, KKB, BS, TW, AD, SG = (W[k] for k in ("RB", "KB", "VB", "KKB", "BS", "TW", "AD", "SG"))
            T, NT, nj = path.T, path.NT, path.nj
            NEG_E = -math.exp(-0.5)
            with p.scope() as sc:
                YACC = sc.sb("YACCr", [128, NT, 256])
                kag = sc.sb("kag", [128, 256])
                lng = sc.sb("lng", [128, 256])
                lnb = sc.sb("lnb", [128, 256])
                w0b = sc.sb("w0b", [128, 2, 256])
                a0b = sc.sb("a0b", [128, 2, 256])
                wup = sc.sb("wup", [128, 256], BF16)
                aup = sc.sb("aup", [128, 256], BF16)
                gup = sc.sb("gup", [128, 256], BF16)
                for tl, nm in ((kag, "rwkv_k_a"), (lng, "rwkv_ln_g"), (lnb, "rwkv_ln_b")):
                    bcast_load(tl[:], A[nm][l:l + 1, :])
                bcast_load(w0b[:], A["rwkv_w0"][l])
                bcast_load(a0b[:], A["rwkv_a0"][l])
                p.dma(out=wup[:], in_=A["rwkv_w_up"][l], eng="pool")
                p.dma(out=aup[:], in_=A["rwkv_a_up"][l], eng="pool")
                p.dma(out=gup[:], in_=A["rwkv_g_up"][l], eng="pool")
                p.dve.memset(ap=YACC[:], constant=0.0)
                with p.scope() as ssc:
                    nsd = path.ns * 2
                    ST = [ssc.sb(f"RST{i}", [64, 256]) for i in range(nsd)]
                    st0 = ssc.sb("rst0", [64, 4, 64])

                    def tempset(i):
                        X = {}
                        for nm in ("LW", "af", "ab", "kt", "e1", "e1i", "e1x", "e2"):
                            X[nm] = ssc.sb(f"{nm}{i}", [128, 256])
                        X["xw"] = X["e1"]
                        X["Vf"] = X["e1x"]
                        X["Q4"] = ssc.sb(f"Q4{i}", [128, 4, 256], BF16)
                        X["Abe"] = ssc.sb(f"Abe{i}", [128, 256])
                        X["Kte"] = ssc.sb(f"Kte{i}", [128, 256])
                        X["ZSf"] = ssc.sb(f"ZSf{i}", [128, 256])
                        X["NUf"] = ssc.sb(f"NUf{i}", [128, 256])
                        X["CV"] = ssc.sb(f"CV{i}", [64, 8])
                        X["TT"] = ssc.sb(f"TT{i}", [64, 16, 128], BF16)
                        X["AM"] = ssc.sb(f"AM{i}", [128, 4, 512], BF16)
                        for nm in ("N0", "NT0", "R0", "R1", "RT0", "RT1", "B0", "B1", "B2", "B3"):
                            X[nm] = ssc.sb(f"{nm}{i}", [128, 4, 128])
                        X["STt"] = ssc.sb(f"STt{i}", [64, 4, 64], BF16)
                        X["NU"] = ssc.sb(f"NU{i}", [128, 256], BF16)
                        return X
                    TS = [tempset(0), tempset(1)]
                    for s in range(path.ns):
                        for d in range(2):
                            sd = s * 2 + d
                            if path.is_ctx:
                                p.dve.memset(ap=ST[sd][:], constant=0.0)
                            else:
                                p.dma(out=ST[sd][:], in_=A["st_rw"][l, d])

                    def mm4(lhs_, rhs_):
                        b_ = p.bank()
                        for h in range(4):
                            p.pe.matmul(out=b_[:, h * 128:(h + 1) * 128], lhsT=lhs_[:, h, :], rhs=rhs_[:, h, :], start=True, stop=True)
                        return V(b_[:, 0:512], 4)

                    def chunk(d, s, step, X):
                        j = step if d == 0 else nj - 1 - step
                        t = path.tidx(s, j)
                        sd = s * 2 + d
                        tok = slice(t * 128, (t + 1) * 128)
                        LW, xw, af, ab, kt, e1, e1i, e1x, e2 = (X[k] for k in ("LW", "xw", "af", "ab", "kt", "e1", "e1i", "e1x", "e2"))
                        Q4, Abe, Kte, CV, TT, AM, STt, NU = (X[k] for k in ("Q4", "Abe", "Kte", "CV", "TT", "AM", "STt", "NU"))
                        bW = p.bank()
                        p.pe.matmul(out=bW[:, 0:256], lhsT=TW[d * 64:(d + 1) * 64, tok], rhs=wup[d * 64:(d + 1) * 64, :], start=True, stop=True)
                        bA = p.bank()
                        p.pe.matmul(out=bA[:, 0:256], lhsT=AD[d * 64:(d + 1) * 64, tok], rhs=aup[d * 64:(d + 1) * 64, :], start=True, stop=True)
                        p.dve.tensor_tensor(out=xw[:], in0=bW[:, 0:256], in1=w0b[:, d, :], op=ALU.add)
                        p.dve.tensor_tensor(out=af[:], in0=bA[:, 0:256], in1=a0b[:, d, :], op=ALU.add)
                        yield
                        p.act.activation(out=LW[:], in_=xw[:], func=AF.Sigmoid)
                        p.act.activation(out=af[:], in_=af[:], func=AF.Sigmoid)
                        yield
                        p.dve.tensor_scalar_mul(out=LW[:], in0=LW[:], scalar1=NEG_E)
                        p.dve.tensor_tensor(out=ab[:], in0=af[:], in1=KKB[:, t, :], op=ALU.mult)
                        p.dve.scalar_tensor_tensor(out=kt[:], in0=af[:], scalar=-1.0, in1=kag[:], op0=ALU.add, op1=ALU.mult)
                        p.dve.scalar_tensor_tensor(out=kt[:], in0=kt[:], scalar=1.0, in1=KB[:, t, :], op0=ALU.add, op1=ALU.mult)
                        bX = [p.bank() for _ in range(3)]
                        for i in range(3):
                            p.pe.matmul(out=bX[i][:, 0:256], lhsT=rwm[:, d, i, :], rhs=LW[:], start=True, stop=True)
                        bV = p.bank()
                        for h in range(4):
                            p.pe.matmul(out=bV[0:64, h * 2:(h + 1) * 2], lhsT=LW[:, h * 64:(h + 1) * 64], rhs=rwsel[:, d, :], start=True, stop=True)
                        yield
                        p.act.activation(out=e1[:], in_=bX[0][:, 0:256], func=AF.Exp)
                        p.act.activation(out=e1i[:], in_=bX[0][:, 0:256], func=AF.Exp, scale=-1.0)
                        p.act.activation(out=e2[:], in_=bX[1][:, 0:256], func=AF.Exp)
                        p.act.activation(out=e1x[:], in_=bX[2][:, 0:256], func=AF.Exp)
                        p.act.activation(out=CV[:], in_=bV[0:64, 0:8], func=AF.Exp)
                        yield
                        p.dve.tensor_tensor(out=Q4[:, 0, :], in0=KKB[:, t, :], in1=e1x[:], op=ALU.mult)
                        p.pool.tensor_tensor(out=Q4[:, 1, :], in0=RB[:, t, :], in1=e1[:], op=ALU.mult)
                        p.dve.tensor_tensor(out=Q4[:, 2, :], in0=ab[:], in1=e1i[:], op=ALU.mult)
                        p.pool.tensor_tensor(out=Q4[:, 3, :], in0=kt[:], in1=e1i[:], op=ALU.mult)
                        p.dve.tensor_tensor(out=Abe[:], in0=ab[:], in1=e2[:], op=ALU.mult)
                        p.pool.tensor_tensor(out=Kte[:], in0=kt[:], in1=e2[:], op=ALU.mult)
                        yield
                        bT = [p.bank().bitcast(BF16) for _ in range(2)]
                        for h in range(4):
                            for a in range(4):
                                idx = h * 4 + a
                                p.pe.transpose(out=bT[idx // 8][0:64, (idx % 8) * 128:(idx % 8 + 1) * 128], in_=Q4[:, a, h * 64:(h + 1) * 64],
                                               identity=ident[:])
                        yield
                        p.act.copy(out=TT[:, 0:8, :], in_=V(bT[0][0:64, 0:1024], 8))
                        p.act.copy(out=TT[:, 8:16, :], in_=V(bT[1][0:64, 0:1024], 8))
                        yield
                        for hp in range(2):
                            bks_ = []
                            for h in (2 * hp, 2 * hp + 1):
                                bA4 = p.bank()
                                rq = TT[:, h * 4:h * 4 + 2, :].rearrange("p a t -> p (a t)")
                                p.pe.matmul(out=bA4[:, 0:256], lhsT=TT[:, h * 4 + 2, :], rhs=rq, start=True, stop=True)
                                p.pe.matmul(out=bA4[:, 256:512], lhsT=TT[:, h * 4 + 3, :], rhs=rq, start=True, stop=True)
                                bks_.append((h, bA4))
                            yield
                            for (h, bA4) in bks_:
                                p.dve.tensor_tensor(out=X["N0"][:, h, :], in0=bA4[:, 0:128], in1=rwmk[:, d, 0:128], op=ALU.mult)
                                p.dve.tensor_tensor(out=AM[:, h, 128:512], in0=bA4[:, 128:512], in1=rwmk[:, d, 128:512], op=ALU.mult)
                        bA1 = p.bank()
                        for h in range(4):
                            p.pe.matmul(out=bA1[:, h * 128:(h + 1) * 128], lhsT=TT[:, h * 4 + 0, :], rhs=TT[:, h * 4 + 2, :], start=True, stop=True)
                        yield
                        p.dve.tensor_tensor(out=X["NT0"][:], in0=V(bA1[:, 0:512], 4), in1=BC(rwmkt[:, d, :].unsqueeze(1), [128, 4, 128]), op=ALU.mult)
                        uN, uT = (0, 1) if d == 0 else (1, 0)
                        mB = lambda ul, li: BC(blk[:, ul, li, :].unsqueeze(1), [128, 4, 128])
                        idb = BC(identf[:].unsqueeze(1), [128, 4, 128])
                        N0, NT0 = X["N0"], X["NT0"]
                        Rm, RTm = [X["R0"], X["R1"]], [X["RT0"], X["RT1"]]
                        Pb, PTb = [X["B0"], X["B1"]], [X["B2"], X["B3"]]
                        p.dve.tensor_tensor(out=Pb[0][:], in0=N0[:], in1=mB(0, 0), op=ALU.mult)
                        p.pool.tensor_tensor(out=PTb[0][:], in0=NT0[:], in1=mB(0, 0), op=ALU.mult)
                        p.dve.tensor_tensor(out=Rm[0][:], in0=Pb[0][:], in1=idb, op=ALU.add)
                        p.pool.tensor_tensor(out=RTm[0][:], in0=PTb[0][:], in1=idb, op=ALU.add)
                        yield
                        cur = 0
                        for k in range(2):
                            nxt = 1 - cur
                            m1 = mm4(PTb[cur], Pb[cur])
                            m2 = mm4(Pb[cur], PTb[cur])
                            yield
                            p.dve.tensor_copy(out=Pb[nxt][:], in_=m1)
                            p.act.copy(out=PTb[nxt][:], in_=m2)
                            yield
                            m3 = mm4(PTb[nxt], Rm[cur])
                            m4 = mm4(Rm[cur], PTb[nxt])
                            yield
                            p.dve.tensor_tensor(out=Rm[nxt][:], in0=m3, in1=Rm[cur][:], op=ALU.add)
                            p.dve.tensor_tensor(out=RTm[nxt][:], in0=m4, in1=RTm[cur][:], op=ALU.add)
                            yield
                            cur = nxt
                        Xm, X2m = Pb[1], PTb[1]
                        for li in range(4):
                            nxt = 1 - cur
                            Dm, DTm = Rm[cur], RTm[cur]
                            m1 = mm4(NT0, Dm)
                            m2 = mm4(N0, DTm) if li < 3 else None
                            yield
                            p.dve.tensor_tensor(out=Xm[:], in0=m1, in1=mB(uN, li + 1), op=ALU.mult)
                            if li < 3:
                                p.dve.tensor_tensor(out=X2m[:], in0=m2, in1=mB(uT, li + 1), op=ALU.mult)
                            yield
                            m3 = mm4(DTm, Xm)
                            m4 = mm4(Dm, X2m) if li < 3 else None
                            yield
                            p.dve.tensor_tensor(out=Rm[nxt][:], in0=m3, in1=Dm[:], op=ALU.add)
                            if li < 3:
                                p.dve.tensor_tensor(out=RTm[nxt][:], in0=m4, in1=DTm[:], op=ALU.add)
                            yield
                            cur = nxt
                        RT = Rm[cur]
                        ZSf, NUf, Vf = X["ZSf"], X["NUf"], X["Vf"]
                        p.pool.tensor_copy(out=Vf[:], in_=VB[:, t, :])
                        cv3 = CV[:].rearrange("p (h two) -> p h two", two=2)
                        p.dve.tensor_tensor(out=STt[:], in0=V(ST[sd][:], 4), in1=BC(cv3[:, :, 1:2], [64, 4, 64]), op=ALU.mult)
                        yield
                        bZ = p.bank()
                        for h in range(4):
                            hs = slice(h * 64, (h + 1) * 64)
                            p.pe.matmul(out=bZ[:, hs], lhsT=TT[:, h * 4 + 0, :], rhs=STt[:, h, :], start=True, stop=False)
                            p.pe.matmul(out=bZ[:, hs], lhsT=AM[:, h, 256:384], rhs=VB[:, t, hs], start=False, stop=True)
                        yield
                        p.act.copy(out=ZSf[:], in_=bZ[:, 0:256])
                        yield
                        bU = p.bank()
                        for h in range(4):
                            hs = slice(h * 64, (h + 1) * 64)
                            p.pe.matmul(out=bU[:, hs], lhsT=RT[:, h, :], rhs=ZSf[:, hs], start=True, stop=True)
                        yield
                        p.act.mul(out=NU[:], in_=bU[:, 0:256], mul=-1.0)
                        p.dve.tensor_scalar_mul(out=NUf[:], in0=bU[:, 0:256], scalar1=-1.0)
                        yield
                        bYr = p.bank()
                        for h in range(4):
                            hs = slice(h * 64, (h + 1) * 64)
                            p.pe.matmul(out=bYr[:, hs], lhsT=TT[:, h * 4 + 1, :], rhs=STt[:, h, :], start=True, stop=False)
                            p.pe.matmul(out=bYr[:, hs], lhsT=AM[:, h, 128:256], rhs=NU[:, hs], start=False, stop=False)
                            p.pe.matmul(out=bYr[:, hs], lhsT=AM[:, h, 384:512], rhs=VB[:, t, hs], start=False, stop=True)
                        bSN = p.bank()
                        for h in range(4):
                            hs = slice(h * 64, (h + 1) * 64)
                            p.pe.matmul(out=bSN[0:64, hs], lhsT=Abe[:, hs], rhs=NUf[:, hs], start=True, stop=False)
                            p.pe.matmul(out=bSN[0:64, hs], lhsT=Kte[:, hs], rhs=Vf[:, hs], start=False, stop=True)
                        yield
                        p.dve.tensor_tensor(out=S(YACC[:, t, :], t), in0=bYr[:, 0:256], in1=S(YACC[:, t, :], t), op=ALU.add)
                        p.dve.tensor_tensor(out=V(ST[sd][:], 4), in0=V(ST[sd][:], 4), in1=BC(cv3[:, :, 0:1], [64, 4, 64]), op=ALU.mult)
                        p.dve.tensor_tensor(out=ST[sd][:], in0=ST[sd][:], in1=bSN[0:64, 0:256], op=ALU.add)

                    for step in range(nj):
                        for s in range(path.ns):
                            gens = [chunk(d, s, step, TS[d]) for d in range(2)]
                            while gens:
                                for g in list(gens):
                                    try:
                                        next(g)
                                    except StopIteration:
                                        gens.remove(g)
                    if path.is_ctx:
                        for s in range(path.ns):
                            for d in range(2):
                                bk = p.bank()
                                for h in range(4):
                                    p.pe.transpose(out=bk[0:64, h * 64:(h + 1) * 64], in_=ST[s * 2 + d][:, h * 64:(h + 1) * 64], identity=identf[0:64, 0:64])
                                p.dve.tensor_copy(out=st0[:], in_=V(bk[0:64, 0:256], 4))
                                p.dma(out=O["o_rw"][s, l, d].rearrange("h v k -> v h k"), in_=st0[:], key=("st_s", 0))
                mu = sc.sb("mu", [128, 4])
                cen = sc.sb("cen", [128, 256])
                rmsl = group_rms(sc, None, 4, 64, 64e-5, None, None, "l")
                yn = sc.sb("yn", [128, 256])
                bon = sc.sb("bon", [128, 256])
                yb = [sc.sb(f"ybr{i}", [128, 256], BF16) for i in range(2)]
                for t in range(NT):
                    tok = slice(t * 128, (t + 1) * 128)
                    p.dve.tensor_reduce(out=mu[:], in_=V(YACC[:, t, :], 4), axis=AX.X, op=ALU.add)
                    p.dve.tensor_scalar_mul(out=mu[:], in0=mu[:], scalar1=-1.0 / 64)
                    p.dve.tensor_tensor(out=V(cen[:], 4), in0=V(YACC[:, t, :], 4), in1=BC(mu[:].unsqueeze(2), [128, 4, 64]), op=ALU.add)
                    rmsl(cen[:], lng[:], yn[:])
                    p.dve.tensor_tensor(out=yn[:], in0=yn[:], in1=lnb[:], op=ALU.add)
                    p.pool.tensor_tensor(out=V(bon[:], 4), in0=V(VB[:, t, :], 4), in1=BC(BS[:, t, :].unsqueeze(2), [128, 4, 64]), op=ALU.mult)
                    p.dve.tensor_tensor(out=yn[:], in0=yn[:], in1=bon[:], op=ALU.add)
                    bG = p.bank()
                    p.pe.matmul(out=bG[:, 0:256], lhsT=SG[:, tok], rhs=gup[:], start=True, stop=True)
                    p.dve.tensor_tensor(out=yb[t % 2][:], in0=yn[:], in1=bG[:, 0:256], op=ALU.mult)
                    store_mix(path, yb[t % 2][:], t, 4)

        def run_path(path):
            cond_setup(path)
            hshape = [128, 8, LS + 4 if path.name == "S" else NSP * (LP + 4)]
            for l in range(layers):
                with p.scope() as scl0:
                    rws = rwkv_alloc(path, scl0)
                    with p.scope() as scl:
                        hnT = scl.sb("hnT", hshape, BF16)
                        p.pool.memset(ap=hnT[:], constant=0.0)
                        stage_A(path, l, hnT)
                        if stop == "A":
                            dump("hnT", hnT[:, :, 0:DBG["hnT"].shape[2]] if "hnT" in DBG else None)
                            raise Stop()
                        for kind in ("ssd", "nat", "diff"):
                            if kind == "ssd":
                                ssd_mixer(path, l, hnT)
                            else:
                                attn_mixer(path, l, hnT, kind)
                            if stop == kind:
                                dump_mix(path)
                                raise Stop()
                        rwkv_proj(path, l, hnT, rws)
                    rwkv_scan(path, l, rws)
                if stop == "mix":
                    dump_mix(path)
                    raise Stop()
                with p.scope() as scl2:
                    hnT2 = scl2.sb("hnT2", hshape, BF16)
                    stage_CD(path, l, hnT2)
                if stop == "L0":
                    raise Stop()

        try:
            for path in (PP, PS_):
                if path.name in paths:
                    run_path(path)
        except Stop:
            pass
        stats = p.finalize()
        print("build stats:", stats)
    return nc


def _prep_inputs(inputs):
    f = lambda a: np.ascontiguousarray(np.asarray(a, dtype=np.float32))
    consts = host_consts()
    w = {}
    for k in ("w_mod", "b_mod", "norm1_g", "norm2_g", "w_in", "w_out", "ssd_conv_w", "ssd_conv_b", "ssd_d", "ssd_norm_g",
              "nat_q_g", "nat_k_g", "rwkv_conv_w", "rwkv_w0", "rwkv_a0", "rwkv_g_up", "rwkv_k_k", "rwkv_k_a", "rwkv_ln_g",
              "rwkv_ln_b", "diff_q_g", "diff_k_g", "diff_subln_g", "w_ff1", "w_ff2"):
        w[k] = f(inputs[k])
    w["ssd_a_log"] = f(inputs["ssd_a_log"]).reshape(DEPTH, 8)
    w["ssd_dt_bias"] = f(inputs["ssd_dt_bias"]).reshape(DEPTH, 8)
    w["rwkv_w_up"] = f(inputs["rwkv_w_up"]).reshape(DEPTH, 128, 256)
    w["rwkv_a_up"] = f(inputs["rwkv_a_up"]).reshape(DEPTH, 128, 256)
    w["rwkv_r_k"] = f(inputs["rwkv_r_k"]).reshape(DEPTH, 256)
    w["diff_lam"] = f(inputs["diff_lam"]).reshape(DEPTH, 128)
    rb = f(inputs["nat_rel_bias"])
    bm = np.empty((DEPTH, H, 128, NPAT, 128), np.float32)
    for pi, (dr, dc, ok) in enumerate(NAT_IDX):
        g = rb[:, :, dr, dc]
        bm[:, :, :, pi, :] = np.where(ok[None, None], g, np.float32(NEGBIG))
    w["nat_bm"] = bm
    for k, shp in WEIGHT_SHAPES.items():
        assert w[k].shape == tuple(shp), (k, w[k].shape, shp)
    xp = f(inputs["x_prompt"])
    xs = f(inputs["x_sample"])
    cvec = f(inputs["c"])
    cctx = f(inputs["c_ctx"])
    in_maps = []
    for core in range(8):
        b = core // 4
        m = dict(w)
        m.update(consts)
        m["xp"] = xp[2 * core:2 * core + 2].reshape(NSP * LP, D)
        m["xs"] = xs[b]
        m["st_ssd"] = f(inputs["state_ssd"])[b].transpose(0, 1, 4, 2, 3).reshape(DEPTH, 2, 64, 256)
        m["c_nk"] = f(inputs["cache_nat_k"])[b].reshape(DEPTH, LCTX, 256)
        m["c_nv"] = f(inputs["cache_nat_v"])[b].reshape(DEPTH, LCTX, 256)
        m["st_rw"] = f(inputs["state_rwkv"])[b].transpose(0, 1, 4, 2, 3).reshape(DEPTH, 2, 64, 256)
        m["c_dk"] = f(inputs["cache_diff_k"])[b].reshape(DEPTH, LCTX, 256)
        m["c_dv"] = f(inputs["cache_diff_v"])[b].reshape(DEPTH, LCTX, 256)
        cond = np.stack([cctx, cvec[b]], 0)
        m["cond"] = np.ascontiguousarray(cond.reshape(2, 8, 128).transpose(0, 2, 1))
        in_maps.append({k: np.ascontiguousarray(v) for k, v in m.items()})
    return in_maps


_NC_CACHE = {}


def kernel(**inputs):
    in_maps = _prep_inputs(inputs)
    if "nc" not in _NC_CACHE:
        _NC_CACHE["nc"] = build()
    res = run_bass_kernel_spmd(_NC_CACHE["nc"], in_maps, core_ids=list(range(8)))
    r = res.results
    B = 16
    y_prompt = np.concatenate([r[c]["yp"].reshape(NSP, LP, D) for c in range(8)], 0)
    y_sample = np.stack([np.concatenate([r[4 * b + j]["ys"] for j in range(4)], 0) for b in range(2)], 0)
    cat = lambda k: np.concatenate([r[c][k] for c in range(8)], 0)
    new_ssd = cat("o_ssd").reshape(B, DEPTH, 2, H, 64, 64)
    new_nk = cat("o_nk").reshape(B, DEPTH, LP, H, HD)
    new_nv = cat("o_nv").reshape(B, DEPTH, LP, H, HD)
    new_rw = cat("o_rw").reshape(B, DEPTH, 2, H, 64, 64)
    new_dk = cat("o_dk").reshape(B, DEPTH, LP, H, 2, 32)
    new_dv = cat("o_dv").reshape(B, DEPTH, LP, H, HD)
    return tuple(np.asarray(a, dtype=np.float32) for a in
                 (y_prompt, y_sample, new_ssd, new_nk, new_nv, new_rw, new_dk, new_dv))
```

```python
import math
import numpy as np
from contextlib import ExitStack
import concourse.bass as bass
import concourse.mybir as mybir
from concourse.bass_utils import run_bass_kernel_spmd

F32 = mybir.dt.float32
BF16 = mybir.dt.bfloat16
AF = mybir.ActivationFunctionType
ALU = mybir.AluOpType
AX = mybir.AxisListType

D = 1024
DEPTH = 2
H = 4
HD = 64
NIN = 3464
DFF = 4096
LP, NSP = 256, 2
LS = 2048
LCTX = 512
OFF = dict(z=0, xs=256, B=512, C=640, dt=768, nq=776, nk=1032, nv=1288, rr=1544, rk=1800, rv=2056,
           wdn=2312, adn=2440, gdn=2568, dq=2696, dk=2952, dv=3208)
NEGBIG = -30000.0
import os
SKIP = set()


class S:
    def __init__(self, ap, sub):
        self.ap, self.sub = ap, sub


class _EngProxy:
    def __init__(self, prog, name):
        self.prog, self.name = prog, name

    def __getattr__(self, meth):
        def call(**kw):
            return self.prog._record(self.name, meth, kw)
        return call


class _Scope:
    def __init__(self, prog):
        self.prog = prog
        self.es = ExitStack()
        self.names = set()

    def __enter__(self):
        self.es.__enter__()
        return self

    def sb(self, name, shape, dt=F32):
        self.prog.n_t += 1
        nm = f"{name}_{self.prog.n_t}"
        self.names.add(nm)
        return self.es.enter_context(self.prog.nc.sbuf_tensor(nm, list(shape), dt))

    def __exit__(self, *a):
        self.prog.barrier()
        for key in list(self.prog.dma_slot):
            if key[0] in self.names:
                sl = self.prog.dma_slot.pop(key)
                if sl not in self.prog.store_slots:
                    self.prog.slot_free.setdefault(self.prog.slot_eng[sl], []).append(sl)
        return self.es.__exit__(*a)


class Prog:
    WRITE_KW = ("out", "accum_out", "ap")
    ENGS = ("pe", "dve", "act", "pool", "sp")

    def __init__(self, nc, es):
        self.nc, self.es = nc, es
        self.E = {"pe": nc.tensor, "dve": nc.vector, "act": nc.scalar, "pool": nc.gpsimd, "sp": nc.sync}
        self.ops = []
        self.res = {}
        self.latest = {}
        self.pe, self.dve, self.act, self.pool = (_EngProxy(self, n) for n in ("pe", "dve", "act", "pool"))
        self.sems = {}
        self.n_t = 0
        self.banks = []
        self.bank_i = 0
        self.dma_slot = {}
        self.slot_free = {}
        self.slot_eng = {}
        self.n_slot = 0
        self.store_slots = set()

    def sb(self, name, shape, dt=F32):
        return self.es.enter_context(self.nc.sbuf_tensor(name, list(shape), dt))

    def scope(self):
        return _Scope(self)

    def make_banks(self, nrot=6):
        for i in range(8):
            self.banks.append(self.es.enter_context(self.nc.psum_tensor(f"pb{i}", [128, 512], F32)))
        self.nrot = nrot

    def bank(self):
        b = self.banks[self.bank_i % self.nrot]
        self.bank_i += 1
        d = self.res.get(b.name, {})
        e = d.get(None)
        if e is not None and e["w"] is not None and not e["r"]:
            raise RuntimeError(f"PSUM bank {b.name} reused before it was consumed")
        return b

    def _entries(self, name, sub):
        d = self.res.setdefault(name, {})
        if sub is None:
            return list(d.values())
        out = []
        if None in d:
            out.append(d[None])
        if sub in d:
            out.append(d[sub])
        return out

    def _key(self, a):
        if isinstance(a, S):
            ap, sub = a.ap, a.sub
        else:
            ap, sub = a, None
        if str(ap.space) == "DRAM":
            return ap, None
        return ap, (ap.name, sub)

    def _add_dep(self, deps, opid, eng):
        o = self.ops[opid]
        if eng == "pe" and o["eng"] == "pe" and o["dma"] is None:
            return
        sk = o["semkey"]
        if sk not in deps or deps[sk] < opid:
            deps[sk] = opid

    def _access(self, eng, reads, writes, opid, semkey):
        deps = {}
        for (name, sub) in reads:
            for e in self._entries(name, sub):
                if e["w"] is not None:
                    self._add_dep(deps, e["w"], eng)
                if name.startswith("pb"):
                    for rk, r in e["r"].items():
                        if rk != semkey:
                            self._add_dep(deps, r, eng)
        for (name, sub) in writes:
            for e in self._entries(name, sub):
                if e["w"] is not None:
                    self._add_dep(deps, e["w"], eng)
                for r in e["r"].values():
                    self._add_dep(deps, r, eng)
        for (name, sub) in reads:
            d = self.res.setdefault(name, {})
            e = d.setdefault(sub, {"w": None, "r": {}})
            e["r"][semkey] = opid
        for (name, sub) in writes:
            d = self.res.setdefault(name, {})
            if sub is None:
                d.clear()
            d[sub] = {"w": opid, "r": {}}
        return deps

    def _record(self, eng, meth, kw, dma=None):
        reads, writes, args = [], [], {}
        extra_r = kw.pop("_r", [])
        extra_w = kw.pop("_w", [])
        for k, v in kw.items():
            if isinstance(v, (S, bass.AP)):
                ap, key = self._key(v)
                args[k] = ap
                if key is not None:
                    (writes if k in self.WRITE_KW else reads).append(key)
            else:
                args[k] = v
        opid = len(self.ops)
        if dma is not None:
            sbkeys = writes if writes else reads
            dk = dma if dma is not True else sbkeys[0]
            if dk not in self.dma_slot:
                fl = self.slot_free.setdefault(eng, [])
                if fl:
                    self.dma_slot[dk] = fl.pop()
                else:
                    self.dma_slot[dk] = self.n_slot
                    self.slot_eng[self.n_slot] = eng
                    self.n_slot += 1
            assert self.slot_eng[self.dma_slot[dk]] == eng, ("DMA semaphore shared across queues", dk)
            semkey = ("dma", self.dma_slot[dk])
            if not writes:
                self.store_slots.add(self.dma_slot[dk])
        else:
            semkey = eng
        reads = reads + list(extra_r)
        writes = writes + list(extra_w)
        deps = self._access(eng, reads, writes, opid, semkey)
        self.ops.append(dict(eng=eng, meth=meth, args=args, deps=deps, dma=dma, semkey=semkey, id=opid))
        self.latest[semkey] = opid
        return opid

    def dma(self, out, in_, eng="sp", key=True, **kw):
        return self._record(eng, "dma_start", dict(out=out, in_=in_, **kw), dma=key)

    def barrier(self):
        snap = dict(self.latest)
        for eng in self.ENGS:
            deps = dict(snap)
            if eng == "pe":
                deps.pop("pe", None)
            self.ops.append(dict(eng=eng, meth=None, args={}, deps=deps, dma=None, semkey=eng, id=len(self.ops)))

    def finalize(self):
        nc = self.nc
        needs = set()
        for o in self.ops:
            for d in o["deps"].values():
                needs.add(d)
        cnt, val = {}, {}
        for o in self.ops:
            if o["meth"] is None:
                continue
            sk = o["semkey"]
            if o["dma"] is not None:
                cnt[sk] = cnt.get(sk, 0) + 16
                val[o["id"]] = cnt[sk]
            elif o["id"] in needs:
                cnt[sk] = cnt.get(sk, 0) + 1
                val[o["id"]] = cnt[sk]
        for i, sk in enumerate(cnt):
            self.sems[sk] = self.es.enter_context(nc.semaphore(f"sem{i}"))
        seen = {e: {} for e in self.E}
        n_wait = 0
        n_ins = 0
        for o in self.ops:
            eng = self.E[o["eng"]]
            for sk, d in o["deps"].items():
                v = val[d]
                if seen[o["eng"]].get(sk, 0) < v:
                    eng.wait_ge(self.sems[sk], v)
                    seen[o["eng"]][sk] = v
                    n_wait += 1
            if o["meth"] is None:
                continue
            ins = getattr(eng, o["meth"])(**o["args"])
            n_ins += 1
            if o["dma"] is not None:
                ins.then_inc(self.sems[o["semkey"]], 16)
            elif o["id"] in needs:
                ins.then_inc(self.sems[o["semkey"]], 1)
        for sk, c in cnt.items():
            if isinstance(sk, tuple) and sk[0] == "dma" and seen["sp"].get(sk, 0) < c:
                nc.sync.wait_ge(self.sems[sk], c)
        self.stats = dict(n_ins=n_ins, n_wait=n_wait, n_sems=len(self.sems))
        return self.stats


def nat_patterns():
    R, W, KR = 32, 64, 8
    rs = np.clip(np.arange(R) - KR // 2, 0, R - KR)
    cs = np.clip(np.arange(W) - 8, 0, W - 16)
    pats, pairs, idx = {}, [], []
    for i in range(16):
        q = np.arange(128) + 128 * i
        qr, qc = q // W, q % W
        lst = []
        for j in range(16):
            k = np.arange(128) + 128 * j
            kr, kc = k // W, k % W
            ok = ((kr[:, None] >= rs[qr][None, :]) & (kr[:, None] < rs[qr][None, :] + KR)
                  & (kc[:, None] >= cs[qc][None, :]) & (kc[:, None] < cs[qc][None, :] + 16))
            if not ok.any():
                continue
            dr = np.where(ok, kr[:, None] - qr[None, :] + 7, 0)
            dc = np.where(ok, np.clip(kc[:, None] - qc[None, :], -15, 15) + 15, 0)
            key = (dr.tobytes(), dc.tobytes(), ok.tobytes())
            if key not in pats:
                pats[key] = len(idx)
                idx.append((dr, dc, ok))
            lst.append((j, pats[key]))
        pairs.append(lst)
    return pairs, idx


NAT_PAIRS, NAT_IDX = nat_patterns()
NPAT = len(NAT_IDX)


def host_consts():
    c = {}
    ar = np.arange(128)
    trif = (ar[:, None] <= ar[None, :]).astype(np.float32)
    trib = (ar[:, None] >= ar[None, :]).astype(np.float32)
    c["c_ident"] = np.eye(128, dtype=np.float32)
    c["c_tri"] = np.stack([trif, trib], 1)
    c["c_ones"] = np.ones((128, 128), np.float32)
    neg = np.stack([np.where(trif > 0, 0.0, NEGBIG), np.where(trib > 0, 0.0, NEGBIG)], 1)
    c["c_neg4"] = np.repeat(neg[:, :, None, :], 4, axis=2).reshape(128, 2, 512).astype(np.float32)
    rwm = np.zeros((128, 2, 3, 128), np.float32)
    sel = np.zeros((128, 2, 2), np.float32)
    mk = np.zeros((128, 2, 512), np.float32)
    mkt = np.zeros((128, 2, 128), np.float32)
    for d in range(2):
        tri = trif if d == 0 else trib
        strict = tri - np.eye(128, dtype=np.float32)
        im = (ar <= 63).astype(np.float32) if d == 0 else (ar >= 64).astype(np.float32)
        rwm[:, d, 0, :] = tri - im[:, None]
        rwm[:, d, 1, :] = 1.0 - tri
        rwm[:, d, 2, :] = strict - im[:, None]
        sel[:, d, 0] = 1.0
        sel[:, d, 1] = im
        mk[:, d, 0:128] = -strict
        mk[:, d, 128:256] = tri
        mk[:, d, 256:384] = strict
        mk[:, d, 384:512] = tri
        mkt[:, d, :] = -strict.T
    c["c_rwm"], c["c_rwsel"], c["c_rwmk"], c["c_rwmkt"] = rwm, sel, mk, mkt
    bu = np.zeros((128, 5, 128), np.float32)
    bu[:, 0, :] = (ar[:, None] // 8 == ar[None, :] // 8)
    for li in range(4):
        b = 8 << li
        bu[:, li + 1, :] = ((ar[:, None] // (2 * b) == ar[None, :] // (2 * b)) & ((ar[:, None] // b) % 2 == 0) & ((ar[None, :] // b) % 2 == 1))
    c["c_blk"] = np.stack([bu, bu.transpose(2, 1, 0)], 1)
    c["c_cmask"] = np.stack([((ar % 64) < 32), ((ar % 64) >= 32)], 1).astype(np.float32)
    pos = np.arange(LS)
    inv = (1.0 / (10000.0 ** (np.arange(0, 16, 2, dtype=np.float32) / 16))).astype(np.float32)
    ang_r = (pos // 64).astype(np.float32)[:, None] * inv[None, :]
    ang_c = (pos % 64).astype(np.float32)[:, None] * inv[None, :]
    cos = np.concatenate([np.cos(ang_r), np.cos(ang_r), np.cos(ang_c), np.cos(ang_c)], 1).astype(np.float32)
    sin = np.stack([np.sin(ang_r), np.sin(ang_c)], 1).astype(np.float32)
    c["c_ropec"] = np.ascontiguousarray(cos.reshape(16, 128, 32).transpose(1, 0, 2))
    c["c_ropes"] = np.ascontiguousarray(sin.reshape(16, 128, 16).transpose(1, 0, 2))
    return c


CONST_SHAPES = {k: v.shape for k, v in host_consts().items()}

WEIGHT_SHAPES = dict(
    w_mod=(DEPTH, D, 6 * D), b_mod=(DEPTH, 6 * D), norm1_g=(DEPTH, D), norm2_g=(DEPTH, D),
    w_in=(DEPTH, D, NIN), w_out=(DEPTH, D, D), ssd_conv_w=(DEPTH, 5, 512), ssd_conv_b=(DEPTH, 512),
    ssd_a_log=(DEPTH, 8), ssd_dt_bias=(DEPTH, 8), ssd_d=(DEPTH, 4), ssd_norm_g=(DEPTH, 256),
    nat_q_g=(DEPTH, 64), nat_k_g=(DEPTH, 64), nat_bm=(DEPTH, H, 128, NPAT, 128),
    rwkv_conv_w=(DEPTH, 3, 1152), rwkv_w0=(DEPTH, 2, 256), rwkv_w_up=(DEPTH, 128, 256),
    rwkv_a0=(DEPTH, 2, 256), rwkv_a_up=(DEPTH, 128, 256), rwkv_g_up=(DEPTH, 128, 256),
    rwkv_k_k=(DEPTH, 256), rwkv_k_a=(DEPTH, 256), rwkv_r_k=(DEPTH, 256), rwkv_ln_g=(DEPTH, 256),
    rwkv_ln_b=(DEPTH, 256), diff_q_g=(DEPTH, 32), diff_k_g=(DEPTH, 32), diff_lam=(DEPTH, 128),
    diff_subln_g=(DEPTH, 64), w_ff1=(DEPTH, D, DFF), w_ff2=(DEPTH, DFF, D),
)
ACT_SHAPES = dict(xp=(NSP * LP, D), xs=(LS, D), st_ssd=(DEPTH, 2, 64, 256), c_nk=(DEPTH, LCTX, 256),
                  c_nv=(DEPTH, LCTX, 256), st_rw=(DEPTH, 2, 64, 256), c_dk=(DEPTH, LCTX, 256),
                  c_dv=(DEPTH, LCTX, 256), cond=(2, 128, 8))
OUT_SHAPES = dict(yp=(NSP * LP, D), ys=(512, D), o_ssd=(NSP, DEPTH, 2, H, 64, 64), o_nk=(NSP, DEPTH, LP, 256),
                  o_nv=(NSP, DEPTH, LP, 256), o_rw=(NSP, DEPTH, 2, H, 64, 64), o_dk=(NSP, DEPTH, LP, 256),
                  o_dv=(NSP, DEPTH, LP, 256))


class Path:
    def __init__(self, name, ns, L, is_ctx, ci, xin, xscr, yout):
        self.name, self.ns, self.L, self.is_ctx, self.ci = name, ns, L, is_ctx, ci
        self.T = ns * L
        self.NT = self.T // 128
        self.nj = L // 128
        self.tiles = [(s, j) for s in range(ns) for j in range(self.nj)]
        self.xin, self.xscr, self.yout = xin, xscr, yout

    def tidx(self, s, j):
        return s * self.nj + j


def V(ap, a):
    return ap.rearrange("p (a b) -> p a b", a=a)


def BC(ap, shape):
    return ap.to_broadcast(list(shape))


def build(stop=None, dbg=None, paths="PS", layers=DEPTH, substop=None, mixers=("ssd", "nat", "diff", "rwkv")):
    nc = bass.Bass("TRN2", target_bir_lowering=False)
    A = {}
    for k, shp in {**ACT_SHAPES, **WEIGHT_SHAPES, **CONST_SHAPES}.items():
        A[k] = nc.dram_tensor(k, list(shp), F32, kind="ExternalInput").ap()
    O = {k: nc.dram_tensor(k, list(shp), F32, kind="ExternalOutput").ap() for k, shp in OUT_SHAPES.items()}
    DBG = {}
    if dbg:
        for k, shp in dbg.items():
            DBG[k] = nc.dram_tensor("dbg_" + k, list(shp), F32, kind="ExternalOutput").ap()
    xscrP = nc.dram_tensor("xscrP", [NSP * LP, D], F32, kind="Internal").ap()
    xscrS = nc.dram_tensor("xscrS", [LS, D], F32, kind="Internal").ap()
    mixscr = {"P": nc.dram_tensor("mixP", [NSP * LP, D], BF16, kind="Internal").ap(),
              "S": nc.dram_tensor("mixS", [LS, D], BF16, kind="Internal").ap()}
    qscr = nc.dram_tensor("qscr", [LS, 256], BF16, kind="Internal").ap()
    mixD = nc.dram_tensor("mixD", [LS, 1024], F32, kind="Internal").ap()
    PP = Path("P", NSP, LP, True, 0, A["xp"], xscrP, O["yp"])
    PS_ = Path("S", 1, LS, False, 1, A["xs"], xscrS, O["ys"])

    class Stop(Exception):
        pass

    es = ExitStack()
    if True:
        p = Prog(nc, es)
        p.make_banks(8)
        acc0, acc1 = p.banks[6], p.banks[7]
        ident = p.sb("ident", [128, 128], BF16)
        identf = p.sb("identf", [128, 128])
        tri = p.sb("tri", [128, 2, 128])
        onesf = p.sb("onesf", [128, 128])
        neg4 = p.sb("neg4", [128, 2, 512])
        rwm = p.sb("rwm", [128, 2, 3, 128])
        rwsel = p.sb("rwsel", [128, 2, 2])
        rwmk = p.sb("rwmk", [128, 2, 512])
        rwmkt = p.sb("rwmkt", [128, 2, 128])
        ropec = p.sb("ropec", [128, 16, 32])
        ropes = p.sb("ropes", [128, 16, 16])
        cmask = p.sb("cmask", [128, 2])
        blk = p.sb("blk", [128, 2, 5, 128], BF16)
        wring = [p.sb(f"wr{i}", [128, 2048], BF16) for i in range(6)]
        CB = p.sb("CB", [128, 8, 128], BF16)
        cnd = p.sb("cnd", [128, 8])
        cnds = p.sb("cnds", [128, 8])
        p.dma(out=ident[:], in_=A["c_ident"], eng="pool")
        p.dma(out=identf[:], in_=A["c_ident"])
        p.dma(out=tri[:], in_=A["c_tri"])
        p.dma(out=onesf[:], in_=A["c_ones"])
        p.dma(out=neg4[:], in_=A["c_neg4"])
        p.dma(out=rwm[:], in_=A["c_rwm"])
        p.dma(out=rwsel[:], in_=A["c_rwsel"])
        p.dma(out=rwmk[:], in_=A["c_rwmk"])
        p.dma(out=rwmkt[:], in_=A["c_rwmkt"])
        p.dma(out=ropec[:], in_=A["c_ropec"])
        p.dma(out=ropes[:], in_=A["c_ropes"])
        p.dma(out=cmask[:], in_=A["c_cmask"])
        p.dma(out=blk[:], in_=A["c_blk"], eng="pool")

        wi = [0]

        def wslot():
            t = wring[wi[0] % len(wring)]
            wi[0] += 1
            return t

        def wview(t, n, k=8):
            return t[:, 0:k * n].rearrange("p (k n) -> p k n", k=k)

        def load_w(dap, n):
            v = wview(wslot(), n)
            p.dma(out=v, in_=dap.rearrange("(k p) n -> p k n", p=128), eng="pool")
            return v

        def load_taps(cwb, l, c0, n, K, cwname, cw0):
            raw = load_w(A["w_in"][l, :, c0:c0 + n], n)
            if K == 1:
                return [(raw, 0)]
            p.dma(out=cwb[:, 0:K, 0:n], in_=A[cwname][l, :, cw0:cw0 + n].partition_broadcast(128))
            res = []
            for k in range(K):
                tv = wview(wslot(), n)
                p.pool.tensor_tensor(out=tv, in0=raw, in1=BC(cwb[:, k, 0:n].unsqueeze(1), [128, 8, n]), op=ALU.mult)
                res.append((tv, k - K // 2))
            return res

        def bcast_load(dst_ap, src_row):
            p.dma(out=dst_ap, in_=src_row.partition_broadcast(128))

        def dump_mix(path):
            if "mix" in DBG:
                p.dma(out=DBG["mix"], in_=mixscr[path.name], eng="pool", key=("st_dbg", 0),
                      _r=[("mix" + path.name, t) for t in range(path.NT)])

        def dump(name, ap):
            if name in DBG:
                p.dma(out=DBG[name], in_=ap, eng="pool")

        def hview(hnT, path):
            return hnT[:, :, 0:path.ns * (path.L + 4)].rearrange("p k (s l) -> p k s l", s=path.ns)

        def group_rms(sc, src, G, gs, eps, gain, out_ap, tag, l2=False):
            n = G * gs
            sq = sc.sb("rms_sq" + tag, [128, n])
            ss = sc.sb("rms_ss" + tag, [128, G])
            ms = sc.sb("rms_ms" + tag, [128, G])
            sd = sc.sb("rms_sd" + tag, [128, G])
            rs = sc.sb("rms_rs" + tag, [128, G])
            tm = sc.sb("rms_tm" + tag, [128, n])

            def run(src, gain, out_ap):
                p.act.activation(out=sq[:], in_=src, func=AF.Square)
                p.dve.tensor_reduce(out=ss[:], in_=V(sq[:], G), axis=AX.X, op=ALU.add)
                if l2:
                    p.act.activation(out=sd[:], in_=ss[:], func=AF.Sqrt)
                    p.dve.tensor_single_scalar(out=ms[:], in_=sd[:], scalar=eps, op=ALU.max)
                    p.dve.reciprocal(out=rs[:], in_=ms[:])
                else:
                    p.dve.tensor_scalar(out=ms[:], in0=ss[:], scalar1=1.0 / gs, scalar2=eps, op0=ALU.mult, op1=ALU.add)
                    p.act.activation(out=sd[:], in_=ms[:], func=AF.Sqrt)
                    p.dve.reciprocal(out=rs[:], in_=sd[:])
                p.dve.tensor_tensor(out=V(tm[:], G), in0=V(src, G), in1=BC(rs[:].unsqueeze(2), [128, G, gs]), op=ALU.mult)
                if gain is None:
                    p.pool.tensor_copy(out=out_ap, in_=tm[:])
                else:
                    p.pool.tensor_tensor(out=out_ap, in0=tm[:], in1=gain, op=ALU.mult)
            return run

        mixc = [0]

        JOWN = [None]

        def own_rows(t):
            if JOWN[0] is None:
                JOWN[0] = nc.sync.partition_id() % 4
            return bass.ds(JOWN[0] * 512 + t * 128, 128)

        def store_mix(path, src_bf, t, c0, own=False):
            mixc[0] += 1
            if own:
                p.dma(out=mixD[own_rows(t), :], in_=src_bf, key=("st_m", mixc[0] % 2), _w=[("mixD", t)])
                return
            p.dma(out=mixscr[path.name][t * 128:(t + 1) * 128, c0 * 128:(c0 + 2) * 128], in_=src_bf,
                  key=("st_m", mixc[0] % 2), _w=[("mix" + path.name, t)])

        def cond_setup(path):
            p.dma(out=cnd[:], in_=A["cond"][path.ci])
            p.act.activation(out=cnds[:], in_=cnd[:], func=AF.Silu)
            p.dve.tensor_copy(out=CB[:], in_=BC(cnds[:].unsqueeze(2), [128, 8, 128]))

        def mod_block(l, blk, dst):
            bcast_load(dst[:], A["b_mod"][l:l + 1, blk * 1024:(blk + 1) * 1024])
            for q in range(4):
                c0 = blk * 1024 + q * 256
                w = load_w(A["w_mod"][l, :, c0:c0 + 256], 256)
                bk = p.bank()
                for kc in range(8):
                    p.pe.matmul(out=bk[:, 0:256], lhsT=CB[:, kc, :], rhs=w[:, kc, :], start=(kc == 0), stop=(kc == 7))
                p.dve.tensor_tensor(out=dst[:, q * 256:(q + 1) * 256], in0=bk[:, 0:256],
                                    in1=dst[:, q * 256:(q + 1) * 256], op=ALU.add)

        def make_gain(l, gname, sct, tmp):
            bcast_load(tmp[:], A[gname][l:l + 1, :])
            p.dve.scalar_tensor_tensor(out=sct[:], in0=sct[:], scalar=1.0, in1=tmp[:], op0=ALU.add, op1=ALU.mult)

        def norm_tile(sc_t, xtile, G, SH, hnT, path, s, j):
            tmpB, hnb, ssq, ms, sd, rstd = sc_t
            p.act.activation(out=tmpB[:], in_=xtile, func=AF.Square, accum_out=ssq[:])
            p.dve.tensor_scalar(out=ms[:], in0=ssq[:], scalar1=1.0 / D, scalar2=1e-6, op0=ALU.mult, op1=ALU.add)
            p.act.activation(out=sd[:], in_=ms[:], func=AF.Sqrt)
            p.dve.reciprocal(out=rstd[:], in_=sd[:])
            p.dve.scalar_tensor_tensor(out=tmpB[:], in0=xtile, scalar=rstd[:, 0:1], in1=G[:], op0=ALU.mult, op1=ALU.mult)
            p.pool.tensor_tensor(out=hnb[:], in0=tmpB[:], in1=SH[:], op=ALU.add)
            bk = p.bank().bitcast(BF16)
            for kc in range(8):
                p.pe.transpose(out=bk[:, kc * 128:(kc + 1) * 128], in_=hnb[:, kc * 128:(kc + 1) * 128], identity=ident[:])
            p.act.copy(out=hview(hnT, path)[:, :, s, 2 + 128 * j:2 + 128 * (j + 1)], in_=V(bk[:, 0:1024], 8))

        def norm_scratch(sc):
            return (sc.sb("tmpB", [128, 1024]), sc.sb("hnb", [128, 1024], BF16), sc.sb("ssq", [128, 1]),
                    sc.sb("nms", [128, 1]), sc.sb("nsd", [128, 1]), sc.sb("nrstd", [128, 1]))

        def stage_A(path, l, hnT):
            with p.scope() as sc:
                G1 = sc.sb("G1", [128, 1024])
                SH1 = sc.sb("SH1", [128, 1024])
                tmpA = sc.sb("tmpA", [128, 1024])
                xt = [sc.sb(f"xt{i}", [128, 1024]) for i in range(2)]
                nsc = norm_scratch(sc)
                mod_block(l, 0, SH1)
                mod_block(l, 1, G1)
                make_gain(l, "norm1_g", G1, tmpA)
                xsrc = path.xin if l == 0 else path.xscr
                for (s, j) in path.tiles:
                    t = path.tidx(s, j)
                    xb = xt[t % 2]
                    rk = [] if l == 0 else [("xscr" + path.name, t)]
                    p.dma(out=xb[:], in_=xsrc[t * 128:(t + 1) * 128, :], _r=rk)
                    norm_tile(nsc, xb[:], G1, SH1, hnT, path, s, j)

        def stage_CD(path, l, hnT):
            last = (l == DEPTH - 1)
            own = last and path.name == "S" and l > 0
            NTall = path.NT
            if own:
                rows = own_rows
                tiles = [(0, ti) for ti in range(4)]
                rkeys = lambda name, t: [(name, tt) for tt in range(NTall)]
                okey = "xown"
            else:
                rows = lambda t: slice(t * 128, (t + 1) * 128)
                tiles = path.tiles
                rkeys = lambda name, t: [(name, t)]
                okey = "xscr" + path.name
            with p.scope() as sc:
                g1 = sc.sb("g1", [128, 1024])
                G2 = sc.sb("G2", [128, 1024])
                SH2 = sc.sb("SH2", [128, 1024])
                tmpA = sc.sb("tmpA", [128, 1024])
                xt = [sc.sb(f"xt{i}", [128, 1024]) for i in range(2)]
                nsc = norm_scratch(sc)
                mxb = [sc.sb(f"mxb{i}", [128, 1024], BF16) for i in range(2)]
                mxT = sc.sb("mxT", [128, 8, 128], BF16)
                mod_block(l, 2, g1)
                mod_block(l, 3, SH2)
                mod_block(l, 4, G2)
                make_gain(l, "norm2_g", G2, tmpA)
                wo = [load_w(A["w_out"][l, :, q * 256:(q + 1) * 256], 256) for q in range(4)]
                xsrc = path.xin if l == 0 else path.xscr
                for (s, j) in tiles:
                    t = path.tidx(s, j)
                    xb = xt[t % 2]
                    rk = [] if l == 0 else rkeys("xscr" + path.name, t)
                    p.dma(out=xb[:], in_=xsrc[rows(t), :], _r=rk)
                    mb = mxb[t % 2]
                    p.dma(out=mb[:], in_=mixscr[path.name][rows(t), :], _r=rkeys("mix" + path.name, t))
                    if own:
                        p.dma(out=tmpA[:], in_=mixD[rows(t), :], _r=[("mixD", t)])
                        p.dve.tensor_copy(out=mb[:, 768:1024], in_=tmpA[:, 0:256])
                    bkm = p.bank().bitcast(BF16)
                    for kc in range(8):
                        p.pe.transpose(out=bkm[:, kc * 128:(kc + 1) * 128], in_=mb[:, kc * 128:(kc + 1) * 128], identity=ident[:])
                    p.act.copy(out=mxT[:], in_=V(bkm[:, 0:1024], 8))
                    bks = [p.bank(), p.bank()]
                    for q in range(4):
                        bk = bks[q // 2]
                        for kc in range(8):
                            p.pe.matmul(out=bk[:, (q % 2) * 256:(q % 2 + 1) * 256], lhsT=mxT[:, kc, :],
                                        rhs=wo[q][:, kc, :], start=(kc == 0), stop=(kc == 7))
                    for hf in range(2):
                        p.dve.tensor_tensor(out=tmpA[:, hf * 512:(hf + 1) * 512], in0=bks[hf][:, 0:512],
                                            in1=g1[:, hf * 512:(hf + 1) * 512], op=ALU.mult)
                    p.pool.tensor_tensor(out=xb[:], in0=xb[:], in1=tmpA[:], op=ALU.add)
                    p.dma(out=path.xscr[rows(t), :], in_=xb[:], key=("st_x", t % 2),
                          _w=([(okey, t)] + (rkeys("xscr" + path.name, t) if own else [])))
                    norm_tile(nsc, xb[:], G2, SH2, hnT, path, s, j)
                mod_block(l, 5, g1)
                hT = sc.sb("hT", [128, 32, 512], BF16)
                tr = [sc.sb(f"tr{i}", [128, 512]) for i in range(2)]
                xh = [sc.sb(f"xh{i}", [128, 512]) for i in range(2)]
                hv = hview(hnT, path)
                nblk = 1 if own else path.T // 512
                ydst = path.yout if last else path.xscr
                for b in range(nblk):
                    if path.ns == 2:
                        rhs_of = lambda kc: hv[:, kc, :, 2:2 + 256]
                        outv = lambda bk: bk[:, 0:512].rearrange("p (s l) -> p s l", s=2)
                    else:
                        rhs_of = lambda kc, b=b: hv[:, kc, 0, 2 + 512 * b:2 + 512 * (b + 1)]
                        outv = lambda bk: bk[:, 0:512]
                    for cp in range(16):
                        w = load_w(A["w_ff1"][l, :, cp * 256:(cp + 1) * 256], 256)
                        for c2 in range(2):
                            c = cp * 2 + c2
                            bk = p.bank()
                            for kc in range(8):
                                p.pe.matmul(out=outv(bk), lhsT=w[:, kc, c2 * 128:(c2 + 1) * 128], rhs=rhs_of(kc),
                                            start=(kc == 0), stop=(kc == 7))
                            trb = tr[c % 2]
                            p.act.activation(out=trb[:], in_=bk[:, 0:512], func=AF.Relu)
                            p.dve.tensor_tensor(out=hT[:, c, :], in0=trb[:], in1=trb[:], op=ALU.mult)
                    for hf in range(2):
                        bks = [p.bank() for _ in range(4)]
                        for cq in range(8):
                            wv = wview(wslot(), 512, k=4)
                            p.dma(out=wv, in_=A["w_ff2"][l, cq * 512:(cq + 1) * 512, hf * 512:(hf + 1) * 512]
                                  .rearrange("(c p) n -> p c n", p=128), eng="pool")
                            for cc in range(4):
                                c = cq * 4 + cc
                                for ti in range(4):
                                    p.pe.matmul(out=bks[ti][:, 0:512], lhsT=hT[:, c, ti * 128:(ti + 1) * 128], rhs=wv[:, cc, :],
                                                start=(c == 0), stop=(c == 31))
                        for ti in range(4):
                            t = b * 4 + ti
                            xb = xh[ti % 2]
                            p.dma(out=xb[:], in_=path.xscr[rows(t), hf * 512:(hf + 1) * 512], _r=[(okey, t)])
                            trb = tr[ti % 2]
                            p.dve.tensor_tensor(out=trb[:], in0=bks[ti][:, 0:512], in1=g1[:, hf * 512:(hf + 1) * 512], op=ALU.mult)
                            p.dve.tensor_tensor(out=xb[:], in0=xb[:], in1=trb[:], op=ALU.add)
                            p.dma(out=ydst[t * 128:(t + 1) * 128, hf * 512:(hf + 1) * 512], in_=xb[:], key=("st_xh", ti % 2),
                                  _w=[] if last else [("xscr" + path.name, t)])

        def attn_mixer(path, l, hnT, kind):
            nat = (kind == "nat")
            G, gs = (4, 64) if nat else (8, 32)
            ncomp = 1 if nat else 2
            scale = gs ** -0.5
            qoff = OFF["nq"] if nat else OFF["dq"]
            T, NT = path.T, path.NT
            nctx = 0 if path.is_ctx else LCTX // 128
            lam_init = 0.8 - 0.6 * math.exp(-0.3 * l)
            p.nrot = 6
            own = (not nat) and (not path.is_ctx) and l == DEPTH - 1 and l > 0
            TQ = 512 if own else T
            NTQ = TQ // 128
            with p.scope() as sc:
                QT = sc.sb("QT", [128, 2, TQ], BF16)
                KTs = [sc.sb(f"KT{c}", [128, 2, nctx * 128 + T], BF16) for c in range(ncomp)]
                V1 = sc.sb("V1", [128, nctx + NT, 4, 65], BF16)
                YO = sc.sb("YO", [128, NTQ, 256], BF16 if (nat or not path.is_ctx) else F32)
                gq = sc.sb("gq", [128, gs])
                gk = sc.sb("gk", [128, gs])
                QKG = sc.sb("QKG", [128, 512])
                vst = [sc.sb(f"vst{i}", [128, 256]) for i in range(2 if path.is_ctx else 1)] * 2
                qkn = [sc.sb(f"qkn{i}", [128, 512]) for i in range(2)]
                qkb = sc.sb("qkb", [128, 512], BF16)
                rms = group_rms(sc, None, 2 * G, gs, 1e-6, None, None, "a")
                p.pool.memset(ap=V1[:, :, :, 64:65], constant=1.0)
                bcast_load(gq[:], A["nat_q_g" if nat else "diff_q_g"][l:l + 1, :])
                bcast_load(gk[:], A["nat_k_g" if nat else "diff_k_g"][l:l + 1, :])
                p.dve.tensor_scalar(out=V(QKG[:, 0:256], G), in0=BC(gq[:].unsqueeze(1), [128, G, gs]), scalar1=scale,
                                    scalar2=None, op0=ALU.mult)
                p.dve.tensor_copy(out=V(QKG[:, 256:512], G), in_=BC(gk[:].unsqueeze(1), [128, G, gs]))
                if not nat:
                    lv = sc.sb("lv", [128, 128])
                    lp = sc.sb("lp", [128, 64])
                    ls = sc.sb("ls", [128, 2])
                    le = sc.sb("le", [128, 2])
                    lam = sc.sb("lam", [128, 1])
                    nlam = sc.sb("nlam", [128, 1])
                    rz = sc.sb("rz", [128, 2])
                    t0 = sc.sb("t0", [128, 64])
                    r1l = sc.sb("r1l", [128, 1])
                    gsub = sc.sb("gsub", [128, 64])
                    GS4 = sc.sb("GS4", [128, 256])
                    bcast_load(lv[:], A["diff_lam"][l:l + 1, :])
                    p.dve.tensor_tensor(out=lp[:, 0:32], in0=lv[:, 0:32], in1=lv[:, 32:64], op=ALU.mult)
                    p.dve.tensor_tensor(out=lp[:, 32:64], in0=lv[:, 64:96], in1=lv[:, 96:128], op=ALU.mult)
                    p.dve.tensor_reduce(out=ls[:], in_=V(lp[:], 2), axis=AX.X, op=ALU.add)
                    p.act.activation(out=le[:], in_=ls[:], func=AF.Exp)
                    p.dve.tensor_tensor(out=lam[:], in0=le[:, 0:1], in1=le[:, 1:2], op=ALU.subtract)
                    p.dve.tensor_scalar(out=nlam[:], in0=lam[:], scalar1=-1.0, scalar2=-lam_init, op0=ALU.mult, op1=ALU.add)
                    bcast_load(gsub[:], A["diff_subln_g"][l:l + 1, :])
                    p.dve.tensor_scalar(out=V(GS4[:], 4), in0=BC(gsub[:].unsqueeze(1), [128, 4, 64]), scalar1=1.0 - lam_init,
                                        scalar2=None, op0=ALU.mult)
                else:
                    rzn = sc.sb("rzn", [128, 1])
                def put_keys(src, k0):
                    if nat:
                        p.act.copy(out=KTs[0][:, :, k0:k0 + 128], in_=V(src, 2))
                    else:
                        for c in range(2):
                            p.act.activation(out=KTs[c][:, :, k0:k0 + 128], in_=V(src, 2), func=AF.Copy, scale=cmask[:, c:c + 1])

                if nctx:
                    ckn, cvn = ("c_nk", "c_nv") if nat else ("c_dk", "c_dv")
                    for c in range(nctx):
                        kb = qkn[c % 2]
                        p.dma(out=kb[:, 0:256], in_=A[ckn][l, c * 128:(c + 1) * 128, :])
                        p.dve.tensor_copy(out=qkb[:, 0:256], in_=kb[:, 0:256])
                        bk = p.bank().bitcast(BF16)
                        for hp in range(2):
                            p.pe.transpose(out=bk[:, hp * 128:(hp + 1) * 128], in_=qkb[:, hp * 128:(hp + 1) * 128], identity=ident[:])
                        put_keys(bk[:, 0:256], c * 128)
                        vb = vst[c % 2]
                        p.dma(out=vb[:], in_=A[cvn][l, c * 128:(c + 1) * 128, :])
                        p.pool.tensor_copy(out=V1[:, c, :, 0:64], in_=V(vb[:], 4))
                wq = load_w(A["w_in"][l, :, qoff:qoff + 256], 256)
                wk = load_w(A["w_in"][l, :, qoff + 256:qoff + 512], 256)
                wv = load_w(A["w_in"][l, :, qoff + 512:qoff + 768], 256)
                hv = hview(hnT, path)
                okn, ovn = ("o_nk", "o_nv") if nat else ("o_dk", "o_dv")
                if (not nat) and nctx:
                    rt1 = sc.sb("rt1", [128, 512])
                    rt2 = sc.sb("rt2", [128, 512])
                if substop == "setup":
                    raise Stop()
                for (s, j) in path.tiles:
                    t = path.tidx(s, j)
                    lhs = lambda kc: hv[:, kc, s, 2 + 128 * j:2 + 128 * (j + 1)]
                    bA, bB = p.bank(), p.bank()
                    for kc in range(8):
                        p.pe.matmul(out=bA[:, 0:256], lhsT=lhs(kc), rhs=wq[:, kc, :], start=(kc == 0), stop=(kc == 7))
                    for kc in range(8):
                        p.pe.matmul(out=bA[:, 256:512], lhsT=lhs(kc), rhs=wk[:, kc, :], start=(kc == 0), stop=(kc == 7))
                    for kc in range(8):
                        p.pe.matmul(out=bB[:, 0:256], lhsT=lhs(kc), rhs=wv[:, kc, :], start=(kc == 0), stop=(kc == 7))
                    vb = vst[t % 2]
                    p.act.copy(out=vb[:], in_=bB[:, 0:256])
                    if path.is_ctx and 'st1' not in SKIP:
                        p.dma(out=O[ovn][s, l, j * 128:(j + 1) * 128, :], in_=vb[:], key=("st_v", t % 2))
                    p.pool.tensor_copy(out=V1[:, nctx + t, :, 0:64], in_=V(vb[:], 4))
                    qn = qkn[t % 2]
                    if 'rms' in SKIP:
                        p.dve.tensor_copy(out=qn[:], in_=bA[:, 0:512])
                    else:
                        rms(bA[:, 0:512], QKG[:], qn[:])
                    if path.is_ctx and 'st2' not in SKIP:
                        p.dma(out=O[okn][s, l, j * 128:(j + 1) * 128, :], in_=qn[:, 256:512], key=("st_k", t % 2))
                    if (not nat) and nctx:
                        u = qn[:]
                        p.dve.tensor_tensor(out=V(rt1[:], 16), in0=V(u, 16), in1=BC(ropec[:, j, :].unsqueeze(1), [128, 16, 32]), op=ALU.mult)
                        u5 = u.rearrange("p (g f b e) -> p g f b e", g=16, f=2, b=2)
                        t5 = rt2[:].rearrange("p (g f b e) -> p g f b e", g=16, f=2, b=2)
                        o5 = qkb[:].rearrange("p (g f b e) -> p g f b e", g=16, f=2, b=2)
                        a5 = rt1[:].rearrange("p (g f b e) -> p g f b e", g=16, f=2, b=2)
                        sn = BC(ropes[:, j, :].rearrange("p (f e) -> p f e", f=2).unsqueeze(1), [128, 16, 2, 8])
                        p.pool.tensor_tensor(out=t5[:, :, :, 0, :], in0=u5[:, :, :, 1, :], in1=sn, op=ALU.mult)
                        p.pool.tensor_tensor(out=t5[:, :, :, 1, :], in0=u5[:, :, :, 0, :], in1=sn, op=ALU.mult)
                        p.dve.tensor_tensor(out=o5[:, :, :, 0, :], in0=a5[:, :, :, 0, :], in1=t5[:, :, :, 0, :], op=ALU.subtract)
                        p.dve.tensor_tensor(out=o5[:, :, :, 1, :], in0=a5[:, :, :, 1, :], in1=t5[:, :, :, 1, :], op=ALU.add)
                    else:
                        p.dve.tensor_copy(out=qkb[:], in_=qn[:])
                    if 'tr' in SKIP:
                        continue
                    bk = p.bank().bitcast(BF16)
                    for g in (range(2, 4) if own else range(4)):
                        p.pe.transpose(out=bk[:, g * 128:(g + 1) * 128], in_=qkb[:, g * 128:(g + 1) * 128], identity=ident[:])
                    if own:
                        p.dma(out=qscr[t * 128:(t + 1) * 128, :], in_=qkb[:, 0:256], key=("st_q", 0), _w=[("qscr", t)])
                    else:
                        p.act.copy(out=QT[:, :, t * 128:(t + 1) * 128], in_=V(bk[:, 0:256], 2))
                    put_keys(bk[:, 256:512], (nctx + t) * 128)
                if own:
                    for ti in range(NTQ):
                        qo = qkn[ti % 2][:, 0:128].bitcast(BF16)
                        p.dma(out=qo, in_=qscr[own_rows(ti), :], _r=[("qscr", tt) for tt in range(NT)])
                        bk = p.bank().bitcast(BF16)
                        for g in range(2):
                            p.pe.transpose(out=bk[:, g * 128:(g + 1) * 128], in_=qo[:, g * 128:(g + 1) * 128], identity=ident[:])
                        p.act.copy(out=QT[:, :, ti * 128:(ti + 1) * 128], in_=V(bk[:, 0:256], 2))
                if substop == "proj":
                    raise Stop()
                if path.is_ctx:
                    blocks = [(s * path.L, path.L, [(s * path.nj + jj, None) for jj in range(path.nj)]) for s in range(path.ns)]
                elif nat:
                    blocks = [(i * 128, 128, [(c, None) for c in range(nctx)] + [(nctx + jj, pat) for (jj, pat) in NAT_PAIRS[i]])
                              for i in range(NT)]
                else:
                    blocks = [(b * 256, 256, [(c, None) for c in range(nctx + NT)]) for b in range(TQ // 256)]
                maxk = max(len(b[2]) for b in blocks)
                maxq = max(b[1] for b in blocks)
                with p.scope() as csc:
                    Eb = [csc.sb(f"Eb{i}", [128, maxk, maxq], BF16) for i in range(2)]
                    BM = csc.sb("BM", [128, NPAT, 128], BF16) if (nat and nctx) else None
                    items = [(h, q0, nq, keys, c) for h in range(4) for (q0, nq, keys) in blocks for c in range(ncomp)]
                    bm_head = [None]

                    def phase1(i):
                        h, q0, nq, keys, c = items[i]
                        if BM is not None and bm_head[0] != h:
                            p.dma(out=BM[:], in_=A["nat_bm"][l, h], eng="pool")
                            bm_head[0] = h
                        r0, r1 = 64 * (h % 2), 64 * (h % 2) + 64
                        KT = KTs[c]
                        E = Eb[i % 2]
                        npk = max(1, 512 // nq)
                        for ki0 in range(0, len(keys), npk):
                            grp = keys[ki0:ki0 + npk]
                            ng = len(grp)
                            bk = p.bank()
                            for gi, (kt, pat) in enumerate(grp):
                                ov = bk[:, gi * nq:(gi + 1) * nq]
                                p.pe.matmul(out=ov, lhsT=KT[r0:r1, h // 2, kt * 128:(kt + 1) * 128], rhs=QT[r0:r1, h // 2, q0:q0 + nq],
                                            start=True, stop=(pat is None))
                                if pat is not None:
                                    p.pe.matmul(out=ov, lhsT=ident[:], rhs=BM[:, pat, :], start=False, stop=True)
                            p.act.activation(out=E[:, ki0:ki0 + ng, 0:nq], in_=V(bk[:, 0:ng * nq], ng), func=AF.Exp)

                    def phase2(i):
                        h, q0, nq, keys, c = items[i]
                        E = Eb[i % 2]
                        ob = (acc0, acc1)[c]
                        for qi in range(nq // 128):
                            for ki, (kt, pat) in enumerate(keys):
                                p.pe.matmul(out=ob[:, qi * 65:(qi + 1) * 65], lhsT=E[:, ki, qi * 128:(qi + 1) * 128],
                                            rhs=V1[:, kt, h, :], start=(ki == 0), stop=(ki == len(keys) - 1))
                        if c != ncomp - 1:
                            return
                        for qi in range(nq // 128):
                            t = q0 // 128 + qi
                            o0 = acc0[:, qi * 65:qi * 65 + 64]
                            if nat:
                                p.dve.reciprocal(out=rzn[:], in_=acc0[:, qi * 65 + 64:qi * 65 + 65])
                                p.dve.tensor_scalar(out=YO[:, t, h * 64:(h + 1) * 64], in0=o0, scalar1=rzn[:, 0:1], scalar2=None,
                                                    op0=ALU.mult)
                            else:
                                o1 = acc1[:, qi * 65:qi * 65 + 64]
                                p.dve.reciprocal(out=rz[:, 0:1], in_=acc0[:, qi * 65 + 64:qi * 65 + 65])
                                p.dve.reciprocal(out=rz[:, 1:2], in_=acc1[:, qi * 65 + 64:qi * 65 + 65])
                                p.dve.tensor_tensor(out=r1l[:], in0=rz[:, 1:2], in1=nlam[:], op=ALU.mult)
                                p.dve.tensor_scalar(out=t0[:], in0=o0, scalar1=rz[:, 0:1], scalar2=None, op0=ALU.mult)
                                p.dve.scalar_tensor_tensor(out=YO[:, t, h * 64:(h + 1) * 64], in0=o1, scalar=r1l[:, 0:1], in1=t0[:],
                                                           op0=ALU.mult, op1=ALU.add)
                    phase1(0)
                    for i in range(len(items)):
                        if i + 1 < len(items):
                            phase1(i + 1)
                        phase2(i)
                if substop == "core":
                    raise Stop()
                c0 = 2 if nat else 6
                if nat:
                    for t in range(NT):
                        store_mix(path, YO[:, t, :], t, c0)
                else:
                    rms2 = group_rms(sc, None, 4, 64, 1e-6, None, None, "b")
                    if own:
                        yb = [sc.sb(f"ybw{i}", [128, 1024]) for i in range(2)]
                        for t in range(NTQ):
                            rms2(YO[:, t, :], GS4[:], yb[t % 2][:, 0:256])
                            store_mix(path, yb[t % 2][:], t, c0, own=True)
                    else:
                        yb = [sc.sb(f"yb{i}", [128, 256], BF16) for i in range(2)]
                        for t in range(NTQ):
                            rms2(YO[:, t, :], GS4[:], yb[t % 2][:])
                            store_mix(path, yb[t % 2][:], t, c0)
            p.nrot = 8

        def ssd_mixer(path, l, hnT):
            T, NT, nj = path.T, path.NT, path.nj
            hv = hview(hnT, path)
            with p.scope() as sc:
                ZS = sc.sb("ZS", [128, NT, 256], BF16)
                XS = sc.sb("XS", [128, NT, 256], BF16)
                BTM = sc.sb("BTM", [128, NT, 128], BF16)
                DT = sc.sb("DT", [128, NT, 8])
                LA = sc.sb("LA", [128, NT, 8])
                YACC = sc.sb("YACC", [128, NT, 256])
                BCT = sc.sb("BCT", [64, 4, T], BF16)
                Abc = sc.sb("Abc", [128, 8])
                dtb = sc.sb("dtb", [128, 8])
                Dbc = sc.sb("Dbc", [128, 4])
                cbx = sc.sb("cbx", [128, 384])
                cbf = sc.sb("cbf", [64, 4])
                ng = sc.sb("ng", [128, 256])
                bcast_load(Abc[:], A["ssd_a_log"][l:l + 1, :])
                p.act.activation(out=Abc[:], in_=Abc[:], func=AF.Exp)
                p.dve.tensor_scalar_mul(out=Abc[:], in0=Abc[:], scalar1=-1.0)
                bcast_load(dtb[:], A["ssd_dt_bias"][l:l + 1, :])
                bcast_load(Dbc[:], A["ssd_d"][l:l + 1, :])
                bcast_load(cbx[:], A["ssd_conv_b"][l:l + 1, 0:384])
                p.dma(out=cbf[:], in_=A["ssd_conv_b"][l, 256:512].rearrange("(c p) -> p c", p=64), allow_slow_non_contiguous=True)
                bcast_load(ng[:], A["ssd_norm_g"][l:l + 1, :])
                with p.scope() as psc:
                    cwb = psc.sb("cwb", [128, 5, 256])
                    tA = psc.sb("tA", [128, 384])
                    xsf = psc.sb("xsf", [128, 256])
                    dtr = psc.sb("dtr", [128, 8])
                    dte0 = psc.sb("dte0", [128, 8])
                    lhs = lambda s, j, kc, sh=0: hv[:, kc, s, 2 + 128 * j + sh:2 + 128 * (j + 1) + sh]
                    wz = load_w(A["w_in"][l, :, OFF["z"]:OFF["z"] + 256], 256)
                    wdt = load_w(A["w_in"][l, :, OFF["dt"]:OFF["dt"] + 8], 8)
                    for (s, j) in (path.tiles if 'j1' not in SKIP else []):
                        t = path.tidx(s, j)
                        bk = p.bank()
                        for kc in range(8):
                            p.pe.matmul(out=bk[:, 0:256], lhsT=lhs(s, j, kc), rhs=wz[:, kc, :], start=(kc == 0), stop=(kc == 7))
                        for kc in range(8):
                            p.pe.matmul(out=bk[:, 256:264], lhsT=lhs(s, j, kc), rhs=wdt[:, kc, :], start=(kc == 0), stop=(kc == 7))
                        p.act.activation(out=ZS[:, t, :], in_=bk[:, 0:256], func=(AF.Copy if 'e1' in SKIP else AF.Silu))
                        if 'nodt' in SKIP:
                            continue
                        p.dve.tensor_tensor(out=dtr[:], in0=bk[:, 256:264], in1=dtb[:], op=ALU.add)
                        p.act.activation(out=dte0[:], in_=dtr[:], func=AF.Exp)
                        p.dve.tensor_scalar_add(out=dtr[:], in0=dte0[:], scalar1=1.0)
                        p.act.activation(out=DT[:, t, :], in_=dtr[:], func=AF.Ln)
                        p.dve.tensor_tensor(out=LA[:, t, :], in0=DT[:, t, :], in1=Abc[:], op=ALU.mult)
                    for (c0, n, which) in (((OFF["xs"], 256, "x"), (OFF["B"], 128, "b")) if 'j2' not in SKIP else ()):
                        taps = load_taps(cwb, l, c0, n, 5, "ssd_conv_w", c0 - OFF["xs"])
                        for (s, j) in path.tiles:
                            t = path.tidx(s, j)
                            bk = p.bank()
                            i, ntot = 0, len(taps) * 8
                            for (wv_, sh) in taps:
                                for kc in range(8):
                                    p.pe.matmul(out=bk[:, 0:n], lhsT=lhs(s, j, kc, sh), rhs=wv_[:, kc, :], start=(i == 0), stop=(i == ntot - 1))
                                    i += 1
                            cb0 = c0 - OFF["xs"]
                            p.dve.tensor_tensor(out=tA[:, 0:n], in0=bk[:, 0:n], in1=cbx[:, cb0:cb0 + n], op=ALU.add)
                            if which == "x":
                                p.act.activation(out=xsf[:], in_=tA[:, 0:n], func=AF.Silu)
                                p.pool.tensor_copy(out=XS[:, t, :], in_=xsf[:])
                                p.dve.tensor_tensor(out=V(YACC[:, t, :], 4), in0=V(xsf[:], 4), in1=BC(Dbc[:].unsqueeze(2), [128, 4, 64]), op=ALU.mult)
                            else:
                                p.act.activation(out=BTM[:, t, :], in_=tA[:, 0:n], func=AF.Silu)
                    taps = load_taps(cwb, l, OFF["B"], 256, 5, "ssd_conv_w", OFF["B"] - OFF["xs"])
                    nb = min(512, path.L)
                    for s in (range(path.ns) if 'j3' not in SKIP else []):
                        for b0 in range(0, path.L, nb):
                            for pc in range(4):
                                bk = p.bank()
                                i, ntot = 0, len(taps) * 8
                                for (wv_, sh) in taps:
                                    for kc in range(8):
                                        p.pe.matmul(out=bk[0:64, 0:nb], lhsT=wv_[:, kc, pc * 64:(pc + 1) * 64],
                                                    rhs=hv[:, kc, s, 2 + b0 + sh:2 + b0 + nb + sh], start=(i == 0), stop=(i == ntot - 1))
                                        i += 1
                                tok0 = s * path.L + b0
                                p.act.activation(out=BCT[:, pc, tok0:tok0 + nb], in_=bk[0:64, 0:nb], func=AF.Silu, bias=cbf[:, pc:pc + 1])
                if substop == "ssdproj":
                    raise Stop()
                nsd = path.ns * 2
                ST = [sc.sb(f"ST{i}", [64, 256]) for i in range(nsd)]
                STb = [sc.sb(f"STb{i}", [64, 256], BF16) for i in range(nsd)]
                st0 = sc.sb("st0", [64, 4, 64])
                R1 = [sc.sb(f"R1_{i}", [128, 512]) for i in range(2)]
                R2 = [sc.sb(f"R2_{i}", [128, 512]) for i in range(2)]
                LM = [sc.sb(f"LM{i}", [128, 512], BF16) for i in range(2)]
                GM = [sc.sb(f"GM{i}", [128, 512], BF16) for i in range(2)]
                css = [sc.sb(f"css{i}", [128, 8]) for i in range(2)]
                ecs = [sc.sb(f"ecs{i}", [128, 4]) for i in range(2)]
                dec = [sc.sb(f"dec{i}", [128, 4]) for i in range(2)]
                dif = [sc.sb(f"dif{i}", [128, 4]) for i in range(2)]
                dte = [sc.sb(f"dte{i}", [128, 4]) for i in range(2)]
                ddt = [sc.sb(f"ddt{i}", [128, 4]) for i in range(2)]
                xdt = [sc.sb(f"xdt{i}", [128, 256], BF16) for i in range(2)]
                xdd = [sc.sb(f"xdd{i}", [128, 256], BF16) for i in range(2)]
                yo = [sc.sb(f"yo{i}", [128, 256]) for i in range(2)]
                for s in range(path.ns):
                    for d in range(2):
                        sd = s * 2 + d
                        if path.is_ctx:
                            p.dve.memset(ap=ST[sd][:], constant=0.0)
                            p.pool.memset(ap=STb[sd][:], constant=0.0)
                        else:
                            p.dma(out=ST[sd][:], in_=A["st_ssd"][l, d])
                            p.act.copy(out=STb[sd][:], in_=ST[sd][:])
                if substop == "ssdinit":
                    raise Stop()
                def schunk(d, s, step):
                    j = step if d == 0 else nj - 1 - step
                    t = path.tidx(s, j)
                    sd = s * 2 + d
                    q = d
                    tok = slice(t * 128, (t + 1) * 128)
                    la_d = LA[:, t, d * 4:(d + 1) * 4]
                    dt_d = DT[:, t, d * 4:(d + 1) * 4]
                    p.dve.tensor_tensor(out=V(R1[q][:], 4), in0=BC(tri[:, d, :].unsqueeze(1), [128, 4, 128]),
                                        in1=BC(la_d.unsqueeze(2), [128, 4, 128]), op=ALU.mult)
                    p.pool.tensor_scalar_mul(out=V(R2[q][:], 4), in0=BC(la_d.unsqueeze(2), [128, 4, 128]), scalar1=-1.0)
                    yield
                    bS = p.bank()
                    p.pe.matmul(out=bS[:, 0:512], lhsT=onesf[:], rhs=R1[q][:], start=True, stop=False)
                    p.pe.matmul(out=bS[:, 0:512], lhsT=tri[:, d, :], rhs=R2[q][:], start=False, stop=False)
                    p.pe.matmul(out=bS[:, 0:512], lhsT=identf[:], rhs=neg4[:, d, :], start=False, stop=True)
                    p.act.activation(out=LM[q][:], in_=bS[:, 0:512], func=AF.Exp)
                    bC = p.bank()
                    p.pe.matmul(out=bC[:, 0:4], lhsT=tri[:, d, :], rhs=la_d, start=True, stop=True)
                    p.pe.matmul(out=bC[:, 4:8], lhsT=onesf[:], rhs=la_d, start=True, stop=True)
                    yield
                    p.dve.tensor_copy(out=css[q][:], in_=bC[:, 0:8])
                    p.act.activation(out=ecs[q][:], in_=css[q][:, 0:4], func=AF.Exp)
                    p.act.activation(out=dec[q][:], in_=css[q][:, 4:8], func=AF.Exp)
                    yield
                    p.dve.tensor_tensor(out=dif[q][:], in0=css[q][:, 4:8], in1=css[q][:, 0:4], op=ALU.subtract)
                    p.act.activation(out=dte[q][:], in_=dif[q][:], func=AF.Exp)
                    yield
                    p.dve.tensor_tensor(out=ddt[q][:], in0=dt_d, in1=dte[q][:], op=ALU.mult)
                    p.dve.tensor_tensor(out=V(xdt[q][:], 4), in0=V(XS[:, t, :], 4), in1=BC(dt_d.unsqueeze(2), [128, 4, 64]), op=ALU.mult)
                    p.pool.tensor_tensor(out=V(xdd[q][:], 4), in0=V(XS[:, t, :], 4), in1=BC(ddt[q][:].unsqueeze(2), [128, 4, 64]), op=ALU.mult)
                    yield
                    bB = p.bank()
                    for g in range(2):
                        p.pe.matmul(out=bB[:, g * 128:(g + 1) * 128], lhsT=BCT[:, g, tok], rhs=BCT[:, 2 + g, tok], start=True, stop=True)
                    gm4 = GM[q][:].rearrange("p (g r l) -> p g r l", g=2, r=2)
                    lm4 = LM[q][:].rearrange("p (g r l) -> p g r l", g=2, r=2)
                    p.dve.tensor_tensor(out=gm4, in0=BC(V(bB[:, 0:256], 2).unsqueeze(2), [128, 2, 2, 128]), in1=lm4, op=ALU.mult)
                    yield
                    bY = p.bank()
                    for h in range(4):
                        p.pe.matmul(out=bY[:, h * 64:(h + 1) * 64], lhsT=GM[q][:, h * 128:(h + 1) * 128], rhs=xdt[q][:, h * 64:(h + 1) * 64],
                                    start=True, stop=True)
                    bO = p.bank()
                    for h in range(4):
                        p.pe.matmul(out=bO[:, h * 64:(h + 1) * 64], lhsT=BCT[:, 2 + h // 2, tok], rhs=STb[sd][:, h * 64:(h + 1) * 64],
                                    start=True, stop=True)
                    p.dve.tensor_tensor(out=V(yo[q][:], 4), in0=V(bO[:, 0:256], 4), in1=BC(ecs[q][:].unsqueeze(2), [128, 4, 64]), op=ALU.mult)
                    yield
                    p.dve.tensor_tensor(out=S(YACC[:, t, :], t), in0=bY[:, 0:256], in1=S(YACC[:, t, :], t), op=ALU.add)
                    p.pool.tensor_tensor(out=S(YACC[:, t, :], t), in0=S(YACC[:, t, :], t), in1=yo[q][:], op=ALU.add)
                    yield
                    bN = p.bank()
                    for g in range(2):
                        p.pe.matmul(out=bN[0:64, g * 128:(g + 1) * 128], lhsT=BTM[:, t, g * 64:(g + 1) * 64], rhs=xdd[q][:, g * 128:(g + 1) * 128],
                                    start=True, stop=True)
                    p.dve.tensor_tensor(out=V(ST[sd][:], 4), in0=V(ST[sd][:], 4), in1=BC(dec[q][0:64, :].unsqueeze(2), [64, 4, 64]), op=ALU.mult)
                    p.dve.tensor_tensor(out=ST[sd][:], in0=ST[sd][:], in1=bN[0:64, 0:256], op=ALU.add)
                    p.act.copy(out=STb[sd][:], in_=ST[sd][:])
                for step in range(nj):
                    for s in range(path.ns):
                        gens = [schunk(d, s, step) for d in range(2)]
                        while gens:
                            for g in list(gens):
                                try:
                                    next(g)
                                except StopIteration:
                                    gens.remove(g)
                if path.is_ctx:
                    for s in range(path.ns):
                        for d in range(2):
                            bk = p.bank()
                            for h in range(4):
                                p.pe.transpose(out=bk[0:64, h * 64:(h + 1) * 64], in_=ST[s * 2 + d][:, h * 64:(h + 1) * 64], identity=identf[0:64, 0:64])
                            p.dve.tensor_copy(out=st0[:], in_=V(bk[0:64, 0:256], 4))
                            p.dma(out=O["o_ssd"][s, l, d].rearrange("h p n -> p h n"), in_=st0[:], key=("st_s", 0))
                rmsy = group_rms(sc, None, 1, 256, 1e-6, None, None, "s")
                yg = sc.sb("yg", [128, 256])
                yb = [sc.sb(f"ybs{i}", [128, 256], BF16) for i in range(2)]
                for t in range(NT):
                    p.dve.tensor_tensor(out=yg[:], in0=YACC[:, t, :], in1=ZS[:, t, :], op=ALU.mult)
                    rmsy(yg[:], ng[:], yb[t % 2][:])
                    store_mix(path, yb[t % 2][:], t, 0)

        def rwkv_alloc(path, sc):
            NT, T = path.NT, path.T
            return dict(RB=sc.sb("RB", [128, NT, 256], BF16), KB=sc.sb("KB", [128, NT, 256], BF16),
                        VB=sc.sb("VB", [128, NT, 256], BF16), KKB=sc.sb("KKB", [128, NT, 256], BF16),
                        BS=sc.sb("BS", [128, NT, 4]), TW=sc.sb("TW", [128, T], BF16), AD=sc.sb("AD", [128, T], BF16),
                        SG=sc.sb("SG", [128, T], BF16))

        def rwkv_proj(path, l, hnT, W):
            RB, KB, VB, KKB, BS, TW, AD, SG = (W[k] for k in ("RB", "KB", "VB", "KKB", "BS", "TW", "AD", "SG"))
            hv = hview(hnT, path)
            lhs = lambda s, j, kc, sh=0: hv[:, kc, s, 2 + 128 * j + sh:2 + 128 * (j + 1) + sh]
            with p.scope() as psc:
                kkg = psc.sb("kkg", [128, 256])
                rkg = psc.sb("rkg", [128, 256])
                bcast_load(kkg[:], A["rwkv_k_k"][l:l + 1, :])
                bcast_load(rkg[:], A["rwkv_r_k"][l:l + 1, :])
                cwb = psc.sb("cwb", [128, 5, 256])
                kf = psc.sb("kf", [128, 256])
                t1 = psc.sb("t1", [128, 256])
                t2 = psc.sb("t2", [128, 256])
                l2n = group_rms(psc, None, 4, 64, 1e-12, None, None, "k", l2=True)
                for gi, c0 in enumerate((OFF["rr"], OFF["rk"], OFF["rv"])):
                    taps = load_taps(cwb, l, c0, 256, 3, "rwkv_conv_w", c0 - OFF["rr"])
                    for (s, j) in path.tiles:
                        t = path.tidx(s, j)
                        bk = p.bank()
                        i, ntot = 0, len(taps) * 8
                        for (wv_, sh) in taps:
                            for kc in range(8):
                                p.pe.matmul(out=bk[:, 0:256], lhsT=lhs(s, j, kc, sh), rhs=wv_[:, kc, :], start=(i == 0), stop=(i == ntot - 1))
                                i += 1
                        if gi == 0:
                            p.act.copy(out=RB[:, t, :], in_=bk[:, 0:256])
                        elif gi == 2:
                            p.act.copy(out=VB[:, t, :], in_=bk[:, 0:256])
                        else:
                            p.act.copy(out=kf[:], in_=bk[:, 0:256])
                            p.pool.tensor_copy(out=KB[:, t, :], in_=kf[:])
                            p.dve.tensor_tensor(out=t1[:], in0=kf[:], in1=kkg[:], op=ALU.mult)
                            l2n(t1[:], None, KKB[:, t, :])
                            p.dve.tensor_tensor(out=t2[:], in0=RB[:, t, :], in1=rkg[:], op=ALU.mult)
                            p.pool.tensor_tensor(out=t2[:], in0=t2[:], in1=kf[:], op=ALU.mult)
                            p.dve.tensor_reduce(out=BS[:, t, :], in_=V(t2[:], 4), axis=AX.X, op=ALU.add)
                nb = min(512, path.L)
                for gi, c0 in enumerate((OFF["wdn"], OFF["adn"], OFF["gdn"])):
                    taps = load_taps(cwb, l, c0, 128, 3, "rwkv_conv_w", c0 - OFF["rr"])
                    for s in range(path.ns):
                        for b0 in range(0, path.L, nb):
                            bk = p.bank()
                            i, ntot = 0, len(taps) * 8
                            for (wv_, sh) in taps:
                                for kc in range(8):
                                    p.pe.matmul(out=bk[:, 0:nb], lhsT=wv_[:, kc, :], rhs=hv[:, kc, s, 2 + b0 + sh:2 + b0 + nb + sh],
                                                start=(i == 0), stop=(i == ntot - 1))
                                    i += 1
                            tok0 = s * path.L + b0
                            if gi == 0:
                                p.act.activation(out=TW[:, tok0:tok0 + nb], in_=bk[:, 0:nb], func=AF.Tanh)
                            elif gi == 1:
                                p.dve.tensor_copy(out=AD[:, tok0:tok0 + nb], in_=bk[:, 0:nb])
                            else:
                                p.act.activation(out=SG[:, tok0:tok0 + nb], in_=bk[:, 0:nb], func=AF.Sigmoid)

        def rwkv_scan(path, l, W):
            RB, KB, VB, KKB, BS, TW, AD, SG = (W[k] for k in ("RB", "KB", "VB", "KKB", "BS", "TW", "AD", "SG"))
            T, NT, nj = path.T, path.NT, path.nj
            NEG_E = -math.exp(-0.5)
            with p.scope() as sc:
                YACC = sc.sb("YACCr", [128, NT, 256])
                kag = sc.sb("kag", [128, 256])
                lng = sc.sb("lng", [128, 256])
                lnb = sc.sb("lnb", [128, 256])
                w0b = sc.sb("w0b", [128, 2, 256])
                a0b = sc.sb("a0b", [128, 2, 256])
                wup = sc.sb("wup", [128, 256], BF16)
                aup = sc.sb("aup", [128, 256], BF16)
                gup = sc.sb("gup", [128, 256], BF16)
                for tl, nm in ((kag, "rwkv_k_a"), (lng, "rwkv_ln_g"), (lnb, "rwkv_ln_b")):
                    bcast_load(tl[:], A[nm][l:l + 1, :])
                bcast_load(w0b[:], A["rwkv_w0"][l])
                bcast_load(a0b[:], A["rwkv_a0"][l])
                p.dma(out=wup[:], in_=A["rwkv_w_up"][l], eng="pool")
                p.dma(out=aup[:], in_=A["rwkv_a_up"][l], eng="pool")
                p.dma(out=gup[:], in_=A["rwkv_g_up"][l], eng="pool")
                p.dve.memset(ap=YACC[:], constant=0.0)
                with p.scope() as ssc:
                    nsd = path.ns * 2
                    ST = [ssc.sb(f"RST{i}", [64, 256]) for i in range(nsd)]
                    st0 = ssc.sb("rst0", [64, 4, 64])

                    def tempset(i):
                        X = {}
                        for nm in ("LW", "af", "ab", "kt", "e1", "e1i", "e1x", "e2"):
                            X[nm] = ssc.sb(f"{nm}{i}", [128, 256])
                        X["xw"] = X["e1"]
                        X["Vf"] = X["e1x"]
                        X["Q4"] = ssc.sb(f"Q4{i}", [128, 4, 256], BF16)
                        X["Abe"] = ssc.sb(f"Abe{i}", [128, 256])
                        X["Kte"] = ssc.sb(f"Kte{i}", [128, 256])
                        X["ZSf"] = ssc.sb(f"ZSf{i}", [128, 256])
                        X["NUf"] = ssc.sb(f"NUf{i}", [128, 256])
                        X["CV"] = ssc.sb(f"CV{i}", [64, 8])
                        X["TT"] = ssc.sb(f"TT{i}", [64, 16, 128], BF16)
                        X["AM"] = ssc.sb(f"AM{i}", [128, 4, 512], BF16)
                        for nm in ("N0", "NT0", "R0", "R1", "RT0", "RT1", "B0", "B1", "B2", "B3"):
                            X[nm] = ssc.sb(f"{nm}{i}", [128, 4, 128])
                        X["STt"] = ssc.sb(f"STt{i}", [64, 4, 64], BF16)
                        X["NU"] = ssc.sb(f"NU{i}", [128, 256], BF16)
                        return X
                    TS = [tempset(0), tempset(1)]
                    for s in range(path.ns):
                        for d in range(2):
                            sd = s * 2 + d
                            if path.is_ctx:
                                p.dve.memset(ap=ST[sd][:], constant=0.0)
                            else:
                                p.dma(out=ST[sd][:], in_=A["st_rw"][l, d])

                    def mm4(lhs_, rhs_):
                        b_ = p.bank()
                        for h in range(4):
                            p.pe.matmul(out=b_[:, h * 128:(h + 1) * 128], lhsT=lhs_[:, h, :], rhs=rhs_[:, h, :], start=True, stop=True)
                        return V(b_[:, 0:512], 4)

                    def chunk(d, s, step, X):
                        j = step if d == 0 else nj - 1 - step
                        t = path.tidx(s, j)
                        sd = s * 2 + d
                        tok = slice(t * 128, (t + 1) * 128)
                        LW, xw, af, ab, kt, e1, e1i, e1x, e2 = (X[k] for k in ("LW", "xw", "af", "ab", "kt", "e1", "e1i", "e1x", "e2"))
                        Q4, Abe, Kte, CV, TT, AM, STt, NU = (X[k] for k in ("Q4", "Abe", "Kte", "CV", "TT", "AM", "STt", "NU"))
                        bW = p.bank()
                        p.pe.matmul(out=bW[:, 0:256], lhsT=TW[d * 64:(d + 1) * 64, tok], rhs=wup[d * 64:(d + 1) * 64, :], start=True, stop=True)
                        bA = p.bank()
                        p.pe.matmul(out=bA[:, 0:256], lhsT=AD[d * 64:(d + 1) * 64, tok], rhs=aup[d * 64:(d + 1) * 64, :], start=True, stop=True)
                        p.dve.tensor_tensor(out=xw[:], in0=bW[:, 0:256], in1=w0b[:, d, :], op=ALU.add)
                        p.dve.tensor_tensor(out=af[:], in0=bA[:, 0:256], in1=a0b[:, d, :], op=ALU.add)
                        yield
                        p.act.activation(out=LW[:], in_=xw[:], func=AF.Sigmoid)
                        p.act.activation(out=af[:], in_=af[:], func=AF.Sigmoid)
                        yield
                        p.dve.tensor_scalar_mul(out=LW[:], in0=LW[:], scalar1=NEG_E)
                        p.dve.tensor_tensor(out=ab[:], in0=af[:], in1=KKB[:, t, :], op=ALU.mult)
                        p.dve.scalar_tensor_tensor(out=kt[:], in0=af[:], scalar=-1.0, in1=kag[:], op0=ALU.add, op1=ALU.mult)
                        p.dve.scalar_tensor_tensor(out=kt[:], in0=kt[:], scalar=1.0, in1=KB[:, t, :], op0=ALU.add, op1=ALU.mult)
                        bX = [p.bank() for _ in range(3)]
                        for i in range(3):
                            p.pe.matmul(out=bX[i][:, 0:256], lhsT=rwm[:, d, i, :], rhs=LW[:], start=True, stop=True)
                        bV = p.bank()
                        for h in range(4):
                            p.pe.matmul(out=bV[0:64, h * 2:(h + 1) * 2], lhsT=LW[:, h * 64:(h + 1) * 64], rhs=rwsel[:, d, :], start=True, stop=True)
                        yield
                        p.act.activation(out=e1[:], in_=bX[0][:, 0:256], func=AF.Exp)
                        p.act.activation(out=e1i[:], in_=bX[0][:, 0:256], func=AF.Exp, scale=-1.0)
                        p.act.activation(out=e2[:], in_=bX[1][:, 0:256], func=AF.Exp)
                        p.act.activation(out=e1x[:], in_=bX[2][:, 0:256], func=AF.Exp)
                        p.act.activation(out=CV[:], in_=bV[0:64, 0:8], func=AF.Exp)
                        yield
                        p.dve.tensor_tensor(out=Q4[:, 0, :], in0=KKB[:, t, :], in1=e1x[:], op=ALU.mult)
                        p.pool.tensor_tensor(out=Q4[:, 1, :], in0=RB[:, t, :], in1=e1[:], op=ALU.mult)
                        p.dve.tensor_tensor(out=Q4[:, 2, :], in0=ab[:], in1=e1i[:], op=ALU.mult)
                        p.pool.tensor_tensor(out=Q4[:, 3, :], in0=kt[:], in1=e1i[:], op=ALU.mult)
                        p.dve.tensor_tensor(out=Abe[:], in0=ab[:], in1=e2[:], op=ALU.mult)
                        p.pool.tensor_tensor(out=Kte[:], in0=kt[:], in1=e2[:], op=ALU.mult)
                        yield
                        bT = [p.bank().bitcast(BF16) for _ in range(2)]
                        for h in range(4):
                            for a in range(4):
                                idx = h * 4 + a
                                p.pe.transpose(out=bT[idx // 8][0:64, (idx % 8) * 128:(idx % 8 + 1) * 128], in_=Q4[:, a, h * 64:(h + 1) * 64],
                                               identity=ident[:])
                        yield
                        p.act.copy(out=TT[:, 0:8, :], in_=V(bT[0][0:64, 0:1024], 8))
                        p.act.copy(out=TT[:, 8:16, :], in_=V(bT[1][0:64, 0:1024], 8))
                        yield
                        for hp in range(2):
                            bks_ = []
                            for h in (2 * hp, 2 * hp + 1):
                                bA4 = p.bank()
                                rq = TT[:, h * 4:h * 4 + 2, :].rearrange("p a t -> p (a t)")
                                p.pe.matmul(out=bA4[:, 0:256], lhsT=TT[:, h * 4 + 2, :], rhs=rq, start=True, stop=True)
                                p.pe.matmul(out=bA4[:, 256:512], lhsT=TT[:, h * 4 + 3, :], rhs=rq, start=True, stop=True)
                                bks_.append((h, bA4))
                            yield
                            for (h, bA4) in bks_:
                                p.dve.tensor_tensor(out=X["N0"][:, h, :], in0=bA4[:, 0:128], in1=rwmk[:, d, 0:128], op=ALU.mult)
                                p.dve.tensor_tensor(out=AM[:, h, 128:512], in0=bA4[:, 128:512], in1=rwmk[:, d, 128:512], op=ALU.mult)
                        bA1 = p.bank()
                        for h in range(4):
                            p.pe.matmul(out=bA1[:, h * 128:(h + 1) * 128], lhsT=TT[:, h * 4 + 0, :], rhs=TT[:, h * 4 + 2, :], start=True, stop=True)
                        yield
                        p.dve.tensor_tensor(out=X["NT0"][:], in0=V(bA1[:, 0:512], 4), in1=BC(rwmkt[:, d, :].unsqueeze(1), [128, 4, 128]), op=ALU.mult)
                        uN, uT = (0, 1) if d == 0 else (1, 0)
                        mB = lambda ul, li: BC(blk[:, ul, li, :].unsqueeze(1), [128, 4, 128])
                        idb = BC(identf[:].unsqueeze(1), [128, 4, 128])
                        N0, NT0 = X["N0"], X["NT0"]
                        Rm, RTm = [X["R0"], X["R1"]], [X["RT0"], X["RT1"]]
                        Pb, PTb = [X["B0"], X["B1"]], [X["B2"], X["B3"]]
                        p.dve.tensor_tensor(out=Pb[0][:], in0=N0[:], in1=mB(0, 0), op=ALU.mult)
                        p.pool.tensor_tensor(out=PTb[0][:], in0=NT0[:], in1=mB(0, 0), op=ALU.mult)
                        p.dve.tensor_tensor(out=Rm[0][:], in0=Pb[0][:], in1=idb, op=ALU.add)
                        p.pool.tensor_tensor(out=RTm[0][:], in0=PTb[0][:], in1=idb, op=ALU.add)
                        yield
                        cur = 0
                        for k in range(2):
                            nxt = 1 - cur
                            m1 = mm4(PTb[cur], Pb[cur])
                            m2 = mm4(Pb[cur], PTb[cur])
                            yield
                            p.dve.tensor_copy(out=Pb[nxt][:], in_=m1)
                            p.act.copy(out=PTb[nxt][:], in_=m2)
                            yield
                            m3 = mm4(PTb[nxt], Rm[cur])
                            m4 = mm4(Rm[cur], PTb[nxt])
                            yield
                            p.dve.tensor_tensor(out=Rm[nxt][:], in0=m3, in1=Rm[cur][:], op=ALU.add)
                            p.dve.tensor_tensor(out=RTm[nxt][:], in0=m4, in1=RTm[cur][:], op=ALU.add)
                            yield
                            cur = nxt
                        Xm = Pb[1]
                        for li in range(4):
                            nxt = 1 - cur
                            Dm, DTm = Rm[cur], RTm[cur]
                            m1 = mm4(NT0, Dm)
                            yield
                            p.dve.tensor_tensor(out=Xm[:], in0=m1, in1=mB(uN, li + 1), op=ALU.mult)
                            yield
                            m3 = mm4(DTm, Xm)
                            yield
                            p.dve.tensor_tensor(out=Rm[nxt][:], in0=m3, in1=Dm[:], op=ALU.add)
                            yield
                            if li < 3:
                                bt_ = p.bank()
                                for h in range(4):
                                    p.pe.transpose(out=bt_[:, h * 128:(h + 1) * 128], in_=Rm[nxt][:, h, :], identity=identf[:])
                                yield
                                p.act.copy(out=RTm[nxt][:], in_=V(bt_[:, 0:512], 4))
                            cur = nxt
                        RT = Rm[cur]
                        ZSf, NUf, Vf = X["ZSf"], X["NUf"], X["Vf"]
                        p.pool.tensor_copy(out=Vf[:], in_=VB[:, t, :])
                        cv3 = CV[:].rearrange("p (h two) -> p h two", two=2)
                        p.dve.tensor_tensor(out=STt[:], in0=V(ST[sd][:], 4), in1=BC(cv3[:, :, 1:2], [64, 4, 64]), op=ALU.mult)
                        yield
                        bZ = p.bank()
                        for h in range(4):
                            hs = slice(h * 64, (h + 1) * 64)
                            p.pe.matmul(out=bZ[:, hs], lhsT=TT[:, h * 4 + 0, :], rhs=STt[:, h, :], start=True, stop=False)
                            p.pe.matmul(out=bZ[:, hs], lhsT=AM[:, h, 256:384], rhs=VB[:, t, hs], start=False, stop=True)
                        yield
                        p.act.copy(out=ZSf[:], in_=bZ[:, 0:256])
                        yield
                        bU = p.bank()
                        for h in range(4):
                            hs = slice(h * 64, (h + 1) * 64)
                            p.pe.matmul(out=bU[:, hs], lhsT=RT[:, h, :], rhs=ZSf[:, hs], start=True, stop=True)
                        yield
                        p.act.mul(out=NU[:], in_=bU[:, 0:256], mul=-1.0)
                        p.dve.tensor_scalar_mul(out=NUf[:], in0=bU[:, 0:256], scalar1=-1.0)
                        yield
                        bYr = p.bank()
                        for h in range(4):
                            hs = slice(h * 64, (h + 1) * 64)
                            p.pe.matmul(out=bYr[:, hs], lhsT=TT[:, h * 4 + 1, :], rhs=STt[:, h, :], start=True, stop=False)
                            p.pe.matmul(out=bYr[:, hs], lhsT=AM[:, h, 128:256], rhs=NU[:, hs], start=False, stop=False)
                            p.pe.matmul(out=bYr[:, hs], lhsT=AM[:, h, 384:512], rhs=VB[:, t, hs], start=False, stop=True)
                        bSN = p.bank()
                        for h in range(4):
                            hs = slice(h * 64, (h + 1) * 64)
                            p.pe.matmul(out=bSN[0:64, hs], lhsT=Abe[:, hs], rhs=NUf[:, hs], start=True, stop=False)
                            p.pe.matmul(out=bSN[0:64, hs], lhsT=Kte[:, hs], rhs=Vf[:, hs], start=False, stop=True)
                        yield
                        p.dve.tensor_tensor(out=S(YACC[:, t, :], t), in0=bYr[:, 0:256], in1=S(YACC[:, t, :], t), op=ALU.add)
                        p.dve.tensor_tensor(out=V(ST[sd][:], 4), in0=V(ST[sd][:], 4), in1=BC(cv3[:, :, 0:1], [64, 4, 64]), op=ALU.mult)
                        p.dve.tensor_tensor(out=ST[sd][:], in0=ST[sd][:], in1=bSN[0:64, 0:256], op=ALU.add)

                    for step in range(nj):
                        for s in range(path.ns):
                            gens = [chunk(d, s, step, TS[d]) for d in range(2)]
                            while gens:
                                for g in list(gens):
                                    try:
                                        next(g)
                                    except StopIteration:
                                        gens.remove(g)
                    if path.is_ctx:
                        for s in range(path.ns):
                            for d in range(2):
                                bk = p.bank()
                                for h in range(4):
                                    p.pe.transpose(out=bk[0:64, h * 64:(h + 1) * 64], in_=ST[s * 2 + d][:, h * 64:(h + 1) * 64], identity=identf[0:64, 0:64])
                                p.dve.tensor_copy(out=st0[:], in_=V(bk[0:64, 0:256], 4))
                                p.dma(out=O["o_rw"][s, l, d].rearrange("h v k -> v h k"), in_=st0[:], key=("st_s", 0))
                mu = sc.sb("mu", [128, 4])
                cen = sc.sb("cen", [128, 256])
                rmsl = group_rms(sc, None, 4, 64, 64e-5, None, None, "l")
                yn = sc.sb("yn", [128, 256])
                bon = sc.sb("bon", [128, 256])
                yb = [sc.sb(f"ybr{i}", [128, 256], BF16) for i in range(2)]
                for t in range(NT):
                    tok = slice(t * 128, (t + 1) * 128)
                    p.dve.tensor_reduce(out=mu[:], in_=V(YACC[:, t, :], 4), axis=AX.X, op=ALU.add)
                    p.dve.tensor_scalar_mul(out=mu[:], in0=mu[:], scalar1=-1.0 / 64)
                    p.dve.tensor_tensor(out=V(cen[:], 4), in0=V(YACC[:, t, :], 4), in1=BC(mu[:].unsqueeze(2), [128, 4, 64]), op=ALU.add)
                    rmsl(cen[:], lng[:], yn[:])
                    p.dve.tensor_tensor(out=yn[:], in0=yn[:], in1=lnb[:], op=ALU.add)
                    p.pool.tensor_tensor(out=V(bon[:], 4), in0=V(VB[:, t, :], 4), in1=BC(BS[:, t, :].unsqueeze(2), [128, 4, 64]), op=ALU.mult)
                    p.dve.tensor_tensor(out=yn[:], in0=yn[:], in1=bon[:], op=ALU.add)
                    bG = p.bank()
                    p.pe.matmul(out=bG[:, 0:256], lhsT=SG[:, tok], rhs=gup[:], start=True, stop=True)
                    p.dve.tensor_tensor(out=yb[t % 2][:], in0=yn[:], in1=bG[:, 0:256], op=ALU.mult)
                    store_mix(path, yb[t % 2][:], t, 4)

        def run_path(path):
            cond_setup(path)
            hshape = [128, 8, LS + 4 if path.name == "S" else NSP * (LP + 4)]
            for l in range(layers):
                with p.scope() as scl0:
                    rws = rwkv_alloc(path, scl0)
                    with p.scope() as scl:
                        hnT = scl.sb("hnT", hshape, BF16)
                        p.pool.memset(ap=hnT[:], constant=0.0)
                        stage_A(path, l, hnT)
                        if stop == "A":
                            dump("hnT", hnT[:, :, 0:DBG["hnT"].shape[2]] if "hnT" in DBG else None)
                            raise Stop()
                        for kind in ("ssd", "nat", "diff"):
                            if kind == "ssd":
                                ssd_mixer(path, l, hnT)
                            else:
                                attn_mixer(path, l, hnT, kind)
                            if stop == kind:
                                dump_mix(path)
                                raise Stop()
                        rwkv_proj(path, l, hnT, rws)
                    rwkv_scan(path, l, rws)
                if stop == "mix":
                    dump_mix(path)
                    raise Stop()
                with p.scope() as scl2:
                    hnT2 = scl2.sb("hnT2", hshape, BF16)
                    stage_CD(path, l, hnT2)
                if stop == "L0":
                    raise Stop()

        try:
            for path in (PP, PS_):
                if path.name in paths:
                    run_path(path)
        except Stop:
            pass
        stats = p.finalize()
        print("build stats:", stats)
    return nc


def _prep_inputs(inputs):
    f = lambda a: np.ascontiguousarray(np.asarray(a, dtype=np.float32))
    consts = host_consts()
    w = {}
    for k in ("w_mod", "b_mod", "norm1_g", "norm2_g", "w_in", "w_out", "ssd_conv_w", "ssd_conv_b", "ssd_d", "ssd_norm_g",
              "nat_q_g", "nat_k_g", "rwkv_conv_w", "rwkv_w0", "rwkv_a0", "rwkv_g_up", "rwkv_k_k", "rwkv_k_a", "rwkv_ln_g",
              "rwkv_ln_b", "diff_q_g", "diff_k_g", "diff_subln_g", "w_ff1", "w_ff2"):
        w[k] = f(inputs[k])
    w["ssd_a_log"] = f(inputs["ssd_a_log"]).reshape(DEPTH, 8)
    w["ssd_dt_bias"] = f(inputs["ssd_dt_bias"]).reshape(DEPTH, 8)
    w["rwkv_w_up"] = f(inputs["rwkv_w_up"]).reshape(DEPTH, 128, 256)
    w["rwkv_a_up"] = f(inputs["rwkv_a_up"]).reshape(DEPTH, 128, 256)
    w["rwkv_r_k"] = f(inputs["rwkv_r_k"]).reshape(DEPTH, 256)
    w["diff_lam"] = f(inputs["diff_lam"]).reshape(DEPTH, 128)
    rb = f(inputs["nat_rel_bias"])
    bm = np.empty((DEPTH, H, 128, NPAT, 128), np.float32)
    for pi, (dr, dc, ok) in enumerate(NAT_IDX):
        g = rb[:, :, dr, dc]
        bm[:, :, :, pi, :] = np.where(ok[None, None], g, np.float32(NEGBIG))
    w["nat_bm"] = bm
    for k, shp in WEIGHT_SHAPES.items():
        assert w[k].shape == tuple(shp), (k, w[k].shape, shp)
    xp = f(inputs["x_prompt"])
    xs = f(inputs["x_sample"])
    cvec = f(inputs["c"])
    cctx = f(inputs["c_ctx"])
    in_maps = []
    for core in range(8):
        b = core // 4
        m = dict(w)
        m.update(consts)
        m["xp"] = xp[2 * core:2 * core + 2].reshape(NSP * LP, D)
        m["xs"] = xs[b]
        m["st_ssd"] = f(inputs["state_ssd"])[b].transpose(0, 1, 4, 2, 3).reshape(DEPTH, 2, 64, 256)
        m["c_nk"] = f(inputs["cache_nat_k"])[b].reshape(DEPTH, LCTX, 256)
        m["c_nv"] = f(inputs["cache_nat_v"])[b].reshape(DEPTH, LCTX, 256)
        m["st_rw"] = f(inputs["state_rwkv"])[b].transpose(0, 1, 4, 2, 3).reshape(DEPTH, 2, 64, 256)
        m["c_dk"] = f(inputs["cache_diff_k"])[b].reshape(DEPTH, LCTX, 256)
        m["c_dv"] = f(inputs["cache_diff_v"])[b].reshape(DEPTH, LCTX, 256)
        cond = np.stack([cctx, cvec[b]], 0)
        m["cond"] = np.ascontiguousarray(cond.reshape(2, 8, 128).transpose(0, 2, 1))
        in_maps.append({k: np.ascontiguousarray(v) for k, v in m.items()})
    return in_maps


_NC_CACHE = {}


def kernel(**inputs):
    in_maps = _prep_inputs(inputs)
    if "nc" not in _NC_CACHE:
        _NC_CACHE["nc"] = build()
    res = run_bass_kernel_spmd(_NC_CACHE["nc"], in_maps, core_ids=list(range(8)))
    r = res.results
    B = 16
    y_prompt = np.concatenate([r[c]["yp"].reshape(NSP, LP, D) for c in range(8)], 0)
    y_sample = np.stack([np.concatenate([r[4 * b + j]["ys"] for j in range(4)], 0) for b in range(2)], 0)
    cat = lambda k: np.concatenate([r[c][k] for c in range(8)], 0)
    new_ssd = cat("o_ssd").reshape(B, DEPTH, 2, H, 64, 64)
    new_nk = cat("o_nk").reshape(B, DEPTH, LP, H, HD)
    new_nv = cat("o_nv").reshape(B, DEPTH, LP, H, HD)
    new_rw = cat("o_rw").reshape(B, DEPTH, 2, H, 64, 64)
    new_dk = cat("o_dk").reshape(B, DEPTH, LP, H, 2, 32)
    new_dv = cat("o_dv").reshape(B, DEPTH, LP, H, HD)
    return tuple(np.asarray(a, dtype=np.float32) for a in
                 (y_prompt, y_sample, new_ssd, new_nk, new_nv, new_rw, new_dk, new_dv))
```
